# Optimizing a Trainium2 kernel written in Bass

```python
import math
import jax, jax.numpy as jnp
from jax import lax
import numpy as np

D_MODEL = 2048
BATCH = 16
SEQ = 256
DEPTH = 2
DEC_BATCH = 8
DEC_SEQ = 2048
PAST_LEN = 512

GRID_W = 64
N_BRANCH = 4
MLA_HEADS = 8
Q_LORA = 512
KV_LORA = 256
QK_NOPE = 64
ROPE_DIM = 32
QK_HEAD = QK_NOPE + ROPE_DIM
V_HEAD = 64
ROPE_THETA = 10000.0
CONV_W = 512
SSM_W = 512
SSM_GROUP_CH = 16
SSM_GROUPS = SSM_W // SSM_GROUP_CH
SSM_STATE = 64
POOL_W = 512
POOL_WINDOWS = (2, 4, 8, 16)
POOL_GROUP_CH = POOL_W // 4
MLP_HIDDEN = 4 * D_MODEL
Q_BLOCK = 128
EPS = 1e-6
IN_SIZES = (Q_LORA, KV_LORA, ROPE_DIM, CONV_W, CONV_W, CONV_W, SSM_W, POOL_W, N_BRANCH * D_MODEL)
IN_SPLITS = tuple(int(s) for s in np.cumsum(IN_SIZES)[:-1])
IN_COLS = int(sum(IN_SIZES))

kernel_name = 'hybrid_prefix_diffusion_step'


def rms_norm(x, g):
    xf = x.astype(jnp.float32)
    y = xf * lax.rsqrt(jnp.mean(xf * xf, axis=-1, keepdims=True) + EPS)
    return (y * g.astype(jnp.float32)).astype(x.dtype)


def axial_rope_tables(rows):
    half = ROPE_DIM // 2
    inv_freq = ROPE_THETA ** (-jnp.arange(0, half, 2, dtype=jnp.float32) / half)
    row_pos = jnp.repeat(jnp.arange(rows, dtype=jnp.float32), GRID_W)
    col_pos = jnp.tile(jnp.arange(GRID_W, dtype=jnp.float32), rows)
    ang_r = row_pos[:, None] * inv_freq
    ang_c = col_pos[:, None] * inv_freq
    return (jnp.cos(ang_r)[:, None, :], jnp.sin(ang_r)[:, None, :],
            jnp.cos(ang_c)[:, None, :], jnp.sin(ang_c)[:, None, :])


def rotate_pairs(x, cos, sin):
    x1, x2 = jnp.split(x, 2, axis=-1)
    return jnp.concatenate([x1 * cos - x2 * sin, x1 * sin + x2 * cos], axis=-1)


def apply_axial_rope(x, rope):
    cos_r, sin_r, cos_c, sin_c = rope
    half = ROPE_DIM // 2
    x_nope = x[..., :QK_NOPE]
    x_row = x[..., QK_NOPE:QK_NOPE + half]
    x_col = x[..., QK_NOPE + half:]
    return jnp.concatenate([x_nope, rotate_pairs(x_row, cos_r, sin_r), rotate_pairs(x_col, cos_c, sin_c)], axis=-1)


def mla_keys_values(ckv, k_rope, lp, rope):
    b, n, _ = ckv.shape
    kv = (ckv @ lp['w_ukv']).reshape(b, n, MLA_HEADS, QK_NOPE + V_HEAD)
    k_nope, v = kv[..., :QK_NOPE], kv[..., QK_NOPE:]
    k_pe = jnp.broadcast_to(k_rope[:, :, None, :], (b, n, MLA_HEADS, ROPE_DIM))
    k = rms_norm(jnp.concatenate([k_nope, k_pe.astype(k_nope.dtype)], axis=-1), lp['k_norm_g'])
    if rope is not None:
        k = apply_axial_rope(k, rope)
    return k, v


def block_attention(q, k, v):
    b, lq, h, dh = q.shape
    nb = lq // Q_BLOCK
    qb = jnp.moveaxis(q.reshape(b, nb, Q_BLOCK, h, dh), 1, 0)
    scale = dh ** -0.5

    def one_block(qi):
        s = jnp.einsum('bqhd,bkhd->bhqk', qi, k).astype(jnp.float32) * scale
        p = jax.nn.softmax(s, axis=-1).astype(v.dtype)
        return jnp.einsum('bhqk,bkhe->bqhe', p, v)

    o = lax.map(one_block, qb)
    return jnp.moveaxis(o, 0, 1).reshape(b, lq, h * v.shape[-1])


def short_conv(u, w, bias):
    n = u.shape[1]
    up = jnp.pad(u, ((0, 0), (1, 1), (0, 0)))
    return up[:, 0:n] * w[0] + up[:, 1:n + 1] * w[1] + up[:, 2:n + 2] * w[2] + bias


def _linear_combine(e1, e2):
    a1, b1 = e1
    a2, b2 = e2
    return a2 * a1, a2 * b1 + b2


def s5_direction(u, lam_re, lam_im, log_step, b_re, b_im, c_re, c_im, h0, reverse):
    f32 = jnp.float32
    lam = lax.complex(lam_re.astype(f32), lam_im.astype(f32))
    step = jnp.exp(log_step.astype(f32))[:, None]
    lam_bar = jnp.exp(lam * step)
    bmat = lax.complex(b_re.astype(f32), b_im.astype(f32))
    b_bar = ((lam_bar - 1.0) / lam)[..., None] * bmat
    bu = jnp.einsum('gnc,blgc->blgn', b_bar, u.astype(f32).astype(jnp.complex64))
    if reverse:
        bu = jnp.flip(bu, axis=1)
    bu = bu.at[:, 0].add(lam_bar * h0)
    a = jnp.broadcast_to(lam_bar, bu.shape)
    _, xs = lax.associative_scan(_linear_combine, (a, bu), axis=1)
    h_final = xs[:, -1]
    if reverse:
        xs = jnp.flip(xs, axis=1)
    cmat = lax.complex(c_re.astype(f32), c_im.astype(f32))
    y = jnp.real(jnp.einsum('gcn,blgn->blgc', cmat, xs))
    return y, h_final


def multiscale_pool(u, pool_w, pool_scale):
    b, n, _ = u.shape
    uf = u.astype(jnp.float32)
    cs = jnp.concatenate([jnp.zeros((b, 1, POOL_W), jnp.float32), jnp.cumsum(uf, axis=1)], axis=1)
    t = jnp.arange(n)
    outs = []
    for gi, w in enumerate(POOL_WINDOWS):
        lo = jnp.clip(t - w // 2, 0, n)
        hi = jnp.clip(t + w // 2, 0, n)
        sl = slice(gi * POOL_GROUP_CH, (gi + 1) * POOL_GROUP_CH)
        cs_g = cs[..., sl]
        cnt = (hi - lo).astype(jnp.float32)[None, :, None]
        mean = (cs_g[:, hi] - cs_g[:, lo]) / cnt - uf[..., sl]
        outs.append(jnp.einsum('bnc,cd->bnd', mean, pool_w[gi]))
    return jnp.concatenate(outs, axis=-1) * pool_scale


def mixers(h, lp, rope, ctx):
    b, n, _ = h.shape
    proj = h @ lp['w_in']
    q_a, kv_a, k_rope, conv_u, conv_bg, conv_cg, ssm_u, pool_u, gate_in = jnp.split(proj, IN_SPLITS, axis=-1)

    q = (rms_norm(q_a, lp['q_a_norm_g']) @ lp['w_uq']).reshape(b, n, MLA_HEADS, QK_HEAD)
    q = rms_norm(q, lp['q_norm_g'])
    if rope is not None:
        q = apply_axial_rope(q, rope)
    ckv = rms_norm(kv_a, lp['kv_a_norm_g'])
    k, v = mla_keys_values(ckv, k_rope, lp, rope)
    if ctx is not None:
        ctx_ckv, ctx_krope, ctx_state = ctx
        k_c, v_c = mla_keys_values(ctx_ckv, ctx_krope, lp, None)
        k = jnp.concatenate([k, k_c.astype(k.dtype)], axis=1)
        v = jnp.concatenate([v, v_c.astype(v.dtype)], axis=1)
    branch_a = block_attention(q, k, v) @ lp['w_mla_o']

    branch_b = (conv_bg * short_conv(conv_cg * conv_u, lp['conv_w'], lp['conv_b'])) @ lp['w_conv_o']

    u = ssm_u.reshape(b, n, SSM_GROUPS, SSM_GROUP_CH)
    if ctx is None:
        h0f = jnp.zeros((b, SSM_GROUPS, SSM_STATE), jnp.complex64)
        h0b = h0f
    else:
        st = ctx_state.astype(jnp.float32)
        h0f = lax.complex(st[:, 0, 0], st[:, 0, 1])
        h0b = lax.complex(st[:, 1, 0], st[:, 1, 1])
    y_f, hf = s5_direction(u, lp['ssm_lam_re'][0], lp['ssm_lam_im'][0], lp['ssm_log_step'][0],
                           lp['ssm_b_re'][0], lp['ssm_b_im'][0], lp['ssm_c_re'][0], lp['ssm_c_im'][0], h0f, False)
    y_b, hb = s5_direction(u, lp['ssm_lam_re'][1], lp['ssm_lam_im'][1], lp['ssm_log_step'][1],
                           lp['ssm_b_re'][1], lp['ssm_b_im'][1], lp['ssm_c_re'][1], lp['ssm_c_im'][1], h0b, True)
    y_ssm = (y_f + y_b).reshape(b, n, SSM_W) + lp['ssm_d'] * ssm_u
    glu_a, glu_g = jnp.split(y_ssm @ lp['w_glu'], 2, axis=-1)
    branch_c = glu_a * jax.nn.sigmoid(glu_g)

    branch_d = multiscale_pool(pool_u, lp['pool_w'], lp['pool_scale']) @ lp['w_pool_o']

    gates = jax.nn.sigmoid(gate_in.reshape(b, n, N_BRANCH, D_MODEL).astype(jnp.float32))
    merged = (gates[:, :, 0] * branch_a + gates[:, :, 1] * branch_b
              + gates[:, :, 2] * branch_c + gates[:, :, 3] * branch_d)
    out = merged @ lp['w_o']
    if ctx is None:
        state = jnp.stack([jnp.stack([jnp.real(hf), jnp.imag(hf)], axis=1),
                           jnp.stack([jnp.real(hb), jnp.imag(hb)], axis=1)], axis=1)
        return out, (ckv, k_rope, state)
    return out, None


def trunk_layer(x, mod, lp, rope, ctx):
    shift1, scale1, gate1, shift2, scale2, gate2 = jnp.split(mod, 6, axis=-1)
    h = rms_norm(x, lp['norm1_g']) * (1.0 + scale1) + shift1
    mix, cache = mixers(h, lp, rope, ctx)
    x = x + gate1 * mix
    h = rms_norm(x, lp['norm2_g']) * (1.0 + scale2) + shift2
    x = x + gate2 * (jnp.square(jax.nn.relu(h @ lp['w_mlp1'])) @ lp['w_mlp2'])
    return x, cache


def setup_inputs(seed: int = 0) -> dict:
    key = jax.random.key(seed)
    ks = iter(list(jax.random.split(key, 48)))
    f32 = jnp.float32

    def nrm(shape, scale=1.0):
        return jax.random.normal(next(ks), shape, f32) * scale

    L, G, N, CG = DEPTH, SSM_GROUPS, SSM_STATE, SSM_GROUP_CH
    n_idx = jnp.arange(N, dtype=f32)
    return {
        'x_prompt': nrm((BATCH, SEQ, D_MODEL)),
        'x_sample': nrm((DEC_BATCH, DEC_SEQ, D_MODEL)),
        'c': nrm((DEC_BATCH, D_MODEL)),
        'cache_ckv': nrm((DEC_BATCH, DEPTH, PAST_LEN, KV_LORA)),
        'cache_krope': nrm((DEC_BATCH, DEPTH, PAST_LEN, ROPE_DIM)),
        'state_ssm': nrm((DEC_BATCH, DEPTH, 2, 2, G, N), 0.1),
        'c_ctx': nrm((D_MODEL,)),
        'w_ada': nrm((L, D_MODEL, 6 * D_MODEL), 0.5 * D_MODEL ** -0.5),
        'b_ada': nrm((L, 6 * D_MODEL), 0.02),
        'norm1_g': 1.0 + nrm((L, D_MODEL), 0.02),
        'norm2_g': 1.0 + nrm((L, D_MODEL), 0.02),
        'w_in': nrm((L, D_MODEL, IN_COLS), D_MODEL ** -0.5),
        'q_a_norm_g': 1.0 + nrm((L, Q_LORA), 0.02),
        'kv_a_norm_g': 1.0 + nrm((L, KV_LORA), 0.02),
        'w_uq': nrm((L, Q_LORA, MLA_HEADS * QK_HEAD), Q_LORA ** -0.5),
        'w_ukv': nrm((L, KV_LORA, MLA_HEADS * (QK_NOPE + V_HEAD)), KV_LORA ** -0.5),
        'q_norm_g': 1.0 + nrm((L, QK_HEAD), 0.02),
        'k_norm_g': 1.0 + nrm((L, QK_HEAD), 0.02),
        'w_mla_o': nrm((L, MLA_HEADS * V_HEAD, D_MODEL), (MLA_HEADS * V_HEAD) ** -0.5),
        'conv_w': nrm((L, 3, CONV_W), 3 ** -0.5),
        'conv_b': nrm((L, CONV_W), 0.02),
        'w_conv_o': nrm((L, CONV_W, D_MODEL), CONV_W ** -0.5),
        'ssm_lam_re': -0.5 + nrm((L, 2, G, N), 0.01),
        'ssm_lam_im': math.pi * n_idx + nrm((L, 2, G, N), 0.01),
        'ssm_log_step': jax.random.uniform(next(ks), (L, 2, G), f32, math.log(1e-3), math.log(1e-1)),
        'ssm_b_re': nrm((L, 2, G, N, CG), (2 * CG) ** -0.5),
        'ssm_b_im': nrm((L, 2, G, N, CG), (2 * CG) ** -0.5),
        'ssm_c_re': nrm((L, 2, G, CG, N), (2 * N) ** -0.5),
        'ssm_c_im': nrm((L, 2, G, CG, N), (2 * N) ** -0.5),
        'ssm_d': nrm((L, SSM_W)),
        'w_glu': nrm((L, SSM_W, 2 * D_MODEL), SSM_W ** -0.5),
        'pool_w': nrm((L, 4, POOL_GROUP_CH, POOL_GROUP_CH), POOL_GROUP_CH ** -0.5),
        'pool_scale': 1.0 + nrm((L, POOL_W), 0.1),
        'w_pool_o': nrm((L, POOL_W, D_MODEL), POOL_W ** -0.5),
        'w_o': nrm((L, D_MODEL, D_MODEL), D_MODEL ** -0.5),
        'w_mlp1': nrm((L, D_MODEL, MLP_HIDDEN), D_MODEL ** -0.5),
        'w_mlp2': nrm((L, MLP_HIDDEN, D_MODEL), MLP_HIDDEN ** -0.5),
    }


def reference(x_prompt, x_sample, c, cache_ckv, cache_krope, state_ssm, c_ctx, w_ada, b_ada,
              norm1_g, norm2_g, w_in, q_a_norm_g, kv_a_norm_g, w_uq, w_ukv, q_norm_g, k_norm_g,
              w_mla_o, conv_w, conv_b, w_conv_o, ssm_lam_re, ssm_lam_im, ssm_log_step,
              ssm_b_re, ssm_b_im, ssm_c_re, ssm_c_im, ssm_d, w_glu, pool_w, pool_scale, w_pool_o,
              w_o, w_mlp1, w_mlp2):
    rows = x_sample.shape[1] // GRID_W
    rope = axial_rope_tables(rows)
    silu_ctx = jax.nn.silu(c_ctx)
    silu_c = jax.nn.silu(c)
    y_prompt = x_prompt
    y_sample = x_sample
    ckv_list, krope_list, ssm_list = [], [], []
    for l in range(DEPTH):
        lp = dict(norm1_g=norm1_g[l], norm2_g=norm2_g[l], w_in=w_in[l],
                  q_a_norm_g=q_a_norm_g[l], kv_a_norm_g=kv_a_norm_g[l], w_uq=w_uq[l], w_ukv=w_ukv[l],
                  q_norm_g=q_norm_g[l], k_norm_g=k_norm_g[l], w_mla_o=w_mla_o[l],
                  conv_w=conv_w[l], conv_b=conv_b[l], w_conv_o=w_conv_o[l],
                  ssm_lam_re=ssm_lam_re[l], ssm_lam_im=ssm_lam_im[l], ssm_log_step=ssm_log_step[l],
                  ssm_b_re=ssm_b_re[l], ssm_b_im=ssm_b_im[l], ssm_c_re=ssm_c_re[l], ssm_c_im=ssm_c_im[l],
                  ssm_d=ssm_d[l], w_glu=w_glu[l], pool_w=pool_w[l], pool_scale=pool_scale[l],
                  w_pool_o=w_pool_o[l], w_o=w_o[l], w_mlp1=w_mlp1[l], w_mlp2=w_mlp2[l])
        mod_ctx = (silu_ctx @ w_ada[l] + b_ada[l])[None, None, :]
        y_prompt, (ckv_l, krope_l, ssm_l) = trunk_layer(y_prompt, mod_ctx, lp, None, None)
        ckv_list.append(ckv_l)
        krope_list.append(krope_l)
        ssm_list.append(ssm_l)
        mod_lat = (silu_c @ w_ada[l] + b_ada[l])[:, None, :]
        y_sample, _ = trunk_layer(y_sample, mod_lat, lp, rope,
                                  (cache_ckv[:, l], cache_krope[:, l], state_ssm[:, l]))
    new_ckv = jnp.stack(ckv_list, axis=1)
    new_krope = jnp.stack(krope_list, axis=1)
    new_ssm = jnp.stack(ssm_list, axis=1)
    return (y_prompt, y_sample, new_ckv, new_krope, new_ssm)
```

```python
import math
from contextlib import ExitStack
import numpy as np
import ml_dtypes
import concourse.bass as bass
import concourse.mybir as mybir
from concourse.bass_utils import run_bass_kernel_spmd

F32 = mybir.dt.float32
BF16 = mybir.dt.bfloat16
AF = mybir.ActivationFunctionType
ALU = mybir.AluOpType

D = 2048
T = 2560
NT = 5
TP = 512
TS = 2048
PAST = 512
TK = T + PAST
L = 16
NQ = T // L
EPS = 1e-6
HID = 8192
DEPTH = 2
SEGS = [(0, 256), (256, 256), (512, 2048)]
IN_OFF = dict(q=0, kv=512, kr=768, cu=800, cb=1312, cc=1824, su=2336, pu=2848, g=3360)
IN_COLS = 11552


class Prog:
    ENGS = ("pe", "act", "dve", "pool", "sp")
    KQ = 8

    def __init__(self, nc):
        self.nc = nc
        self.ops = []
        self.lw = {}
        self.rd = {}
        self.last_on = {e: None for e in self.ENGS}
        self.pend = {e: set() for e in self.ENGS}
        self.dma_since = []

    def _add(self, eng, fn, r, w, is_dma):
        idx = len(self.ops)
        deps = set(self.pend[eng])
        self.pend[eng] = set()
        for k in r:
            x = self.lw.get(k)
            if x is not None:
                deps.add(x)
        for k in w:
            x = self.lw.get(k)
            if x is not None:
                deps.add(x)
            for y in self.rd.get(k, ()):
                deps.add(y)
        for k in r:
            self.rd.setdefault(k, []).append(idx)
        for k in w:
            self.lw[k] = idx
            self.rd[k] = []
        deps.discard(idx)
        self.ops.append((eng, fn, deps, is_dma))
        self.last_on[eng] = idx
        if is_dma:
            self.dma_since.append(idx)
        return idx

    def op(self, eng, fn, r=(), w=()):
        return self._add(eng, fn, r, w, False)

    def dma(self, q, out, in_, r=(), w=(), slow=False):
        if slow:
            return self._add(q, lambda e: e.dma_start(out=out, in_=in_, allow_slow_non_contiguous=True), r, w, True)
        return self._add(q, lambda e: e.dma_start(out=out, in_=in_), r, w, True)

    def barrier(self):
        bar = set(self.dma_since)
        for e in self.ENGS:
            if self.last_on[e] is not None:
                bar.add(self.last_on[e])
        for e in self.ENGS:
            self.pend[e] |= bar
        self.dma_since = []

    def emit(self, es):
        nc = self.nc
        ops = self.ops
        needed = [False] * len(ops)
        for (eng_, _, deps, _) in ops:
            for d in deps:
                if eng_ == "pe" and ops[d][0] == "pe" and not ops[d][3]:
                    continue
                needed[d] = True
        esem = {e: es.enter_context(nc.semaphore("s_" + e)) for e in self.ENGS}
        qsem = {e: [es.enter_context(nc.semaphore("q_%s%d" % (e, i))) for i in range(self.KQ)]
                for e in ("sp", "pool", "act")}
        ev = [None] * len(ops)
        cnt = {e: 0 for e in self.ENGS}
        qcnt = {e: 0 for e in qsem}
        pre = [None] * len(ops)
        for i, (eng, fn, deps, is_dma) in enumerate(ops):
            if is_dma:
                n = qcnt[eng]
                qcnt[eng] += 1
                s = qsem[eng][n % self.KQ]
                ev[i] = (s, 16 * (n // self.KQ + 1))
                if n >= self.KQ:
                    pre[i] = (s, 16 * (n // self.KQ))
            elif needed[i]:
                cnt[eng] += 1
                ev[i] = (esem[eng], cnt[eng])
        streams = {e: [] for e in self.ENGS}
        for i, o in enumerate(ops):
            streams[o[0]].append(i)
        block = es.enter_context(nc.Block())

        def run(eng_name, e):
            seen = {}
            for i in streams[eng_name]:
                _, fn, deps, is_dma = ops[i]
                waits = {}
                if pre[i] is not None:
                    waits[pre[i][0]] = pre[i][1]
                for d in deps:
                    if eng_name == "pe" and ops[d][0] == "pe" and not ops[d][3]:
                        continue
                    s, v = ev[d]
                    if waits.get(s, 0) < v:
                        waits[s] = v
                for s, v in waits.items():
                    if seen.get(s, 0) < v:
                        e.wait_ge(s, v)
                        seen[s] = v
                ins = fn(e)
                if is_dma:
                    ins.then_inc(ev[i][0], 16)
                elif ev[i] is not None:
                    ins.then_inc(ev[i][0], 1)
            if eng_name in qsem:
                n = qcnt[eng_name]
                for k in range(min(n, self.KQ)):
                    tot = (n - k + self.KQ - 1) // self.KQ
                    e.wait_ge(qsem[eng_name][k], 16 * tot)

        @block.tensor
        def _(e):
            run("pe", e)

        @block.scalar
        def _(e):
            run("act", e)

        @block.vector
        def _(e):
            run("dve", e)

        @block.gpsimd
        def _(e):
            run("pool", e)

        @block.sync
        def _(e):
            run("sp", e)


def _feat_layout(v, nchunk):
    return np.ascontiguousarray(v.reshape(nchunk, 128).T)


def host_consts():
    c = {}
    c["ident_f"] = np.eye(128, dtype=np.float32)
    half = 16
    inv_freq = (10000.0 ** (-np.arange(0, half, 2, dtype=np.float32) / half)).astype(np.float32)
    rows = TS // 64
    row_pos = np.repeat(np.arange(rows, dtype=np.float32), 64)
    col_pos = np.tile(np.arange(64, dtype=np.float32), rows)
    ang = np.zeros((32, TS), np.float32)
    for r in range(32):
        pos = row_pos if r < 16 else col_pos
        ang[r] = pos * inv_freq[r % 8]
    c["rope_cos"] = np.cos(ang).astype(np.float32)
    c["rope_sin"] = np.sin(ang).astype(np.float32)
    prot = np.zeros((128, 96), np.float32)
    for mm in range(32):
        m = 64 + mm
        if mm % 16 < 8:
            prot[m + 8, m] = -1.0
        else:
            prot[m - 8, m] = 1.0
    c["prot"] = prot
    inv = np.zeros((4, 128, T), np.float32)
    for gi, w in enumerate((2, 4, 8, 16)):
        for (s0, n) in SEGS:
            t = np.arange(n)
            lo = np.clip(t - w // 2, 0, n)
            hi = np.clip(t + w // 2, 0, n)
            inv[gi, :, s0:s0 + n] = (1.0 / (hi - lo).astype(np.float32))[None, :]
    c["pool_inv"] = inv
    mF = np.zeros((2, 128, 256), np.float32)
    mB = np.zeros((2, 128, 256), np.float32)
    dI = np.zeros((2, 128, 256), np.float32)
    for ch in range(2):
        for pp in range(128):
            j = ch * 8 + pp // 16
            cc = pp % 16
            for i in range(16):
                if i >= j:
                    mF[ch, pp, i * 16:(i + 1) * 16] = 1.0
                if j >= i:
                    mB[ch, pp, i * 16:(i + 1) * 16] = 1.0
            dI[ch, pp, j * 16 + cc] = 1.0
    c["ssm_mF"] = mF
    c["ssm_mB"] = mB
    c["ssm_dI"] = dI
    return c


class K:
    pass


def build(debug=(), stop_after=None):
    nc = bass.Bass("TRN2", target_bir_lowering=False)
    k = K()
    k.nc = nc
    k.stop_after = stop_after
    k.debug = debug
    p = Prog(nc)
    k.p = p

    def din(name, shape, dt=F32):
        return nc.dram_tensor(name, list(shape), dt, kind="ExternalInput").ap()

    def dout(name, shape, dt=F32):
        return nc.dram_tensor(name, list(shape), dt, kind="ExternalOutput").ap()

    def dscr(name, shape, dt):
        kind = "ExternalOutput" if name in debug else "Internal"
        return nc.dram_tensor(name, list(shape), dt, kind=kind).ap()

    xin = din("xin", [T, D])
    cvecT = din("cvecT", [128, 16, 2])
    cache_ckv = din("cache_ckv", [DEPTH, PAST, 256])
    cache_kr = din("cache_kr", [DEPTH, PAST, 32])
    state_in = din("state_in", [DEPTH, 2, 2, 32, 64])
    w_ada = din("w_ada", [DEPTH, D, 6 * D])
    b_adaT = din("b_adaT", [DEPTH, 128, 96])
    n1gT = din("n1gT", [DEPTH, 128, 16])
    n2gT = din("n2gT", [DEPTH, 128, 16])
    w_in = din("w_in", [DEPTH, D, IN_COLS])
    qagT = din("qagT", [DEPTH, 128, 4])
    kvgT = din("kvgT", [DEPTH, 128, 2])
    w_uq = din("w_uq", [DEPTH, 512, 768])
    w_ukv = din("w_ukv", [DEPTH, 256, 1024])
    qng = din("qng", [DEPTH, 96, 1])
    kng = din("kng", [DEPTH, 96, 1])
    w_mla_o = din("w_mla_o", [DEPTH, 512, D])
    conv_wT = din("conv_wT", [DEPTH, 128, 4, 3])
    conv_bT = din("conv_bT", [DEPTH, 128, 4])
    w_conv_o = din("w_conv_o", [DEPTH, 512, D])
    ssm_lam_re = din("ssm_lam_re", [DEPTH, 2, 32, 64])
    ssm_lam_im = din("ssm_lam_im", [DEPTH, 2, 32, 64])
    ssm_log_step = din("ssm_log_step", [DEPTH, 2, 32])
    ssm_b_re = din("ssm_b_re", [DEPTH, 2, 32, 64, 16])
    ssm_b_im = din("ssm_b_im", [DEPTH, 2, 32, 64, 16])
    ssm_c_re = din("ssm_c_re", [DEPTH, 2, 32, 16, 64])
    ssm_c_im = din("ssm_c_im", [DEPTH, 2, 32, 16, 64])
    ssm_d = din("ssm_d", [DEPTH, 512])
    w_glu = din("w_glu", [DEPTH, 512, 2 * D])
    pool_w = din("pool_w", [DEPTH, 4, 128, 128])
    pool_sT = din("pool_sT", [DEPTH, 128, 4])
    w_pool_o = din("w_pool_o", [DEPTH, 512, D])
    w_o = din("w_o", [DEPTH, D, D])
    w_mlp1 = din("w_mlp1", [DEPTH, D, HID])
    w_mlp2 = din("w_mlp2", [DEPTH, HID, D])
    c_ident = din("ident_f", [128, 128])
    c_cos = din("rope_cos", [32, TS])
    c_sin = din("rope_sin", [32, TS])
    c_prot = din("prot", [128, 96])
    c_pinv = din("pool_inv", [4, 128, T])
    c_mF = din("ssm_mF", [2, 128, 256])
    c_mB = din("ssm_mB", [2, 128, 256])
    c_dI = din("ssm_dI", [2, 128, 256])
    c_dvec = din("ssm_dvec", [128, DEPTH, 32])
    yout = dout("yout", [T, D])
    o_ckv = dout("o_ckv", [2, DEPTH, 256, 256])
    o_kr = dout("o_kr", [2, DEPTH, 256, 32])
    o_ssm = dout("o_ssm", [2, DEPTH, 2, 2, 32, 64])
    XT = [dscr("XT%d" % i, [16, 128, T], F32) for i in range(2)]
    PROJ = dscr("PROJ", [20, 128, T], BF16)
    KV32 = dscr("KV32", [3, 128, T], F32)
    UTOK = dscr("UTOK", [T, 512], BF16)
    YTOK = dscr("YTOK", [T, 512], BF16)
    GS = dscr("GS", [64, 128, T], BF16)
    MT = dscr("MT", [16, 128, T], BF16)
    AT = dscr("AT", [64, 128, T], BF16)
    W2B = dscr("W2B", [16, 128, 64 * 128], BF16)
    SSM_BLT = dscr("SSM_BLT", [DEPTH, 2, 32, 2, 128, 128], BF16)
    SSM_ML = dscr("SSM_ML", [DEPTH, 2, 32, 2, 128, 256], BF16)
    SSM_CS = dscr("SSM_CS", [DEPTH, 2, 32, 64, 2, 256], BF16)
    if "DBG_BI" in debug:
        k.DBG_BI = dscr("DBG_BI", [16, 128, T], BF16)

    es = ExitStack()
    k.es = es

    def sb(name, shape, dt):
        return es.enter_context(nc.sbuf_tensor("sb_" + name, list(shape), dt))

    PSD = [es.enter_context(nc.psum_tensor("psd%d" % i, [128, 1024], F32)) for i in range(4)]
    PS = [PSD[i // 2][:, (i % 2) * 512:(i % 2 + 1) * 512] for i in range(8)]
    psc = [0]

    def nextps():
        i = psc[0] % 8
        psc[0] += 1
        return i

    ident_f = sb("ident_f", [128, 128], F32)
    ident_b = sb("ident_b", [128, 128], BF16)
    ones_b = sb("ones_b", [128, 128], BF16)
    ones_f = sb("ones_f", [128, 128], F32)
    modv = sb("modv", [128, DEPTH, 96, 2], F32)
    A1 = sb("A1", [128, DEPTH, 16, 2], F32)
    A2 = sb("A2", [128, DEPTH, 16, 2], F32)
    n1g = sb("n1g", [128, DEPTH, 16], F32)
    n2g = sb("n2g", [128, DEPTH, 16], F32)
    CAA = sb("CAA", [128, DEPTH, 2, 32], F32)
    CAB = sb("CAB", [128, DEPTH, 2, 32], F32)

    p.dma("sp", ident_f[:], c_ident[:, :], w=["ident_f"])
    p.op("dve", lambda e: e.tensor_copy(out=ident_b[:], in_=ident_f[:]), r=["ident_f"], w=["ident_b"])
    p.op("pool", lambda e: e.memset(ones_b[:], 1.0), w=["ones_b"])
    p.op("pool", lambda e: e.memset(ones_f[:], 1.0), w=["ones_f"])
    for l in range(DEPTH):
        p.dma("sp", n1g[:, l, :], n1gT[l], w=["n1g"])
        p.dma("sp", n2g[:, l, :], n2gT[l], w=["n2g"])

    def A(buf, c, t0, n):
        return buf[:, c * T + t0: c * T + t0 + n]

    with ExitStack() as ph:
        sT = ph.enter_context(nc.sbuf_tensor("sT", [128, 16, 2], F32))
        sg = ph.enter_context(nc.sbuf_tensor("sgT", [128, 16, 2], F32))
        wab = [ph.enter_context(nc.sbuf_tensor("wab%d" % i, [128, 16, 512], F32)) for i in range(3)]
        wabb = [ph.enter_context(nc.sbuf_tensor("wabb%d" % i, [128, 16, 512], BF16)) for i in range(2)]
        sTb = ph.enter_context(nc.sbuf_tensor("sTb", [128, 16, 2], BF16))
        bad = ph.enter_context(nc.sbuf_tensor("bad", [128, DEPTH, 96], F32))
        p.dma("sp", sT[:], cvecT[:, :, :], w=["sT"])
        p.op("act", lambda e: e.activation(out=sg[:], in_=sT[:], func=AF.Sigmoid), r=["sT"], w=["sg"])
        p.op("dve", lambda e: e.tensor_tensor(out=sTb[:], in0=sT[:], in1=sg[:], op=ALU.mult), r=["sT", "sg"], w=["sTb"])
        for l in range(DEPTH):
            p.dma("sp", bad[:, l, :], b_adaT[l], w=["bad"])
        gl = [(l, g) for l in range(DEPTH) for g in range(24)]

        def ada_dma(i):
            l, g = gl[i]
            src = w_ada[l, :, g * 512:(g + 1) * 512].rearrange("(kc p) n -> p kc n", p=128)
            p.dma("sp", wab[i % 3][:], src, w=[("wab", i % 3)])

        def ada_cast(i):
            wf = wab[i % 3]; wb = wabb[i % 2]
            p.op("act", lambda e, wf=wf, wb=wb: e.copy(out=wb[:, 0:6, :], in_=wf[:, 0:6, :]), r=[("wab", i % 3)], w=[("wabb", i % 2)])
            p.op("pool", lambda e, wf=wf, wb=wb: e.tensor_copy(out=wb[:, 6:10, :], in_=wf[:, 6:10, :]), r=[("wab", i % 3)], w=[("wabb", i % 2)])
            p.op("dve", lambda e, wf=wf, wb=wb: e.tensor_copy(out=wb[:, 10:16, :], in_=wf[:, 10:16, :]), r=[("wab", i % 3)], w=[("wabb", i % 2)])
        ada_dma(0); ada_dma(1); ada_cast(0)
        for i, (l, g) in enumerate(gl):
            if i + 2 < len(gl):
                ada_dma(i + 2)
            if i + 1 < len(gl):
                ada_cast(i + 1)
            wb = wabb[i % 2]; wk = ("wabb", i % 2)
            pi = 7
            for fc in range(4):
                col = (g * 4 + fc) * 2
                for kc in range(16):
                    p.op("pe", lambda e, wb=wb, kc=kc, fc=fc, col=col: e.matmul(
                        PS[pi][:, col:col + 2], wb[:, kc, fc * 128:(fc + 1) * 128], sTb[:, kc, :],
                        start=(kc == 0), stop=(kc == 15)), r=[wk, "sTb"], w=[("ps", pi)])
            if g != 23:
                continue
            p.op("dve", lambda e, l=l: e.tensor_tensor(
                out=modv[:, l, :, :], in0=PS[7][:, 0:192].rearrange("p (c j) -> p c j", j=2),
                in1=bad[:, l, :].unsqueeze(2).to_broadcast([128, 96, 2]), op=ALU.add),
                r=[("ps", 7), "bad"], w=["modv"])
            for (Ax, ng, c0) in ((A1, n1g, 16), (A2, n2g, 64)):
                p.op("dve", lambda e, Ax=Ax, ng=ng, c0=c0, l=l: e.scalar_tensor_tensor(
                    out=Ax[:, l, :, :], in0=modv[:, l, c0:c0 + 16, :], scalar=1.0,
                    in1=ng[:, l, :].unsqueeze(2).to_broadcast([128, 16, 2]), op0=ALU.add, op1=ALU.mult),
                    r=["modv", "n1g", "n2g"], w=["A1A2"])
    p.barrier()
    for l in range(DEPTH):
        ssm_gen(k, locals(), l)
    BUFA = sb("BUFA", [128, 16 * T], BF16)
    BUFB = sb("BUFB", [128, 16 * T], BF16)
    build_layers(k, locals())
    return k


def build_layers(k, g):
    nc = k.nc
    p = k.p
    PS = g["PS"]; nextps = g["nextps"]; A = g["A"]
    BUFA = g["BUFA"]; BUFB = g["BUFB"]
    ident_f = g["ident_f"]; ident_b = g["ident_b"]; ones_b = g["ones_b"]
    modv = g["modv"]; A1 = g["A1"]; A2 = g["A2"]
    XT = g["XT"]; PROJ = g["PROJ"]; KV32 = g["KV32"]; UTOK = g["UTOK"]; YTOK = g["YTOK"]
    GS = g["GS"]; MT = g["MT"]; AT = g["AT"]
    uid = [0]

    def scoped(ph, name, shape, dt):
        uid[0] += 1
        return ph.enter_context(nc.sbuf_tensor("t%d_%s" % (uid[0], name), list(shape), dt))

    def mod_j(tt):
        return 0 if tt == 0 else 1

    with ExitStack() as ph:
        xt = [scoped(ph, "xt%d" % i, [128, D], F32) for i in range(2)]
        st = [scoped(ph, "st%d" % i, [128, 4, 128], F32) for i in range(3)]
        si = 0
        for ti in range(T // 128):
            xb = xt[ti % 2]; xk = ("xt", ti % 2)
            p.dma("sp", xb[:], g["xin"][ti * 128:(ti + 1) * 128, :], w=[xk])
            for fg in range(4):
                pi = nextps()
                for f4 in range(4):
                    fc = fg * 4 + f4
                    p.op("pe", lambda e, pi=pi, f4=f4, fc=fc, xb=xb: e.transpose(
                        PS[pi][:, f4 * 128:(f4 + 1) * 128], xb[:, fc * 128:(fc + 1) * 128], ident_f[:]),
                        r=[xk, "ident_f"], w=[("ps", pi)])
                sb_ = st[si % 3]; sk = ("st", si % 3); si += 1
                eng = "dve" if fg % 2 == 0 else "act"
                if eng == "dve":
                    p.op("dve", lambda e, pi=pi, sb_=sb_: e.tensor_copy(
                        out=sb_[:], in_=PS[pi][:, :].rearrange("p (a b) -> p a b", a=4)), r=[("ps", pi)], w=[sk])
                else:
                    p.op("act", lambda e, pi=pi, sb_=sb_: e.copy(
                        out=sb_[:], in_=PS[pi][:, :].rearrange("p (a b) -> p a b", a=4)), r=[("ps", pi)], w=[sk])
                dst = XT[0][fg * 4:(fg + 1) * 4, :, ti * 128:(ti + 1) * 128].rearrange("c p t -> p c t")
                p.dma("sp", dst, sb_[:], r=[sk], w=["XT0"])
    p.barrier()

    def norm(XTd, xkey, Amod, l, shift_c0, dst):
        with ExitStack() as ph:
            xc = [scoped(ph, "xc%d" % i, [128, T], F32) for i in range(2)]
            sq = [scoped(ph, "sq%d" % i, [128, T], BF16) for i in range(2)]
            RB = scoped(ph, "RB", [128, T], F32)
            pss = [nextps() for _ in range(NT)]
            for fc in range(16):
                xb = xc[fc % 2]; xk = ("xc", fc % 2)
                p.dma("sp", xb[:], XTd[fc], r=[xkey], w=[xk])
                sb_ = sq[fc % 2]; sk = ("sq", fc % 2)
                p.op("act", lambda e, xb=xb, sb_=sb_: e.activation(out=sb_[:], in_=xb[:], func=AF.Square),
                     r=[xk], w=[sk])
                for tt in range(NT):
                    p.op("pe", lambda e, tt=tt, sb_=sb_, fc=fc: e.matmul(
                        PS[pss[tt]][:, :], ones_b[:, :], sb_[:, tt * 512:(tt + 1) * 512],
                        start=(fc == 0), stop=(fc == 15)), r=[sk, "ones_b"], w=[("ps", pss[tt])])
            for tt in range(NT):
                sl = RB[:, tt * 512:(tt + 1) * 512]
                p.op("dve", lambda e, tt=tt, sl=sl: e.tensor_scalar(
                    out=sl, in0=PS[pss[tt]][:, :], scalar1=1.0 / D, scalar2=EPS, op0=ALU.mult, op1=ALU.add),
                    r=[("ps", pss[tt])], w=[("RB", tt)])
                p.op("act", lambda e, sl=sl: e.sqrt(out=sl, in_=sl), r=[("RB", tt)], w=[("RB", tt)])
                p.op("dve", lambda e, sl=sl: e.reciprocal(out=sl, in_=sl), r=[("RB", tt)], w=[("RB", tt)])
            for fc in range(16):
                xb = xc[fc % 2]; xk = ("xc", fc % 2)
                p.dma("sp", xb[:], XTd[fc], r=[xkey], w=[xk])
                for tt in range(NT):
                    j = mod_j(tt)
                    sl = xb[:, tt * 512:(tt + 1) * 512]
                    p.op("dve", lambda e, sl=sl, tt=tt, fc=fc, j=j: e.scalar_tensor_tensor(
                        out=sl, in0=sl, scalar=Amod[:, l, fc, j:j + 1], in1=RB[:, tt * 512:(tt + 1) * 512],
                        op0=ALU.mult, op1=ALU.mult), r=[xk, ("RB", tt), "A1A2"], w=[xk])
                    p.op("act", lambda e, sl=sl, tt=tt, fc=fc, j=j: e.activation(
                        out=A(dst, fc, tt * 512, 512), in_=sl, func=AF.Identity,
                        bias=modv[:, l, shift_c0 + fc, j:j + 1], scale=1.0), r=[xk, "modv"], w=["BUF"])
        p.barrier()

    k.norm = norm
    for l in range(DEPTH):
        layer(k, g, l, scoped, norm)
        if k.stop_after:
            break
    with ExitStack() as ph:
        xc = [scoped(ph, "oc%d" % i, [128, 4, 512], F32) for i in range(2)]
        ot = [scoped(ph, "ot%d" % i, [128, 4, 512], F32) for i in range(2)]
        it = 0
        for tt in range(NT):
            for fg in range(4):
                xb = xc[it % 2]; xk = ("oc", it % 2)
                src = XT[0][fg * 4:(fg + 1) * 4, :, tt * 512:(tt + 1) * 512].rearrange("c p t -> p c t")
                p.dma("sp", xb[:], src, r=["XT0"], w=[xk])
                ob = ot[it % 2]; ok = ("ot", it % 2); it += 1
                for t4 in range(4):
                    pi = nextps()
                    for f4 in range(4):
                        p.op("pe", lambda e, pi=pi, f4=f4, t4=t4, xb=xb: e.transpose(
                            PS[pi][:, f4 * 128:(f4 + 1) * 128], xb[:, f4, t4 * 128:(t4 + 1) * 128], ident_f[:]),
                            r=[xk, "ident_f"], w=[("ps", pi)])
                    if t4 % 2 == 0:
                        p.op("dve", lambda e, pi=pi, ob=ob, t4=t4: e.tensor_copy(out=ob[:, t4, :], in_=PS[pi][:, :]),
                             r=[("ps", pi)], w=[ok])
                    else:
                        p.op("act", lambda e, pi=pi, ob=ob, t4=t4: e.copy(out=ob[:, t4, :], in_=PS[pi][:, :]),
                             r=[("ps", pi)], w=[ok])
                dst = g["yout"][tt * 512:(tt + 1) * 512, fg * 512:(fg + 1) * 512].rearrange("(a p) f -> p a f", p=128)
                p.dma("sp", dst, ob[:], r=[ok], w=["yout"])
    p.barrier()
    p.emit(k.es)


def rot(lst):
    st = [0]

    def nxt():
        v = lst[st[0] % len(lst)]
        st[0] += 1
        return v
    return nxt


def layer(k, g, l, scoped, norm):
    nc = k.nc
    p = k.p
    PS = g["PS"]; A = g["A"]
    BUFA = g["BUFA"]; BUFB = g["BUFB"]
    ident_f = g["ident_f"]; ident_b = g["ident_b"]; ones_b = g["ones_b"]
    modv = g["modv"]; A1 = g["A1"]; A2 = g["A2"]
    XT = g["XT"]; PROJ = g["PROJ"]; KV32 = g["KV32"]; UTOK = g["UTOK"]; YTOK = g["YTOK"]
    GS = g["GS"]; MT = g["MT"]; AT = g["AT"]
    w_in = g["w_in"][l]

    def mod_j(tt):
        return 0 if tt == 0 else 1

    def wload(wb, wk, src, K, n, c0=0):
        p.dma("pool", wb[:, 0:K, c0:c0 + n], src.rearrange("(kc p) n -> p kc n", p=128), w=[wk])

    norm(XT[0], "XT0", A1, l, 0, BUFA)

    with ExitStack() as ph:
        wbs = [scoped(ph, "wb%d" % i, [128, 16, 256], BF16) for i in range(3)]
        stg = [scoped(ph, "stg%d" % i, [128, T], BF16) for i in range(2)]
        s32 = [scoped(ph, "s32_%d" % i, [128, 512], F32) for i in range(2)]
        stages = [BUFB[:, i * 8192:(i + 1) * 8192].bitcast(F32).rearrange("p (k n) -> p k n", k=16) for i in range(4)]
        sti = [0]; s3i = [0]
        psr = rot(list(range(8)))
        groups = []
        for j in range(2):
            groups.append(("bf", 0 + j * 256, 256, 0 + 2 * j))
        for (c0, d0) in ((800, 4), (1312, 8), (1824, 12), (2848, 16)):
            for j in range(2):
                groups.append(("bf", c0 + j * 256, 256, d0 + 2 * j))
        groups.append(("kv", 512, 256, 0))
        groups.append(("kr", 768, 32, 2))
        for j in range(32):
            groups.append(("gate", 3360 + j * 256, 256, 2 * j))
        for half in range(2):
            groups.append(("ssm", 2336 + half * 256, 256, half))
        pieces = []
        for (kind, c0, n, d0) in groups:
            if kind == "kr":
                pieces.append([(w_in[:, c0:c0 + 32], 0, 16, 64, 32)])
            else:
                pieces.append([(w_in[:, c0:c0 + n], 0, 16, 0, n)])
        ws = WStream(p, "win", stages, wbs)
        kr_idx = [i for i, g_ in enumerate(groups) if g_[0] == "kr"][0]
        kr_wb = wbs[kr_idx % 3]
        orig_cast_hook = {"done": False}

        def compute(gi_, wb, wk):
            kind, c0, n, d0 = groups[gi_]
            if gi_ + 1 == kr_idx:
                p.op("dve", lambda e: e.memset(kr_wb[:, :, 0:64], 0.0), w=[("winwb", kr_idx % 3)])
            if kind == "ssm":
                half = d0
                for ti in range(T // 128):
                    pi = psr()
                    for kc in range(16):
                        p.op("pe", lambda e, pi=pi, wb=wb, kc=kc, ti=ti: e.matmul(
                            PS[pi][:, 0:256], A(BUFA, kc, ti * 128, 128), wb[:, kc, 0:256],
                            start=(kc == 0), stop=(kc == 15)), r=[wk, "BUF"], w=[("ps", pi)])
                    sb_ = stg[sti[0] % 2]; sk = ("stg", sti[0] % 2); sti[0] += 1
                    p.op("dve", lambda e, pi=pi, sb_=sb_: e.tensor_copy(out=sb_[:, 0:256], in_=PS[pi][:, 0:256]),
                         r=[("ps", pi)], w=[sk])
                    p.dma("sp", UTOK[ti * 128:(ti + 1) * 128, half * 256:(half + 1) * 256], sb_[:, 0:256], r=[sk], w=["UTOK"])
                return
            subs = [(0, 96)] if kind == "kr" else [(0, 128), (128, 128)]
            for si, (sc0, sn) in enumerate(subs):
                if kind in ("bf", "gate"):
                    sb_ = stg[sti[0] % 2]; sk = ("stg", sti[0] % 2); sti[0] += 1
                for tt in range(NT):
                    pi = psr()
                    for kc in range(16):
                        p.op("pe", lambda e, pi=pi, wb=wb, kc=kc, sc0=sc0, sn=sn, tt=tt: e.matmul(
                            PS[pi][0:sn, :], wb[:, kc, sc0:sc0 + sn], A(BUFA, kc, tt * 512, 512),
                            start=(kc == 0), stop=(kc == 15)), r=[wk, "BUF"], w=[("ps", pi)])
                    if kind == "bf":
                        p.op("dve", lambda e, pi=pi, sb_=sb_, tt=tt: e.tensor_copy(
                            out=sb_[:, tt * 512:(tt + 1) * 512], in_=PS[pi][:, :]), r=[("ps", pi)], w=[sk])
                    elif kind == "gate":
                        p.op("act", lambda e, pi=pi, sb_=sb_, tt=tt: e.activation(
                            out=sb_[:, tt * 512:(tt + 1) * 512], in_=PS[pi][:, :], func=AF.Sigmoid),
                            r=[("ps", pi)], w=[sk])
                    else:
                        s3 = s32[s3i[0] % 2]; s3k = ("s32", s3i[0] % 2); s3i[0] += 1
                        r0 = 64 if kind == "kr" else 0
                        r1 = 96 if kind == "kr" else 128
                        p.op("dve", lambda e, pi=pi, s3=s3, r0=r0, r1=r1: e.tensor_copy(
                            out=s3[r0:r1, :], in_=PS[pi][r0:r1, :]), r=[("ps", pi)], w=[s3k])
                        p.dma("sp", KV32[d0 + si, r0:r1, tt * 512:(tt + 1) * 512], s3[r0:r1, :], r=[s3k], w=["KV32"])
                if kind == "bf":
                    p.dma("sp", PROJ[d0 + si], sb_[:], r=[sk], w=["PROJ"])
                elif kind == "gate":
                    p.dma("sp", GS[d0 + si], sb_[:], r=[sk], w=["GS"])
        ws.run(pieces, compute)
    p.barrier()
    if k.stop_after == "P2":
        return
    mixers(k, g, l, scoped)
    p.barrier()
    if k.stop_after == "P3":
        return
    tail(k, g, l, scoped, norm)


def make_in_maps(inp):
    f = np.float32
    consts = host_consts()
    shared = dict(consts)

    def featl(a, n):
        return np.ascontiguousarray(a.reshape(DEPTH, n, 128).transpose(0, 2, 1))
    shared["w_ada"] = inp["w_ada"]
    shared["b_adaT"] = featl(inp["b_ada"], 96)
    shared["n1gT"] = featl(inp["norm1_g"], 16)
    shared["n2gT"] = featl(inp["norm2_g"], 16)
    shared["w_in"] = inp["w_in"]
    shared["qagT"] = featl(inp["q_a_norm_g"], 4)
    shared["kvgT"] = featl(inp["kv_a_norm_g"], 2)
    shared["w_uq"] = inp["w_uq"]
    shared["w_ukv"] = inp["w_ukv"]
    shared["qng"] = np.ascontiguousarray(inp["q_norm_g"].reshape(DEPTH, 96, 1))
    shared["kng"] = np.ascontiguousarray(inp["k_norm_g"].reshape(DEPTH, 96, 1))
    shared["w_mla_o"] = inp["w_mla_o"]
    shared["conv_wT"] = np.ascontiguousarray(inp["conv_w"].reshape(DEPTH, 3, 4, 128).transpose(0, 3, 2, 1))
    shared["conv_bT"] = featl(inp["conv_b"], 4)
    shared["w_conv_o"] = inp["w_conv_o"]
    for nm in ("ssm_lam_re", "ssm_lam_im", "ssm_log_step", "ssm_b_re", "ssm_b_im", "ssm_c_re", "ssm_c_im",
               "ssm_d", "w_glu", "pool_w", "w_pool_o", "w_o", "w_mlp1", "w_mlp2"):
        shared[nm] = inp[nm]
    shared["pool_sT"] = featl(inp["pool_scale"], 4)
    dv = inp["ssm_d"].reshape(DEPTH, 32, 16)
    shared["ssm_dvec"] = np.ascontiguousarray(np.tile(dv.transpose(2, 0, 1)[None], (8, 1, 1, 1)).reshape(128, DEPTH, 32))
    maps = []
    for c in range(8):
        m = dict(shared)
        m["xin"] = np.ascontiguousarray(np.concatenate(
            [inp["x_prompt"][2 * c], inp["x_prompt"][2 * c + 1], inp["x_sample"][c]], axis=0))
        cv = np.stack([inp["c_ctx"], inp["c"][c]], axis=0)
        m["cvecT"] = np.ascontiguousarray(cv.reshape(2, 16, 128).transpose(2, 1, 0))
        m["cache_ckv"] = np.ascontiguousarray(inp["cache_ckv"][c])
        m["cache_kr"] = np.ascontiguousarray(inp["cache_krope"][c])
        m["state_in"] = np.ascontiguousarray(inp["state_ssm"][c])
        maps.append(m)
    return maps


_CACHE = {}


def kernel(**inputs):
    inp = {k_: np.asarray(v) for k_, v in inputs.items()}
    if "k" not in _CACHE:
        _CACHE["k"] = build()
    k = _CACHE["k"]
    maps = make_in_maps(inp)
    res = run_bass_kernel_spmd(k.nc, maps, core_ids=list(range(8)))
    R = res.results
    y_prompt = np.zeros((16, 256, D), np.float32)
    y_sample = np.zeros((8, 2048, D), np.float32)
    new_ckv = np.zeros((16, DEPTH, 256, 256), np.float32)
    new_kr = np.zeros((16, DEPTH, 256, 32), np.float32)
    new_ssm = np.zeros((16, DEPTH, 2, 2, 32, 64), np.float32)
    for c in range(8):
        y = R[c]["yout"]
        y_prompt[2 * c] = y[0:256]
        y_prompt[2 * c + 1] = y[256:512]
        y_sample[c] = y[512:]
        new_ckv[2 * c:2 * c + 2] = R[c]["o_ckv"]
        new_kr[2 * c:2 * c + 2] = R[c]["o_kr"]
        new_ssm[2 * c:2 * c + 2] = R[c]["o_ssm"]
    return (y_prompt, y_sample, new_ckv, new_kr, new_ssm)


def mixers(k, g, l, scoped):
    nc = k.nc
    p = k.p
    PS = g["PS"]; A = g["A"]
    BUFA = g["BUFA"]; BUFB = g["BUFB"]
    ident_f = g["ident_f"]; ident_b = g["ident_b"]; ones_b = g["ones_b"]; ones_f = g["ones_f"]; PSD = g["PSD"]
    PROJ = g["PROJ"]; KV32 = g["KV32"]
    QN, CKV, QH, KH, VE, VO, KPE = 0, 10240, 16384, 18944, 22016, 25088, 28160
    SCALE = 96 ** -0.5

    def rstd_from_ps(pi, rows, n, dim, rs, rk):
        p.op("dve", lambda e: e.tensor_scalar(out=rs[0:rows, 0:n], in0=PS[pi][0:rows, 0:n], scalar1=1.0 / dim,
                                              scalar2=EPS, op0=ALU.mult, op1=ALU.add), r=[("ps", pi)], w=[rk])
        p.op("act", lambda e: e.sqrt(out=rs[0:rows, 0:n], in_=rs[0:rows, 0:n]), r=[rk], w=[rk])
        p.op("dve", lambda e: e.reciprocal(out=rs[0:rows, 0:n], in_=rs[0:rows, 0:n]), r=[rk], w=[rk])

    with ExitStack() as ph:
        wuq = scoped(ph, "wuq", [128, 4, 768], BF16)
        wukv = scoped(ph, "wukv", [128, 2, 1024], BF16)
        qag = scoped(ph, "qag", [128, 4], F32)
        kvg = scoped(ph, "kvg", [128, 2], F32)
        qng = scoped(ph, "qng", [96, 1], F32)
        kng = scoped(ph, "kng", [96, 1], F32)
        protf = scoped(ph, "protf", [128, 96], F32)
        f32t = [scoped(ph, "f32t0", [128, 2, 512], F32)] * 2
        esum = [f32t[0][:, 0, :], f32t[0][:, 1, :]]
        sq3 = [scoped(ph, "sq3_%d" % i, [128, 512], BF16) for i in range(3)]
        rs3 = [scoped(ph, "rs3_%d" % i, [128, 512], F32) for i in range(3)]
        tb3 = [scoped(ph, "tb3_%d" % i, [128, 512], F32) for i in range(3)]
        ta2 = [scoped(ph, "ta2_%d" % i, [128, 512], F32) for i in range(2)]
        otile = scoped(ph, "otile", [128, 4, 256], F32)
        Et = [scoped(ph, "Et%d" % i, [128, 1024], BF16) for i in range(2)]
        Et.append(otile[:].rearrange("p a b -> p (a b)").bitcast(BF16)[:, 0:1024])
        etflag = [False]
        esflag = [False]
        dblr = rot([1, 2, 3])
        krt = scoped(ph, "krt", [128, 4, 32], F32)
        cct = otile
        ckr = tb3[1][:, 0:384].rearrange("p (a b) -> p a b", a=4)
        kpf = tb3[0]
        sqi = [0]; rsi = [0]; fi = [0]; ei = [0]
        psr = rot([4, 5, 6, 7])

        st_uq = BUFB[:, 4 * T:4 * T + 6144].bitcast(F32).rearrange("p (k n) -> p k n", k=4)
        st_ukv = BUFB[:, 4 * T + 6144:4 * T + 10240].bitcast(F32).rearrange("p (k n) -> p k n", k=2)
        st_cos = BUFB[:, 4 * T + 10240:4 * T + 14336].bitcast(F32)
        st_sin = BUFB[:, 4 * T + 14336:4 * T + 18432].bitcast(F32)
        p.dma("sp", st_uq, g["w_uq"][l].rearrange("(kc p) n -> p kc n", p=128), w=["st_uq"])
        p.dma("sp", st_ukv, g["w_ukv"][l].rearrange("(kc p) n -> p kc n", p=128), w=["st_ukv"])
        p.op("act", lambda e: e.copy(out=wuq[:], in_=st_uq), r=["st_uq"], w=["wuq"])
        p.op("pool", lambda e: e.tensor_copy(out=wukv[:], in_=st_ukv), r=["st_ukv"], w=["wukv"])
        p.dma("sp", qag[:], g["qagT"][l], w=["qag"])
        p.dma("sp", kvg[:], g["kvgT"][l], w=["kvg"])
        p.dma("sp", qng[:], g["qng"][l], w=["qng"])
        p.dma("sp", kng[:], g["kng"][l], w=["kng"])
        p.dma("sp", st_cos[64:96, :], g["c_cos"][:, :], w=["st_cos"])
        p.dma("sp", st_sin[64:96, :], g["c_sin"][:, :], w=["st_sin"])
        p.op("pool", lambda e: e.tensor_copy(out=BUFA[64:96, 31232:33280], in_=st_cos[64:96, :]), r=["st_cos"], w=["cos"])
        p.op("pool", lambda e: e.tensor_copy(out=BUFA[64:96, 33280:35328], in_=st_sin[64:96, :]), r=["st_sin"], w=["sin"])
        p.dma("sp", protf[:], g["c_prot"][:, :], w=["prot"])
        p.op("pool", lambda e: e.memset(BUFA[:, VE:VE + 3072].rearrange("p (j c) -> p j c", c=128)[:, :, 64:128], 1.0), w=["V"])
        p.op("pool", lambda e: e.memset(BUFA[:, VO:VO + 3072].rearrange("p (j c) -> p j c", c=128)[:, :, 0:64], 1.0), w=["V"])
        p.op("pool", lambda e: e.memset(ckr[:], 0.0), w=[("tb3", 1)])
        for c in range(4):
            p.dma("sp", A(BUFA, c, 0, T), PROJ[c], r=["PROJ"], w=[("qn", c)])
        for tt in range(NT):
            pi = psr()
            for c in range(4):
                sq = sq3[sqi[0] % 3]; sk = ("sq3", sqi[0] % 3); sqi[0] += 1
                p.op("act", lambda e, sq=sq, c=c, tt=tt: e.activation(out=sq[:], in_=A(BUFA, c, tt * 512, 512), func=AF.Square),
                     r=[("qn", c)], w=[sk])
                p.op("pe", lambda e, sq=sq, c=c, pi=pi: e.matmul(PS[pi][:, :], ones_b[:, :], sq[:], start=(c == 0), stop=(c == 3)),
                     r=[sk, "ones_b"], w=[("ps", pi)])
            rs = rs3[rsi[0] % 3]; rk = ("rs3", rsi[0] % 3); rsi[0] += 1
            rstd_from_ps(pi, 128, 512, 512.0, rs, rk)
            for c in range(4):
                p.op("dve", lambda e, c=c, tt=tt, rs=rs: e.scalar_tensor_tensor(
                    out=A(BUFA, c, tt * 512, 512), in0=A(BUFA, c, tt * 512, 512), scalar=qag[:, c:c + 1], in1=rs[:, :],
                    op0=ALU.mult, op1=ALU.mult), r=[("qn", c), rk, "qag"], w=[("qn", c)])
        for tt in range(NT):
            ft = f32t[0]; fk = ("f32t", 0); fi[0] += 1
            p.dma("sp", ft[:], KV32[0:2, :, tt * 512:(tt + 1) * 512].rearrange("c p t -> p c t"), r=["KV32"], w=[fk])
            pi = psr()
            for c in range(2):
                sq = sq3[sqi[0] % 3]; sk = ("sq3", sqi[0] % 3); sqi[0] += 1
                p.op("act", lambda e, sq=sq, c=c, ft=ft: e.activation(out=sq[:], in_=ft[:, c, :], func=AF.Square), r=[fk], w=[sk])
                p.op("pe", lambda e, sq=sq, c=c, pi=pi: e.matmul(PS[pi][:, :], ones_b[:, :], sq[:], start=(c == 0), stop=(c == 1)),
                     r=[sk, "ones_b"], w=[("ps", pi)])
            rs = rs3[rsi[0] % 3]; rk = ("rs3", rsi[0] % 3); rsi[0] += 1
            rstd_from_ps(pi, 128, 512, 256.0, rs, rk)
            for c in range(2):
                p.op("dve", lambda e, c=c, ft=ft, rs=rs: e.scalar_tensor_tensor(
                    out=ft[:, c, :], in0=ft[:, c, :], scalar=kvg[:, c:c + 1], in1=rs[:, :], op0=ALU.mult, op1=ALU.mult),
                    r=[fk, rk, "kvg"], w=[fk])
                p.op("act", lambda e, c=c, ft=ft, tt=tt: e.copy(
                    out=BUFA[:, CKV + c * TK + tt * 512: CKV + c * TK + (tt + 1) * 512], in_=ft[:, c, :]), r=[fk], w=["ckvT"])
            if tt == 0:
                for t4 in range(4):
                    pi2 = psr()
                    for c in range(2):
                        p.op("pe", lambda e, pi2=pi2, c=c, t4=t4, ft=ft: e.transpose(
                            PS[pi2][:, c * 128:(c + 1) * 128], ft[:, c, t4 * 128:(t4 + 1) * 128], ident_f[:]),
                            r=[fk, "ident_f"], w=[("ps", pi2)])
                    p.op("dve", lambda e, pi2=pi2, t4=t4: e.tensor_copy(out=otile[:, t4, :], in_=PS[pi2][:, 0:256]),
                         r=[("ps", pi2)], w=["otile"])
                for b_ in range(2):
                    p.dma("sp", g["o_ckv"][b_, l, :, :].rearrange("(h p) f -> p h f", p=128), otile[:, 2 * b_:2 * b_ + 2, :], r=["otile"], w=["o_ckv"])
        for tt in range(NT):
            p.dma("sp", kpf[64:96, :], KV32[2, 64:96, tt * 512:(tt + 1) * 512], r=["KV32"], w=[("tb3", 0)])
            p.op("dve", lambda e, tt=tt: e.tensor_copy(out=BUFA[64:96, KPE + tt * 512: KPE + (tt + 1) * 512], in_=kpf[64:96, :]),
                 r=[("tb3", 0)], w=["kpe"])
            if tt == 0:
                pi2 = psr()
                for t4 in range(4):
                    p.op("pe", lambda e, pi2=pi2, t4=t4: e.transpose(
                        PS[pi2][:, t4 * 32:(t4 + 1) * 32], kpf[64:96, t4 * 128:(t4 + 1) * 128], ident_f[64:96, 64:96]),
                        r=[("tb3", 0), "ident_f"], w=[("ps", pi2)])
                p.op("dve", lambda e, pi2=pi2: e.tensor_copy(out=krt[:], in_=PS[pi2][:, 0:128].rearrange("p (a b) -> p a b", a=4)),
                     r=[("ps", pi2)], w=["krt"])
                for b_ in range(2):
                    p.dma("sp", g["o_kr"][b_, l, :, :].rearrange("(h p) f -> p h f", p=128), krt[:, 2 * b_:2 * b_ + 2, :], r=["krt"], w=["o_kr"])
        p.dma("sp", cct[:], g["cache_ckv"][l].rearrange("(a p) f -> p a f", p=128), w=["otile"])
        p.dma("sp", ckr[:, :, 64:96], g["cache_kr"][l].rearrange("(a p) f -> p a f", p=128), r=[], w=[("tb3", 1)])
        for c in range(2):
            pi2 = psr()
            for t4 in range(4):
                p.op("pe", lambda e, pi2=pi2, c=c, t4=t4: e.transpose(
                    PS[pi2][:, t4 * 128:(t4 + 1) * 128], cct[:, t4, c * 128:(c + 1) * 128], ident_f[:]),
                    r=["otile", "ident_f"], w=[("ps", pi2)])
            p.op("act", lambda e, pi2=pi2, c=c: e.copy(out=BUFA[:, CKV + c * TK + T: CKV + c * TK + T + 512], in_=PS[pi2][:, :]),
                 r=[("ps", pi2)], w=["ckvT"])
        pi2 = psr()
        for t4 in range(4):
            p.op("pe", lambda e, pi2=pi2, t4=t4: e.transpose(
                PS[pi2][0:96, t4 * 128:(t4 + 1) * 128], ckr[:, t4, :], ident_f[:]), r=[("tb3", 1), "ident_f"], w=[("ps", pi2)])
        p.op("dve", lambda e, pi2=pi2: e.tensor_copy(out=BUFA[64:96, KPE + T: KPE + T + 512], in_=PS[pi2][64:96, :]),
             r=[("ps", pi2)], w=["kpe"])

        COSO, SINO, KPS = 31232, 33280, 35328
        for ti in range(6):
            p.op("act", lambda e, ti=ti: e.activation(out=BUFA[64:96, KPS + ti * 512:KPS + (ti + 1) * 512],
                                                     in_=BUFA[64:96, KPE + ti * 512:KPE + (ti + 1) * 512], func=AF.Square),
                 r=["kpe"], w=["kps"])
        psr8 = rot(list(range(8)))
        ucnt = [0]

        class U_:
            pass

        def stageA(u):
            pi = psr8(); u.pi = pi
            i3 = ucnt[0] % 3; ucnt[0] += 1
            u.i3 = i3
            sq = sq3[i3]; sk = ("sq3", i3)
            if u.kind == "q":
                for kc in range(4):
                    p.op("pe", lambda e, pi=pi, kc=kc, ti=u.ti, h=u.h: e.matmul(
                        PS[pi][0:96, :], wuq[:, kc, h * 96:(h + 1) * 96], A(BUFA, kc, ti * 512, 512),
                        start=(kc == 0), stop=(kc == 3)), r=["wuq", ("qn", kc)], w=[("ps", pi)])
                u.rows = 96
            else:
                for kc in range(2):
                    p.op("pe", lambda e, pi=pi, kc=kc, ti=u.ti, h=u.h: e.matmul(
                        PS[pi][0:64, :], wukv[:, kc, h * 128:h * 128 + 64],
                        BUFA[:, CKV + kc * TK + ti * 512: CKV + kc * TK + (ti + 1) * 512],
                        start=(kc == 0), stop=(kc == 1)), r=["wukv", "ckvT"], w=[("ps", pi)])
                u.rows = 64
            rows = u.rows
            p.op("act", lambda e, pi=pi, sq=sq, rows=rows: e.activation(out=sq[0:rows, :], in_=PS[pi][0:rows, :], func=AF.Square),
                 r=[("ps", pi)], w=[sk])

        def stageB(u):
            pi, i3 = u.pi, u.i3
            sq = sq3[i3]; sk = ("sq3", i3)
            rs = rs3[i3]; rk = ("rs3", i3)
            pj = psr8()
            if u.kind == "q":
                p.op("pe", lambda e, sq=sq, pj=pj: e.matmul(PS[pj][0:96, :], ones_b[0:96, 0:96], sq[0:96, :], start=True, stop=True),
                     r=[sk, "ones_b"], w=[("ps", pj)])
                gv, gkey, dst, dk = qng, "qng", QH, "qkq"
                do_rope = u.ti >= 1
            else:
                p.op("pe", lambda e, sq=sq, pj=pj: e.matmul(PS[pj][0:96, :], ones_b[0:64, 0:96], sq[0:64, :], start=True, stop=False),
                     r=[sk, "ones_b"], w=[("ps", pj)])
                p.op("pe", lambda e, pj=pj, ti=u.ti: e.matmul(PS[pj][0:96, :], ones_b[64:96, 0:96],
                                                              BUFA[64:96, KPS + ti * 512:KPS + (ti + 1) * 512], start=False, stop=True),
                     r=["kps", "ones_b"], w=[("ps", pj)])
                gv, gkey, dst, dk = kng, "kng", KH, "qkk"
                do_rope = 1 <= u.ti <= 4
            rstd_from_ps(pj, 96, 512, 96.0, rs, rk)
            u.do_rope = do_rope
            u.dcol = dst + u.ti * 512
            u.dk = dk
            dcol = u.dcol
            if do_rope:
                tb = tb3[i3]; tk_ = ("tb3", i3)
                outf = lambda r0, r1, tb=tb: tb[r0:r1, :]
                wkeys = [tk_]
            else:
                outf = lambda r0, r1, dcol=dcol: BUFA[r0:r1, dcol:dcol + 512]
                wkeys = [dk]
            if u.kind == "q":
                p.op("dve", lambda e, rs=rs, gv=gv, pi=pi, outf=outf: e.scalar_tensor_tensor(
                    out=outf(0, 96), in0=PS[pi][0:96, :], scalar=gv[0:96, 0:1], in1=rs[0:96, :],
                    op0=ALU.mult, op1=ALU.mult), r=[("ps", pi), rk, gkey], w=wkeys)
            else:
                p.op("dve", lambda e, rs=rs, gv=gv, pi=pi, outf=outf: e.scalar_tensor_tensor(
                    out=outf(0, 64), in0=PS[pi][0:64, :], scalar=gv[0:64, 0:1], in1=rs[0:64, :],
                    op0=ALU.mult, op1=ALU.mult), r=[("ps", pi), rk, gkey], w=wkeys)
                p.op("dve", lambda e, rs=rs, gv=gv, ti=u.ti, outf=outf: e.scalar_tensor_tensor(
                    out=outf(64, 96), in0=BUFA[64:96, KPE + ti * 512:KPE + (ti + 1) * 512], scalar=gv[64:96, 0:1], in1=rs[64:96, :],
                    op0=ALU.mult, op1=ALU.mult), r=["kpe", rk, gkey], w=wkeys)

        def stageC(u):
            if not u.do_rope:
                return
            i3 = u.i3
            tb = tb3[i3]; tk_ = ("tb3", i3)
            ta = ta2[i3 % 2]; tak = ("ta2", i3 % 2)
            ci = u.ti - 1
            dcol = u.dcol
            dk = u.dk
            pr = psr8()
            p.op("pe", lambda e, pr=pr, tb=tb: e.matmul(PS[pr][0:96, :], protf[64:96, 0:96], tb[64:96, :], start=True, stop=True),
                 r=[tk_, "prot"], w=[("ps", pr)])
            p.op("act", lambda e, dcol=dcol, tb=tb: e.copy(out=BUFA[0:64, dcol:dcol + 512], in_=tb[0:64, :]), r=[tk_], w=[dk])
            p.op("dve", lambda e, ci=ci, tb=tb, ta=ta: e.tensor_tensor(out=ta[64:96, :], in0=tb[64:96, :],
                                                                     in1=BUFA[64:96, COSO + ci * 512:COSO + (ci + 1) * 512], op=ALU.mult),
                 r=[tk_, "cos"], w=[tak])
            p.op("dve", lambda e, ci=ci, pr=pr, tb=tb: e.tensor_tensor(out=tb[64:96, :], in0=PS[pr][64:96, :],
                                                                     in1=BUFA[64:96, SINO + ci * 512:SINO + (ci + 1) * 512], op=ALU.mult),
                 r=[("ps", pr), "sin", tk_], w=[tk_])
            p.op("dve", lambda e, dcol=dcol, ta=ta, tb=tb: e.tensor_tensor(out=BUFA[64:96, dcol:dcol + 512], in0=ta[64:96, :], in1=tb[64:96, :], op=ALU.add),
                 r=[tak, tk_], w=[dk])

        for h in range(8):
            units = []
            for (kind, ti) in [("q", tt) for tt in range(NT)] + [("k", kt) for kt in range(6)]:
                u = U_(); u.kind = kind; u.ti = ti; u.h = h
                units.append(u)
            nU = len(units)
            for st_ in range(nU + 2):
                if st_ < nU:
                    stageA(units[st_])
                if 0 <= st_ - 1 < nU:
                    stageB(units[st_ - 1])
                if 0 <= st_ - 2 < nU:
                    stageC(units[st_ - 2])
            voff = (VE if h % 2 == 0 else VO)
            c0 = (h % 2) * 64
            for g3 in range(3):
                pi = psr()
                for j in range(8):
                    kt = g3 * 8 + j
                    for kc in range(2):
                        p.op("pe", lambda e, pi=pi, j=j, kt=kt, kc=kc, h=h: e.matmul(
                            PS[pi][:, j * 64:(j + 1) * 64], BUFA[:, CKV + kc * TK + kt * 128: CKV + kc * TK + (kt + 1) * 128],
                            wukv[:, kc, h * 128 + 64:h * 128 + 128], start=(kc == 0), stop=(kc == 1)),
                            r=["wukv", "ckvT"], w=[("ps", pi)])
                dstv = BUFA[:, voff + g3 * 1024: voff + (g3 + 1) * 1024].rearrange("p (j c) -> p j c", c=128)[:, :, c0:c0 + 64]
                p.op("act", lambda e, pi=pi, dstv=dstv: e.copy(out=dstv, in_=PS[pi][:, :].rearrange("p (j c) -> p j c", c=64)),
                     r=[("ps", pi)], w=["V"])
            r0 = c0
            jobs = [(0, 256, [0, 1]), (256, 256, [2, 3])] + [(512 * tt, 512, list(range(4, 24))) for tt in range(1, 5)]
            for ji, (q0, nq, kts) in enumerate(jobs):
                po, pd = (0, 1)
                pairs = [(kts[i], kts[i + 1]) for i in range(0, len(kts), 2)]
                npair = len(pairs)

                def emitS(pi_, q0=q0, nq=nq, pairs=pairs):
                    dbl = dblr()
                    for hf in range(2):
                        kt = pairs[pi_][hf]
                        p.op("pe", lambda e, dbl=dbl, hf=hf, kt=kt, q0=q0, nq=nq: e.matmul(
                            PS[2 * dbl + hf][:, 0:nq], BUFA[0:96, KH + kt * 128: KH + (kt + 1) * 128], BUFA[0:96, QH + q0: QH + q0 + nq],
                            start=True, stop=True), r=["qkq", "qkk"], w=[("ps", 2 * dbl + hf)])
                    return dbl
                pend = [emitS(0)]
                if npair > 1:
                    pend.append(emitS(1))
                if npair > 2:
                    pend.append(emitS(2))
                first = {"dve": True, "pool": True}
                used = []
                for pi_ in range(npair):
                    dbl = pend[pi_]
                    E = Et[ei[0] % 3]; ek = ("E", ei[0] % 3); ei[0] += 1
                    extra = ["otile"] if (E is Et[2] and not etflag[0]) else []
                    if extra:
                        etflag[0] = True
                    p.op("act", lambda e, dbl=dbl, E=E, nq=nq: e.activation(
                        out=E[:, :].rearrange("p (h c) -> p h c", h=2)[:, :, 0:nq],
                        in_=PSD[dbl][:, :].rearrange("p (h c) -> p h c", h=2)[:, :, 0:nq], func=AF.Exp, scale=SCALE),
                        r=[("ps", 2 * dbl), ("ps", 2 * dbl + 1)], w=[ek] + extra)
                    for hf in range(2):
                        kt = pairs[pi_][hf]
                        first_mm = (pi_ == 0 and hf == 0)
                        last_mm = (pi_ == npair - 1 and hf == 1)
                        p.op("pe", lambda e, E=E, kt=kt, nq=nq, po=po, hf=hf, first_mm=first_mm, last_mm=last_mm, voff=voff: e.matmul(
                            PS[po][:, 0:nq], BUFA[:, voff + kt * 128: voff + (kt + 1) * 128], E[:, hf * 512:hf * 512 + nq],
                            start=first_mm, stop=last_mm), r=[ek, "V"], w=[("ps", po)])
                    if pi_ + 3 < npair:
                        pend.append(emitS(pi_ + 3))
                oh = 64 - r0
                dsb = tb3[ji % 3]; dk_ = ("tb3", ji % 3)
                p.op("act", lambda e, dsb=dsb, po=po, oh=oh, nq=nq: e.copy(out=dsb[oh:oh + 64, 0:nq], in_=PS[po][oh:oh + 64, 0:nq]),
                     r=[("ps", po)], w=[dk_])
                p.op("pe", lambda e, dsb=dsb, pd=pd, oh=oh, r0=r0, nq=nq: e.matmul(
                    PS[pd][r0:r0 + 64, 0:nq], ident_f[oh:oh + 64, oh:oh + 64], dsb[oh:oh + 64, 0:nq], start=True, stop=True),
                    r=[dk_, "ident_f"], w=[("ps", pd)])
                rs = rs3[rsi[0] % 3]; rk = ("rs3", rsi[0] % 3); rsi[0] += 1
                p.op("dve", lambda e, rs=rs, pd=pd, nq=nq, r0=r0: e.reciprocal(out=rs[r0:r0 + 64, 0:nq], in_=PS[pd][r0:r0 + 64, 0:nq]),
                     r=[("ps", pd)], w=[rk])
                p.op("dve", lambda e, rs=rs, po=po, nq=nq, r0=r0, q0=q0, h=h: e.tensor_tensor(
                    out=BUFB[r0:r0 + 64, (h // 2) * T + q0:(h // 2) * T + q0 + nq], in0=PS[po][r0:r0 + 64, 0:nq],
                    in1=rs[r0:r0 + 64, 0:nq], op=ALU.mult), r=[("ps", po), rk], w=["BI"])
    p.barrier()
    if k.stop_after == "MLA":
        return
    mixers2(k, g, l, scoped)


PADW = T + 64


def pcol(t):
    for si, (s0, n) in enumerate(SEGS):
        if s0 <= t < s0 + n:
            return t + 16 * si + 8
    raise ValueError


def mixers2(k, g, l, scoped):
    nc = k.nc
    p = k.p
    PS = g["PS"]; A = g["A"]
    BUFA = g["BUFA"]; BUFB = g["BUFB"]
    PROJ = g["PROJ"]
    psr = rot(list(range(8)))
    with ExitStack() as ph:
        cw = scoped(ph, "cw", [128, 4, 3], F32)
        cb = scoped(ph, "cb", [128, 4], F32)
        ub = [scoped(ph, "cu0", [128, 3, T], BF16)] * 2
        v = scoped(ph, "cv", [128, T], F32)
        acc = scoped(ph, "cacc", [128, T], F32)
        p.dma("sp", cw[:], g["conv_wT"][l], w=["cw"])
        p.dma("sp", cb[:], g["conv_bT"][l], w=["cb"])
        for j in range(4):
            u = ub[0]; uk = ("cu", 0)
            for i3, c in enumerate((4 + j, 8 + j, 12 + j)):
                p.dma("sp", u[:, i3, :], PROJ[c], r=["PROJ"], w=[uk])
            p.op("dve", lambda e, u=u: e.tensor_tensor(out=v[:], in0=u[:, 2, :], in1=u[:, 0, :], op=ALU.mult), r=[uk], w=["cv"])
            p.op("dve", lambda e, j=j: e.tensor_scalar(out=acc[:], in0=v[:], scalar1=cw[:, j, 1:2], scalar2=cb[:, j:j + 1],
                                                      op0=ALU.mult, op1=ALU.add), r=["cv", "cw", "cb"], w=["cacc"])
            for (s0, n) in SEGS:
                p.op("dve", lambda e, j=j, s0=s0, n=n: e.scalar_tensor_tensor(
                    out=acc[:, s0 + 1:s0 + n], in0=v[:, s0:s0 + n - 1], scalar=cw[:, j, 0:1], in1=acc[:, s0 + 1:s0 + n],
                    op0=ALU.mult, op1=ALU.add), r=["cv", "cacc", "cw"], w=["cacc"])
                p.op("dve", lambda e, j=j, s0=s0, n=n: e.scalar_tensor_tensor(
                    out=acc[:, s0:s0 + n - 1], in0=v[:, s0 + 1:s0 + n], scalar=cw[:, j, 2:3], in1=acc[:, s0:s0 + n - 1],
                    op0=ALU.mult, op1=ALU.add), r=["cv", "cacc", "cw"], w=["cacc"])
            p.op("dve", lambda e, j=j, u=u: e.tensor_tensor(out=A(BUFB, 4 + j, 0, T), in0=acc[:], in1=u[:, 1, :], op=ALU.mult),
                 r=["cacc", uk], w=["BI"])
    p.barrier()
    with ExitStack() as ph:
        pu = scoped(ph, "pu", [128, PADW], BF16)
        w2 = scoped(ph, "pw2", [128, PADW], F32)
        w4 = scoped(ph, "pw4", [128, PADW], F32)
        inv = scoped(ph, "pinv", [128, T], F32)
        pm = scoped(ph, "pm", [128, T], BF16)
        pw = scoped(ph, "pw", [128, 4, 128], BF16)
        psc = scoped(ph, "psc", [128, 4], F32)
        p.dma("pool", pw[:], g["pool_w"][l].rearrange("g c d -> c g d"), w=["pw"])
        p.dma("sp", psc[:], g["pool_sT"][l], w=["psc"])
        p.op("pool", lambda e: e.memset(pu[:], 0.0), w=["pu"])
        p.op("pool", lambda e: e.memset(w2[:], 0.0), w=["pw2"])
        p.op("pool", lambda e: e.memset(w4[:], 0.0), w=["pw4"])
        W_ = PADW
        for gi in range(4):
            for (s0, n) in SEGS:
                p.dma("sp", pu[:, pcol(s0):pcol(s0) + n], PROJ[16 + gi, :, s0:s0 + n], r=["PROJ"], w=["pu"])
            p.dma("sp", inv[:], g["c_pinv"][gi], w=["pinv"])
            p.op("dve", lambda e: e.tensor_tensor(out=w2[:, 1:W_], in0=pu[:, 0:W_ - 1], in1=pu[:, 1:W_], op=ALU.add), r=["pu", "pw4"], w=["pw2"])
            cur, curk, oth, othk = w2, "pw2", w4, "pw4"
            sh = 1
            for lev in range(gi):
                p.op("dve", lambda e, cur=cur, oth=oth, sh=sh: e.tensor_tensor(
                    out=oth[:, sh:W_ - sh], in0=cur[:, 0:W_ - 2 * sh], in1=cur[:, 2 * sh:W_], op=ALU.add), r=[curk], w=[othk])
                cur, curk, oth, othk = oth, othk, cur, curk
                sh *= 2
            for (s0, n) in SEGS:
                c0 = pcol(s0)
                p.op("dve", lambda e, cur=cur, s0=s0, n=n, c0=c0: e.tensor_tensor(
                    out=cur[:, c0:c0 + n], in0=cur[:, c0:c0 + n], in1=inv[:, s0:s0 + n], op=ALU.mult), r=[curk, "pinv"], w=[curk])
                p.op("dve", lambda e, cur=cur, s0=s0, n=n, c0=c0: e.tensor_tensor(
                    out=pm[:, s0:s0 + n], in0=cur[:, c0:c0 + n], in1=pu[:, c0:c0 + n], op=ALU.subtract), r=[curk, "pu"], w=["pm"])
            for tt in range(NT):
                pi = psr()
                p.op("pe", lambda e, pi=pi, gi=gi, tt=tt: e.matmul(PS[pi][:, :], pw[:, gi, :], pm[:, tt * 512:(tt + 1) * 512], start=True, stop=True),
                     r=["pw", "pm"], w=[("ps", pi)])
                p.op("act", lambda e, pi=pi, gi=gi, tt=tt: e.activation(
                    out=A(BUFB, 12 + gi, tt * 512, 512), in_=PS[pi][:, :], func=AF.Copy, scale=psc[:, gi:gi + 1]),
                    r=[("ps", pi), "psc"], w=["BI"])
    p.barrier()
    ssm(k, g, l, scoped)
    if "DBG_BI" in k.debug:
        p.barrier()
        for c in range(16):
            p.dma("sp", k.DBG_BI[c], A(BUFB, c, 0, T), r=["BI"], w=["DBG"])


def ssm(k, g, l, scoped):
    nc = k.nc
    p = k.p
    PS = g["PS"]
    BUFA = g["BUFA"]; BUFB = g["BUFB"]
    ident_b = g["ident_b"]; CAA = g["CAA"]; CAB = g["CAB"]
    UTOK = g["UTOK"]; YTOK = g["YTOK"]
    SSM_BLT = g["SSM_BLT"]; SSM_ML = g["SSM_ML"]; SSM_CS = g["SSM_CS"]
    psr = rot(list(range(8)))
    SEQ = [(0, 16, 0), (17, 16, 16), (34, 128, 32)]
    NCOL = 163
    UL0 = 8 * T
    Sv = BUFA[:, 0:20864].bitcast(F32).rearrange("p (r d g c) -> p r d g c", r=2, d=2, g=16)

    def UL(gg, ch, q0, n):
        o = UL0 + (gg * 2 + ch) * 160 + q0
        return BUFB[:, o:o + n]

    with ExitStack() as ph:
        Sbf = scoped(ph, "Sbf", [128, 2, 2, 16, NCOL], BF16)
        blt = [scoped(ph, "sblt%d" % i, [128, 2, 128], BF16) for i in range(4)]
        mlw = [scoped(ph, "mlw%d" % i, [128, 2, 2, 256], BF16) for i in range(2)]
        csw = [scoped(ph, "csw%d" % i, [128, 2, 2, 256], BF16) for i in range(2)]
        yti = [scoped(ph, "yti%d" % i, [128, 512], BF16) for i in range(3)]
        tf = [scoped(ph, "tf%d" % i, [128, 2, 16, 2], F32) for i in range(2)]
        tb = [scoped(ph, "tb%d" % i, [128, 2, 16, 2], F32) for i in range(2)]
        p.dma("sp", BUFA[0:32, 0:8192], UTOK[0:512, :].rearrange("(q i) f -> q (i f)", i=16), r=["UTOK"], w=["UQ"])
        p.dma("sp", BUFA[:, 8192:16384], UTOK[512:2560, :].rearrange("(q i) f -> q (i f)", i=16), r=["UTOK"], w=["UQ"])
        p.op("dve", lambda e: e.tensor_copy(
            out=BUFA[0:32, 16384:24576].rearrange("p (g i c) -> p g i c", g=32, i=16),
            in_=BUFA[0:32, 0:8192].rearrange("p (i g c) -> p g i c", i=16, g=32)), r=["UQ"], w=["UQ2"])
        p.op("pool", lambda e: e.tensor_copy(
            out=BUFA[:, 24576:32768].rearrange("p (g i c) -> p g i c", g=32, i=16),
            in_=BUFA[:, 8192:16384].rearrange("p (i g c) -> p g i c", i=16, g=32)), r=["UQ"], w=["UQ2"])
        n_ = 0
        for gg in range(32):
            for ch in range(2):
                pi = psr()
                psb = PS[pi][:, 0:80].bitcast(BF16)
                o_ = gg * 256 + ch * 128
                p.op("pe", lambda e, psb=psb, o_=o_: e.transpose(psb[:, 0:32], BUFA[0:32, 16384 + o_:16384 + o_ + 128], ident_b[0:32, 0:32]),
                     r=["UQ2", "ident_b"], w=[("ps", pi)])
                p.op("pe", lambda e, psb=psb, o_=o_: e.transpose(psb[:, 32:160], BUFA[:, 24576 + o_:24576 + o_ + 128], ident_b[:, :]),
                     r=["UQ2", "ident_b"], w=[("ps", pi)])
                if n_ % 2 == 0:
                    p.op("dve", lambda e, psb=psb, gg=gg, ch=ch: e.tensor_copy(out=UL(gg, ch, 0, 160), in_=psb[:, 0:160]), r=[("ps", pi)], w=["UL"])
                else:
                    p.op("act", lambda e, psb=psb, gg=gg, ch=ch: e.copy(out=UL(gg, ch, 0, 160), in_=psb[:, 0:160]), r=[("ps", pi)], w=["UL"])
                n_ += 1
        p.op("pool", lambda e: e.memset(BUFA[:, 0:20864], 0.0), w=["UQ", "UQ2", ("S", 0), ("S", 1)])
        for d in range(2):
            col = 34 if d == 0 else 162
            for reim in range(2):
                for gh in range(2):
                    p.dma("sp", Sv[gh * 64:(gh + 1) * 64, reim, d, :, col],
                          g["state_in"][l, d, reim, gh * 16:(gh + 1) * 16, :].rearrange("g n -> n g"), w=[("S", d)], slow=True)
        bi_ = 0
        for d in range(2):
            for gg in range(32):
                gh, g16 = gg // 16, gg % 16
                r0 = gh * 64
                bt = blt[bi_ % 4]; bk = ("sblt", bi_ % 4); bi_ += 1
                p.dma("sp", bt[:], SSM_BLT[l, d, gg].rearrange("c p f -> p c f"), r=["SSM_BLT"], w=[bk])
                pi = psr()
                for reim in range(2):
                    for ch in range(2):
                        p.op("pe", lambda e, pi=pi, r0=r0, reim=reim, ch=ch, bt=bt, gg=gg: e.matmul(
                            PS[pi][r0:r0 + 64, reim * 160:(reim + 1) * 160], bt[:, ch, reim * 64:(reim + 1) * 64], UL(gg, ch, 0, 160),
                            start=(ch == 0), stop=(ch == 1)), r=[bk, "UL"], w=[("ps", pi)])
                for (b0, Q, q0) in SEQ:
                    c0 = b0 + (1 - d)
                    p.op("dve", lambda e, pi=pi, r0=r0, d=d, g16=g16, c0=c0, Q=Q, q0=q0: e.tensor_copy(
                        out=Sv[r0:r0 + 64, :, d, g16, c0:c0 + Q],
                        in_=PS[pi][r0:r0 + 64, 0:320].rearrange("p (r q) -> p r q", r=2)[:, :, q0:q0 + Q]),
                        r=[("ps", pi)], w=[("S", d)])
        if "SSMD" in k.debug and l == 0:
            d1 = nc.dram_tensor("D_SIN", [128, 10432], F32, kind="ExternalOutput").ap()
            p.dma("sp", d1, BUFA[:, 0:20864].bitcast(F32), r=[("S", 0), ("S", 1)], w=["D_SIN"])
            d2 = nc.dram_tensor("D_UL", [128, 10240], BF16, kind="ExternalOutput").ap()
            p.dma("sp", d2, BUFB[:, UL0:UL0 + 10240], r=["UL"], w=["D_UL"])
        for d, eng, tmp in ((0, "dve", tf), (1, "pool", tb)):
            key = ("S", d)
            fs = slice(d * 16, (d + 1) * 16)
            ca4 = CAA[:, l, :, fs].unsqueeze(3).to_broadcast([128, 2, 16, 2])
            cb4 = CAB[:, l, :, fs].unsqueeze(3).to_broadcast([128, 2, 16, 2])
            ca3 = CAA[:, l, :, fs]
            cb3 = CAB[:, l, :, fs]
            t1, t2 = tmp
            tk = "tmp%d" % d
            steps = range(16) if d == 0 else range(15, -1, -1)
            for q in steps:
                pc = q if d == 0 else q + 1
                ncl = q + 1 if d == 0 else q
                prev = Sv[:, :, d, :, pc:pc + 18:17]
                prsw = Sv[:, ::-1, d, :, pc:pc + 18:17]
                new = Sv[:, :, d, :, ncl:ncl + 18:17]
                p.op(eng, lambda e, prev=prev, t1=t1, ca4=ca4: e.tensor_tensor(out=t1[:], in0=prev, in1=ca4, op=ALU.mult), r=[key], w=[tk + "a"])
                p.op(eng, lambda e, prsw=prsw, t2=t2, cb4=cb4: e.tensor_tensor(out=t2[:], in0=prsw, in1=cb4, op=ALU.mult), r=[key], w=[tk + "b"])
                p.op(eng, lambda e, t1=t1, t2=t2: e.tensor_tensor(out=t1[:], in0=t1[:], in1=t2[:], op=ALU.add), r=[tk + "a", tk + "b"], w=[tk + "a"])
                p.op(eng, lambda e, new=new, t1=t1: e.tensor_tensor(out=new, in0=new, in1=t1[:], op=ALU.add), r=[tk + "a", key], w=[key])
            steps = range(128) if d == 0 else range(127, -1, -1)
            for q in steps:
                pc = 34 + (q if d == 0 else q + 1)
                ncl = 34 + (q + 1 if d == 0 else q)
                prev = Sv[:, :, d, :, pc]
                prsw = Sv[:, ::-1, d, :, pc]
                new = Sv[:, :, d, :, ncl]
                p.op(eng, lambda e, prev=prev, t1=t1, ca3=ca3: e.tensor_tensor(out=t1[:, :, :, 0], in0=prev, in1=ca3, op=ALU.mult), r=[key], w=[tk + "a"])
                p.op(eng, lambda e, prsw=prsw, t2=t2, cb3=cb3: e.tensor_tensor(out=t2[:, :, :, 0], in0=prsw, in1=cb3, op=ALU.mult), r=[key], w=[tk + "b"])
                p.op(eng, lambda e, t1=t1, t2=t2: e.tensor_tensor(out=t1[:, :, :, 0], in0=t1[:, :, :, 0], in1=t2[:, :, :, 0], op=ALU.add), r=[tk + "a", tk + "b"], w=[tk + "a"])
                p.op(eng, lambda e, new=new, t1=t1: e.tensor_tensor(out=new, in0=new, in1=t1[:, :, :, 0], op=ALU.add), r=[tk + "a", key], w=[key])
        if "SSMD" in k.debug and l == 0:
            d4 = nc.dram_tensor("D_CAA", [128, DEPTH, 2, 32], F32, kind="ExternalOutput").ap()
            p.dma("sp", d4, CAA[:], w=["D_CAA"])
            d5 = nc.dram_tensor("D_CAB", [128, DEPTH, 2, 32], F32, kind="ExternalOutput").ap()
            p.dma("sp", d5, CAB[:], w=["D_CAB"])
            d3 = nc.dram_tensor("D_S", [128, 10432], F32, kind="ExternalOutput").ap()
            p.dma("sp", d3, BUFA[:, 0:20864].bitcast(F32), r=[("S", 0), ("S", 1)], w=["D_S"])
        for b_ in range(2):
            b0 = SEQ[b_][0]
            for d in range(2):
                col = b0 + 16 if d == 0 else b0
                for reim in range(2):
                    for gh in range(2):
                        p.dma("sp", g["o_ssm"][b_, l, d, reim, gh * 16:(gh + 1) * 16, :].rearrange("g n -> n g"),
                              Sv[gh * 64:(gh + 1) * 64, reim, d, :, col], r=[("S", d)], w=["o_ssm"], slow=True)
        p.op("act", lambda e: e.copy(out=Sbf[:].rearrange("p r d g c -> p (r d g c)"), in_=BUFA[:, 0:20864].bitcast(F32)),
             r=[("S", 0), ("S", 1)], w=["Sbf"])
        p.barrier()
        n_ = 0
        for gg in range(32):
            gh, g16 = gg // 16, gg % 16
            r0 = gh * 64
            mw = mlw[gg % 2]; mk = ("mlw", gg % 2)
            cw = csw[gg % 2]; ck = ("csw", gg % 2)
            for d in range(2):
                p.dma("sp", mw[:, d], SSM_ML[l, d, gg].rearrange("c p f -> p c f"), r=["SSM_ML"], w=[mk])
                p.dma("sp", cw[r0:r0 + 64, d], SSM_CS[l, d, gg], r=["SSM_CS"], w=[ck])
            for (rows, q0, ybase, sc0) in ((16, 0, 0, 0), (16, 16, 16384, 17), (128, 32, 8192, 34)):
                pi = psr()
                mms = []
                for d in range(2):
                    for ch in range(2):
                        mms.append((UL(gg, ch, q0, rows), mw[:, d, ch, :], [mk, "UL"]))
                    for reim in range(2):
                        cc0 = sc0 + d
                        lh = Sbf[r0:r0 + 64, reim, d, g16, cc0:cc0 + rows]
                        mms.append((lh, cw[r0:r0 + 64, d, reim, :], [ck, "Sbf"]))
                for mi, (lh, rh, rk) in enumerate(mms):
                    p.op("pe", lambda e, pi=pi, lh=lh, rh=rh, mi=mi, rows=rows: e.matmul(
                        PS[pi][0:rows, 0:256], lh, rh, start=(mi == 0), stop=(mi == len(mms) - 1)), r=rk, w=[("ps", pi)])
                dst = BUFA[0:rows, ybase:ybase + 8192].rearrange("p (i g c) -> p i g c", i=16, g=32)[:, :, gg, :]
                src = PS[pi][0:rows, 0:256].rearrange("p (i c) -> p i c", i=16)
                if n_ % 2 == 0:
                    p.op("dve", lambda e, dst=dst, src=src: e.tensor_copy(out=dst, in_=src), r=[("ps", pi)], w=["YQ"])
                else:
                    p.op("act", lambda e, dst=dst, src=src: e.copy(out=dst, in_=src), r=[("ps", pi)], w=["YQ"])
                n_ += 1
        p.dma("sp", YTOK[0:256, :].rearrange("(q i) f -> q (i f)", i=16), BUFA[0:16, 0:8192], r=["YQ"], w=["YTOK"])
        p.dma("sp", YTOK[256:512, :].rearrange("(q i) f -> q (i f)", i=16), BUFA[0:16, 16384:24576], r=["YQ"], w=["YTOK"])
        p.dma("sp", YTOK[512:2560, :].rearrange("(q i) f -> q (i f)", i=16), BUFA[:, 8192:16384], r=["YQ"], w=["YTOK"])
        for ti in range(T // 128):
            yt = yti[ti % 3]; yk = ("yti", ti % 3)
            p.dma("sp", yt[:], YTOK[ti * 128:(ti + 1) * 128, :], r=["YTOK"], w=[yk])
            pi = psr()
            psb = PS[pi][:, 0:256].bitcast(BF16)
            for c in range(4):
                p.op("pe", lambda e, psb=psb, c=c, yt=yt: e.transpose(psb[:, c * 128:(c + 1) * 128], yt[:, c * 128:(c + 1) * 128], ident_b[:, :]),
                     r=[yk, "ident_b"], w=[("ps", pi)])
            dst = BUFB[:, 8 * T:12 * T].rearrange("p (c t) -> p c t", c=4)[:, :, ti * 128:(ti + 1) * 128]
            src = psb.rearrange("p (c t) -> p c t", c=4)
            if ti % 2 == 0:
                p.op("dve", lambda e, dst=dst, src=src: e.tensor_copy(out=dst, in_=src), r=[("ps", pi)], w=["BI", "UL"])
            else:
                p.op("act", lambda e, dst=dst, src=src: e.copy(out=dst, in_=src), r=[("ps", pi)], w=["BI", "UL"])
    p.barrier()


def tail(k, g, l, scoped, norm):
    nc = k.nc
    p = k.p
    PS = g["PS"]; A = g["A"]
    BUFA = g["BUFA"]; BUFB = g["BUFB"]
    modv = g["modv"]; A2 = g["A2"]
    XT = g["XT"]; GS = g["GS"]; MT = g["MT"]; AT = g["AT"]
    psr = rot(list(range(8)))

    def mod_j(tt):
        return 0 if tt == 0 else 1

    def kcp(src):
        return src.rearrange("(kc p) n -> p kc n", p=128)

    with ExitStack() as ph:
        wm = [scoped(ph, "wm%d" % i, [128, 20, 256], BF16) for i in range(2)]
        sgt = scoped(ph, "sgt", [128, 512], F32)
        acc = [scoped(ph, "macc%d" % i, [128, 512], F32) for i in range(2)]
        tmp = [scoped(ph, "mtmp%d" % i, [128, 512], F32) for i in range(3)]
        mstg = [scoped(ph, "mstg%d" % i, [128, T], BF16) for i in range(2)]
        ti_ = [0]

        def T_():
            i = ti_[0] % 3
            ti_[0] += 1
            return tmp[i], ("mtmp", i)
        stages = [BUFA[:, 20480 + i * 10240: 20480 + (i + 1) * 10240].bitcast(F32).rearrange("p (k n) -> p k n", k=20) for i in range(2)]
        pieces = []
        for fg in range(8):
            c0 = fg * 256
            pcs = []
            for bi, src in enumerate((g["w_mla_o"][l][:, c0:c0 + 256], g["w_conv_o"][l][:, c0:c0 + 256],
                                      g["w_pool_o"][l][:, c0:c0 + 256], g["w_glu"][l][:, c0:c0 + 256],
                                      g["w_glu"][l][:, 2048 + c0:2048 + c0 + 256])):
                pcs.append((src, bi * 4, 4, 0, 256))
            pieces.append(pcs)

        def compute(fg, w, wk):
            for f2 in range(2):
                fc = fg * 2 + f2
                gb = fc % 2
                gk = ("gbuf", gb)
                for i in range(4):
                    p.dma("sp", BUFA[:, (gb * 4 + i) * T:(gb * 4 + i + 1) * T], GS[i * 16 + fc], r=["GS"], w=[gk])

                def G(i, tt, gb=gb):
                    return BUFA[:, (gb * 4 + i) * T + tt * 512:(gb * 4 + i) * T + (tt + 1) * 512]
                ms = mstg[fc % 2]; mk = ("mstg", fc % 2)
                cs = f2 * 128
                for tt in range(NT):
                    banks = [psr() for _ in range(5)]
                    specs = [(0, 0), (4, 4), (12, 8), (16, 8), (8, 12)]
                    for bnk, (wi_, bch) in zip(banks, specs):
                        for kc in range(4):
                            p.op("pe", lambda e, bnk=bnk, wi_=wi_, bch=bch, kc=kc, tt=tt, w=w, cs=cs: e.matmul(
                                PS[bnk][:, :], w[:, wi_ + kc, cs:cs + 128], A(BUFB, bch + kc, tt * 512, 512),
                                start=(kc == 0), stop=(kc == 3)), r=[wk, "BI"], w=[("ps", bnk)])
                    pa, pb, pga, pgg, pd = banks
                    p.op("act", lambda e, pgg=pgg: e.activation(out=sgt[:], in_=PS[pgg][:, :], func=AF.Sigmoid),
                         r=[("ps", pgg)], w=["sgt"])
                    ac = acc[tt % 2]; ak = ("macc", tt % 2)
                    p.op("dve", lambda e, ac=ac, pa=pa, tt=tt, G=G: e.tensor_tensor(out=ac[:], in0=PS[pa][:, :], in1=G(0, tt), op=ALU.mult),
                         r=[("ps", pa), gk], w=[ak])
                    t1, t1k = T_()
                    p.op("dve", lambda e, t1=t1, pb=pb, tt=tt, G=G: e.tensor_tensor(out=t1[:], in0=PS[pb][:, :], in1=G(1, tt), op=ALU.mult),
                         r=[("ps", pb), gk], w=[t1k])
                    p.op("pool", lambda e, ac=ac, t1=t1: e.tensor_tensor(out=ac[:], in0=ac[:], in1=t1[:], op=ALU.add), r=[ak, t1k], w=[ak])
                    t2, t2k = T_()
                    p.op("dve", lambda e, t2=t2, pga=pga: e.tensor_tensor(out=t2[:], in0=PS[pga][:, :], in1=sgt[:], op=ALU.mult),
                         r=[("ps", pga), "sgt"], w=[t2k])
                    p.op("pool", lambda e, t2=t2, tt=tt, G=G: e.tensor_tensor(out=t2[:], in0=t2[:], in1=G(2, tt), op=ALU.mult), r=[t2k, gk], w=[t2k])
                    p.op("pool", lambda e, ac=ac, t2=t2: e.tensor_tensor(out=ac[:], in0=ac[:], in1=t2[:], op=ALU.add), r=[ak, t2k], w=[ak])
                    t3, t3k = T_()
                    p.op("dve", lambda e, t3=t3, pd=pd, tt=tt, G=G: e.tensor_tensor(out=t3[:], in0=PS[pd][:, :], in1=G(3, tt), op=ALU.mult),
                         r=[("ps", pd), gk], w=[t3k])
                    p.op("pool", lambda e, ac=ac, t3=t3, ms=ms, tt=tt: e.tensor_tensor(out=ms[:, tt * 512:(tt + 1) * 512], in0=ac[:], in1=t3[:], op=ALU.add),
                         r=[ak, t3k], w=[mk])
                p.dma("sp", MT[fc], ms[:], r=[mk], w=["MT"])
        WStream(p, "wm", stages, wm, cast_engs=("act",)).run(pieces, compute)
    p.barrier()
    if "DBG_MT" in k.debug:
        return
    with ExitStack() as ph:
        wbs = [scoped(ph, "wo%d" % i, [128, 16, 256], BF16) for i in range(2)]
        xs = [scoped(ph, "xs%d" % i, [128, T], F32) for i in range(2)]
        for c in range(16):
            p.dma("sp", A(BUFA, c, 0, T), MT[c], r=["MT"], w=["BUF"])
        stages = [BUFB[:, i * 8192:(i + 1) * 8192].bitcast(F32).rearrange("p (k n) -> p k n", k=16) for i in range(4)]
        pieces = [[(g["w_o"][l][:, fg * 256:(fg + 1) * 256], 0, 16, 0, 256)] for fg in range(8)]

        def compute(fg, wb, wk):
            for f2 in range(2):
                fc = fg * 2 + f2
                x = xs[fc % 2]; xk = ("xs", fc % 2)
                p.dma("sp", x[:], XT[0][fc], r=["XT0"], w=[xk])
                for tt in range(NT):
                    pi = psr()
                    j = mod_j(tt)
                    for kc in range(16):
                        p.op("pe", lambda e, pi=pi, wb=wb, kc=kc, tt=tt, f2=f2: e.matmul(
                            PS[pi][:, :], wb[:, kc, f2 * 128:(f2 + 1) * 128], A(BUFA, kc, tt * 512, 512),
                            start=(kc == 0), stop=(kc == 15)), r=[wk, "BUF"], w=[("ps", pi)])
                    p.op("dve", lambda e, pi=pi, x=x, tt=tt, fc=fc, j=j: e.scalar_tensor_tensor(
                        out=x[:, tt * 512:(tt + 1) * 512], in0=PS[pi][:, :], scalar=modv[:, l, 32 + fc, j:j + 1],
                        in1=x[:, tt * 512:(tt + 1) * 512], op0=ALU.mult, op1=ALU.add), r=[("ps", pi), xk, "modv"], w=[xk])
                p.dma("sp", XT[1][fc], x[:], r=[xk], w=["XT1"])
        WStream(p, "wo", stages, wbs).run(pieces, compute)
    p.barrier()
    if "DBG_X1" in k.debug:
        return
    norm(XT[1], "XT1", A2, l, 48, BUFB)
    with ExitStack() as ph:
        wbs = [scoped(ph, "w1_%d" % i, [128, 16, 256], BF16) for i in range(3)]
        stg = [scoped(ph, "astg%d" % i, [128, T], BF16) for i in range(2)]
        rl = [scoped(ph, "rl%d" % i, [128, 512], F32) for i in range(2)]
        ri = [0]
        stages = [BUFA[:, i * 8192:(i + 1) * 8192].bitcast(F32).rearrange("p (k n) -> p k n", k=16) for i in range(4)]
        pieces = [[(g["w_mlp1"][l][:, hg * 256:(hg + 1) * 256], 0, 16, 0, 256)] for hg in range(32)]

        def compute(hg, wb, wk):
            for f2 in range(2):
                hc = hg * 2 + f2
                sb_ = stg[hc % 2]; sk = ("astg", hc % 2)
                for tt in range(NT):
                    pi = psr()
                    for kc in range(16):
                        p.op("pe", lambda e, pi=pi, wb=wb, kc=kc, tt=tt, f2=f2: e.matmul(
                            PS[pi][:, :], wb[:, kc, f2 * 128:(f2 + 1) * 128], A(BUFB, kc, tt * 512, 512),
                            start=(kc == 0), stop=(kc == 15)), r=[wk, "BUF"], w=[("ps", pi)])
                    r_ = rl[ri[0] % 2]; rk = ("rl", ri[0] % 2); ri[0] += 1
                    p.op("act", lambda e, pi=pi, r_=r_: e.activation(out=r_[:], in_=PS[pi][:, :], func=AF.Relu), r=[("ps", pi)], w=[rk])
                    p.op("dve", lambda e, r_=r_, sb_=sb_, tt=tt: e.tensor_tensor(out=sb_[:, tt * 512:(tt + 1) * 512], in0=r_[:], in1=r_[:], op=ALU.mult),
                         r=[rk], w=[sk])
                p.dma("sp", AT[hc], sb_[:], r=[sk], w=["AT"])
        WStream(p, "w1", stages, wbs, cast_engs=("pool", "act")).run(pieces, compute)
    p.barrier()
    W2B = g["W2B"]
    with ExitStack() as ph:
        x1t = [scoped(ph, "x1t%d" % i, [128, 512], F32) for i in range(3)]
        xi = [0]
        wbf = [BUFB[:, 32768:40960].rearrange("p (k n) -> p k n", k=64), BUFA[:, 32768:40960].rearrange("p (k n) -> p k n", k=64)]
        stg2 = [BUFB[:, i * 16384:(i + 1) * 16384].bitcast(F32).rearrange("p (k n) -> p k n", k=64) for i in range(2)]
        for tt in range(NT):
            j = mod_j(tt)
            p.dma("sp", BUFA[:, 0:64 * 512].rearrange("p (c t) -> p c t", t=512),
                  AT[:, :, tt * 512:(tt + 1) * 512].rearrange("c p t -> p c t"), r=["AT"], w=["atile"])

            def compute(fc, wv, wk, tt=tt, j=j):
                if tt == 0:
                    p.dma("sp", W2B[fc], wv.rearrange("p k n -> p (k n)"), r=[wk], w=[("W2B", fc)])
                xt_ = x1t[xi[0] % 3]; xk = ("x1t", xi[0] % 3); xi[0] += 1
                p.dma("sp", xt_[:], XT[1][fc][:, tt * 512:(tt + 1) * 512], r=["XT1"], w=[xk])
                pi = psr()
                for kc in range(64):
                    p.op("pe", lambda e, pi=pi, wv=wv, kc=kc: e.matmul(
                        PS[pi][:, :], wv[:, kc, :], BUFA[:, kc * 512:(kc + 1) * 512],
                        start=(kc == 0), stop=(kc == 63)), r=[wk, "atile"], w=[("ps", pi)])
                p.op("dve", lambda e, pi=pi, xt_=xt_, fc=fc, j=j: e.scalar_tensor_tensor(
                    out=xt_[:], in0=PS[pi][:, :], scalar=modv[:, l, 80 + fc, j:j + 1], in1=xt_[:],
                    op0=ALU.mult, op1=ALU.add), r=[("ps", pi), xk, "modv"], w=[xk])
                p.dma("sp", XT[0][fc][:, tt * 512:(tt + 1) * 512], xt_[:], r=[xk], w=["XT0"])
            if tt == 0:
                pieces = [[(g["w_mlp2"][l][:, fc * 128:(fc + 1) * 128], 0, 64, 0, 128)] for fc in range(16)]
                WStream(p, "w2", stg2, wbf, cast_engs=("act", "pool")).run(pieces, compute)
            else:
                def ld(fc):
                    p.dma("sp", wbf[fc % 2].rearrange("p k n -> p (k n)"), W2B[fc], r=[("W2B", fc)], w=[("w2wb", fc % 2)])
                ld(0)
                for fc in range(16):
                    if fc + 1 < 16:
                        ld(fc + 1)
                    compute(fc, wbf[fc % 2], ("w2wb", fc % 2))
    p.barrier()


def ssm_gen(k, g, l):
    nc = k.nc
    p = k.p
    PS = g["PS"]
    ident_f = g["ident_f"]
    CAA = g["CAA"]; CAB = g["CAB"]
    TWO_PI = 2.0 * math.pi
    uid = [0]
    with ExitStack() as ph:
        def t_(name, shape, dt=F32):
            uid[0] += 1
            return ph.enter_context(nc.sbuf_tensor("g%d_%d_%s" % (l, uid[0], name), list(shape), dt))
        lr = t_("lr", [128, 32]); li = t_("li", [128, 32]); ls = t_("ls", [128, 32])
        Bre = t_("Bre", [128, 32, 16]); Bim = t_("Bim", [128, 32, 16])
        Cre = t_("Cre", [128, 32, 16]); Cim = t_("Cim", [128, 32, 16])
        mF = t_("mF", [128, 2, 256]); mB = t_("mB", [128, 2, 256]); dI = t_("dI", [128, 2, 256])
        dvec = t_("dvec", [128, DEPTH, 32])
        for d in range(2):
            for gh in range(2):
                rows = slice(gh * 64, (gh + 1) * 64)
                gs = slice(gh * 16, (gh + 1) * 16)
                fs = slice(d * 16, (d + 1) * 16)
                p.dma("sp", lr[rows, fs], g["ssm_lam_re"][l, d, gs, :].rearrange("g n -> n g"), w=["lr"], slow=True)
                p.dma("sp", li[rows, fs], g["ssm_lam_im"][l, d, gs, :].rearrange("g n -> n g"), w=["li"], slow=True)
                p.dma("sp", ls[rows, fs], g["ssm_log_step"][l, d, gs].partition_broadcast(64), w=["ls"])
                p.dma("sp", Bre[rows, fs, :], g["ssm_b_re"][l, d, gs, :, :].rearrange("g n c -> n g c"), w=["Bre"])
                p.dma("sp", Bim[rows, fs, :], g["ssm_b_im"][l, d, gs, :, :].rearrange("g n c -> n g c"), w=["Bim"])
                for g16 in range(16):
                    gg = gh * 16 + g16
                    p.dma("sp", Cre[rows, d * 16 + g16, :], g["ssm_c_re"][l, d, gg, :, :].rearrange("c n -> n c"), w=["Cre"], slow=True)
                    p.dma("sp", Cim[rows, d * 16 + g16, :], g["ssm_c_im"][l, d, gg, :, :].rearrange("c n -> n c"), w=["Cim"], slow=True)
        p.dma("sp", mF[:], g["c_mF"].rearrange("c p f -> p c f"), w=["mF"])
        p.dma("sp", mB[:], g["c_mB"].rearrange("c p f -> p c f"), w=["mB"])
        p.dma("sp", dI[:], g["c_dI"].rearrange("c p f -> p c f"), w=["dI"])
        p.dma("sp", dvec[:], g["c_dvec"][:, :, :], w=["dvec"])

        K_ = ["gen"]

        def V(fn):
            p.op("dve", fn, r=K_ + ["lr", "li", "ls", "Bre", "Bim", "Cre", "Cim"], w=K_)

        def ACT(fn):
            p.op("act", fn, r=K_ + ["lr", "li", "ls", "Bre", "Bim", "Cre", "Cim"], w=K_)

        def tt(out, a, b, op):
            V(lambda e: e.tensor_tensor(out=out, in0=a, in1=b, op=op))

        step = t_("step", [128, 32]); a_ = t_("a", [128, 32]); th = t_("th", [128, 32])
        mag = t_("mag", [128, 32]); imag = t_("imag", [128, 32])
        r_ = t_("r", [128, 32]); ri = t_("ri", [128, 32], mybir.dt.int32); rf = t_("rf", [128, 32])
        f_ = t_("f", [128, 32]); fc = t_("fc", [128, 32]); m_ = t_("m", [128, 32])
        sinv = t_("sinv", [128, 32]); cosv = t_("cosv", [128, 32])
        ACT(lambda e: e.activation(out=step[:], in_=ls[:], func=AF.Exp))
        tt(a_[:], lr[:], step[:], ALU.mult)
        tt(th[:], li[:], step[:], ALU.mult)
        ACT(lambda e: e.activation(out=mag[:], in_=a_[:], func=AF.Exp))
        ACT(lambda e: e.activation(out=imag[:], in_=a_[:], func=AF.Exp, scale=-1.0))
        V(lambda e: e.tensor_scalar(out=r_[:], in0=th[:], scalar1=1.0 / TWO_PI, scalar2=None, op0=ALU.mult))
        V(lambda e: e.tensor_copy(out=ri[:], in_=r_[:]))
        V(lambda e: e.tensor_copy(out=rf[:], in_=ri[:]))
        tt(f_[:], r_[:], rf[:], ALU.subtract)
        V(lambda e: e.tensor_scalar(out=fc[:], in0=f_[:], scalar1=0.25, scalar2=None, op0=ALU.add))
        V(lambda e: e.tensor_scalar(out=m_[:], in0=fc[:], scalar1=0.5, scalar2=None, op0=ALU.is_ge))
        tt(fc[:], fc[:], m_[:], ALU.subtract)
        ACT(lambda e: e.activation(out=sinv[:], in_=f_[:], func=AF.Sin, scale=TWO_PI))
        ACT(lambda e: e.activation(out=cosv[:], in_=fc[:], func=AF.Sin, scale=TWO_PI))
        PPr = t_("PPr", [128, 32, 17]); PPi = t_("PPi", [128, 32, 17])
        PNr = t_("PNr", [128, 32, 17]); PNi = t_("PNi", [128, 32, 17])
        tA = t_("tA", [128, 16 * 256]); tB = t_("tB", [128, 16 * 256])

        def cmul(outr, outi, xr, xi, yr, yi, shape):
            n = 1
            for s_ in shape[1:]:
                n *= s_
            pat = {2: None, 3: "p (a b) -> p a b", 4: "p (a b c) -> p a b c"}[len(shape)]

            def view(t):
                v = t[:, 0:n]
                if len(shape) == 3:
                    return v.rearrange(pat, a=shape[1])
                if len(shape) == 4:
                    return v.rearrange(pat, a=shape[1], b=shape[2])
                return v
            ta = view(tA); tb = view(tB)
            tt(ta, xr, yr, ALU.mult)
            tt(tb, xi, yi, ALU.mult)
            tt(outr, ta, tb, ALU.subtract)
            tt(ta, xr, yi, ALU.mult)
            tt(tb, xi, yr, ALU.mult)
            tt(outi, ta, tb, ALU.add)

        for (Pr, Pi, br, bi_, sgn) in ((PPr, PPi, mag, mag, 1.0), (PNr, PNi, imag, imag, -1.0)):
            V(lambda e, Pr=Pr: e.memset(Pr[:, :, 0:1], 1.0))
            V(lambda e, Pi=Pi: e.memset(Pi[:, :, 0:1], 0.0))
            tt(Pr[:, :, 1], br[:], cosv[:], ALU.mult)
            tt(Pi[:, :, 1], bi_[:], sinv[:], ALU.mult)
            if sgn < 0:
                V(lambda e, Pi=Pi: e.tensor_scalar(out=Pi[:, :, 1], in0=Pi[:, :, 1], scalar1=-1.0, scalar2=None, op0=ALU.mult))
            for (o0, n_, s0, k0) in ((2, 1, 1, 1), (3, 2, 1, 2), (5, 4, 1, 4), (9, 8, 1, 8)):
                cmul(Pr[:, :, o0:o0 + n_], Pi[:, :, o0:o0 + n_], Pr[:, :, s0:s0 + n_], Pi[:, :, s0:s0 + n_],
                     Pr[:, :, k0:k0 + 1].to_broadcast([128, 32, n_]), Pi[:, :, k0:k0 + 1].to_broadcast([128, 32, n_]), [128, 32, n_])
        V(lambda e: e.tensor_copy(out=CAA[:, l, 0, :], in_=PPr[:, :, 16]))
        V(lambda e: e.tensor_copy(out=CAA[:, l, 1, :], in_=PPr[:, :, 16]))
        V(lambda e: e.tensor_copy(out=CAB[:, l, 1, :], in_=PPi[:, :, 16]))
        V(lambda e: e.tensor_scalar(out=CAB[:, l, 0, :], in0=PPi[:, :, 16], scalar1=-1.0, scalar2=None, op0=ALU.mult))
        nre = t_("nre", [128, 32]); den = t_("den", [128, 32]); cre = t_("cre", [128, 32]); cim = t_("cim", [128, 32])
        t1 = t_("t1", [128, 32]); t2 = t_("t2", [128, 32])
        V(lambda e: e.tensor_scalar(out=nre[:], in0=PPr[:, :, 1], scalar1=-1.0, scalar2=None, op0=ALU.add))
        tt(t1[:], lr[:], lr[:], ALU.mult)
        tt(t2[:], li[:], li[:], ALU.mult)
        tt(den[:], t1[:], t2[:], ALU.add)
        V(lambda e: e.reciprocal(out=den[:], in_=den[:]))
        tt(t1[:], nre[:], lr[:], ALU.mult)
        tt(t2[:], PPi[:, :, 1], li[:], ALU.mult)
        tt(cre[:], t1[:], t2[:], ALU.add)
        tt(cre[:], cre[:], den[:], ALU.mult)
        tt(t1[:], PPi[:, :, 1], lr[:], ALU.mult)
        tt(t2[:], nre[:], li[:], ALU.mult)
        tt(cim[:], t1[:], t2[:], ALU.subtract)
        tt(cim[:], cim[:], den[:], ALU.mult)
        BBr = t_("BBr", [128, 32, 16]); BBi = t_("BBi", [128, 32, 16])
        cmul(BBr[:], BBi[:], cre[:].unsqueeze(2).to_broadcast([128, 32, 16]), cim[:].unsqueeze(2).to_broadcast([128, 32, 16]),
             Bre[:], Bim[:], [128, 32, 16])
        PCr = t_("PCr", [128, 32, 16]); PCi = t_("PCi", [128, 32, 16])
        PQr = t_("PQr", [128, 32, 16]); PQi = t_("PQi", [128, 32, 16])
        V(lambda e: e.tensor_copy(out=PCr[:, 0:16, :], in_=PPr[:, 0:16, 1:17]))
        V(lambda e: e.tensor_copy(out=PCi[:, 0:16, :], in_=PPi[:, 0:16, 1:17]))
        V(lambda e: e.tensor_copy(out=PQr[:, 0:16, :], in_=PNr[:, 0:16, 1:17]))
        V(lambda e: e.tensor_copy(out=PQi[:, 0:16, :], in_=PNi[:, 0:16, 1:17]))
        cmul(PCr[:, 16:32, :], PCi[:, 16:32, :], PNr[:, 16:32, 0:16], PNi[:, 16:32, 0:16],
             PPr[:, 16:32, 16:17].to_broadcast([128, 16, 16]), PPi[:, 16:32, 16:17].to_broadcast([128, 16, 16]), [128, 16, 16])
        cmul(PQr[:, 16:32, :], PQi[:, 16:32, :], PPr[:, 16:32, 0:16], PPi[:, 16:32, 0:16],
             PNr[:, 16:32, 16:17].to_broadcast([128, 16, 16]), PNi[:, 16:32, 16:17].to_broadcast([128, 16, 16]), [128, 16, 16])
        if "GEN" in k.debug and l == 0:
            for nm, t, shp in (("lr", lr, [128, 32]), ("li", li, [128, 32]), ("ls", ls, [128, 32]), ("step", step, [128, 32]),
                               ("th", th, [128, 32]), ("f", f_, [128, 32]), ("fc", fc, [128, 32]),
                               ("sinv", sinv, [128, 32]), ("cosv", cosv, [128, 32]), ("mag", mag, [128, 32]),
                               ("PPr", PPr, [128, 32, 17]), ("PPi", PPi, [128, 32, 17]), ("PNr", PNr, [128, 32, 17]), ("PNi", PNi, [128, 32, 17]),
                               ("BBr", BBr, [128, 32, 16]), ("BBi", BBi, [128, 32, 16]), ("Cre", Cre, [128, 32, 16]), ("Bre", Bre, [128, 32, 16]),
                               ("PCr", PCr, [128, 32, 16]), ("PCi", PCi, [128, 32, 16]), ("PQr", PQr, [128, 32, 16]), ("PQi", PQi, [128, 32, 16])):
                dt_ = nc.dram_tensor("G_" + nm, shp, F32, kind="ExternalOutput").ap()
                p.dma("sp", dt_, t[:], r=K_ + ["lr", "li", "ls", "Bre", "Bim", "Cre", "Cim"], w=["GDBG"])
        Xr = t_("Xr", [128, 16, 256]); Xi = t_("Xi", [128, 16, 256])
        Qr = t_("Qr", [128, 16, 256]); Qi = t_("Qi", [128, 16, 256])
        BLr = t_("BLr", [128, 16, 256]); BLi = t_("BLi", [128, 16, 256])
        CSb = t_("CSb", [128, 16, 2, 256], BF16)
        mlt = [t_("mlt%d" % i, [128, 256], BF16) for i in range(2)]
        mtmp = t_("mtmp", [128, 256])
        blt = [t_("blt%d" % i, [128, 128], BF16) for i in range(2)]
        cnt = [0]
        sh4 = [128, 16, 16, 16]

        def v4(t):
            return t[:].rearrange("p g (i c) -> p g i c", i=16)
        for d in range(2):
            fs = slice(d * 16, (d + 1) * 16)
            cmul(v4(Xr), v4(Xi), PCr[:, fs, :].unsqueeze(3).to_broadcast(sh4), PCi[:, fs, :].unsqueeze(3).to_broadcast(sh4),
                 Cre[:, fs, :].unsqueeze(2).to_broadcast(sh4), Cim[:, fs, :].unsqueeze(2).to_broadcast(sh4), sh4)
            V(lambda e: e.tensor_scalar(out=Xi[:], in0=Xi[:], scalar1=-1.0, scalar2=None, op0=ALU.mult))
            cmul(v4(Qr), v4(Qi), PQr[:, fs, :].unsqueeze(3).to_broadcast(sh4), PQi[:, fs, :].unsqueeze(3).to_broadcast(sh4),
                 BBr[:, fs, :].unsqueeze(2).to_broadcast(sh4), BBi[:, fs, :].unsqueeze(2).to_broadcast(sh4), sh4)
            cmul(BLr[:], BLi[:], Qr[:], Qi[:], PPr[:, fs, 16:17].to_broadcast([128, 16, 256]), PPi[:, fs, 16:17].to_broadcast([128, 16, 256]),
                 [128, 16, 256])
            ACT(lambda e: e.copy(out=CSb[:, :, 0, :], in_=Xr[:]))
            ACT(lambda e: e.copy(out=CSb[:, :, 1, :], in_=Xi[:]))
            for gh in range(2):
                p.dma("sp", g["SSM_CS"][l, d, gh * 16:(gh + 1) * 16].rearrange("g n r f -> n g (r f)"),
                      CSb[gh * 64:(gh + 1) * 64].rearrange("p g r f -> p g (r f)"), r=K_, w=["SSM_CS"])
            for gg in range(32):
                gh, g16 = gg // 16, gg % 16
                r0 = gh * 64
                for ch in range(2):
                    cs = slice(ch * 128, (ch + 1) * 128)
                    i_ = cnt[0] % 2
                    cnt[0] += 1
                    pi = 4 + (cnt[0] % 2) * 2
                    p.op("pe", lambda e, pi=pi, g16=g16, cs=cs, r0=r0: e.matmul(
                        PS[pi][:, 0:256], Qr[r0:r0 + 64, g16, cs], Xr[r0:r0 + 64, g16, :], start=True, stop=False), r=K_, w=[("ps", pi)])
                    p.op("pe", lambda e, pi=pi, g16=g16, cs=cs, r0=r0: e.matmul(
                        PS[pi][:, 0:256], Qi[r0:r0 + 64, g16, cs], Xi[r0:r0 + 64, g16, :], start=False, stop=True), r=K_, w=[("ps", pi)])
                    msk = (mF if d == 0 else mB)
                    ml = mlt[i_]; mk = ("mlt", i_)
                    if d == 0:
                        p.op("dve", lambda e, pi=pi, ch=ch, msk=msk: e.tensor_tensor(out=mtmp[:], in0=PS[pi][:, 0:256], in1=msk[:, ch, :], op=ALU.mult),
                             r=[("ps", pi), "mF", "mB"], w=["mtmp"])
                        p.op("dve", lambda e, ch=ch, gg=gg, ml=ml: e.scalar_tensor_tensor(
                            out=ml[:], in0=dI[:, ch, :], scalar=dvec[:, l, gg:gg + 1], in1=mtmp[:], op0=ALU.mult, op1=ALU.add),
                            r=["mtmp", "dI", "dvec"], w=[mk])
                    else:
                        p.op("dve", lambda e, pi=pi, ch=ch, msk=msk, ml=ml: e.tensor_tensor(out=ml[:], in0=PS[pi][:, 0:256], in1=msk[:, ch, :], op=ALU.mult),
                             r=[("ps", pi), "mF", "mB"], w=[mk])
                    p.dma("sp", g["SSM_ML"][l, d, gg, ch], ml[:], r=[mk], w=["SSM_ML"])
                    pj = pi + 1
                    p.op("pe", lambda e, pj=pj, g16=g16, cs=cs, r0=r0: e.transpose(
                        PS[pj][:, 0:64], BLr[r0:r0 + 64, g16, cs], ident_f[r0:r0 + 64, r0:r0 + 64]), r=K_ + ["ident_f"], w=[("ps", pj)])
                    p.op("pe", lambda e, pj=pj, g16=g16, cs=cs, r0=r0: e.transpose(
                        PS[pj][:, 64:128], BLi[r0:r0 + 64, g16, cs], ident_f[r0:r0 + 64, r0:r0 + 64]), r=K_ + ["ident_f"], w=[("ps", pj)])
                    bl = blt[i_]; bk = ("blt", i_)
                    p.op("act", lambda e, pj=pj, bl=bl: e.copy(out=bl[:], in_=PS[pj][:, 0:128]), r=[("ps", pj)], w=[bk])
                    p.dma("sp", g["SSM_BLT"][l, d, gg, ch], bl[:], r=[bk], w=["SSM_BLT"])
    p.barrier()


class WStream:
    def __init__(self, p, name, stages, wbs, cast_engs=("act", "pool")):
        self.p = p
        self.name = name
        self.stages = stages
        self.wbs = wbs
        self.cast_engs = cast_engs
        self.ci = 0

    def run(self, groups, compute):
        p = self.p
        ns, nw = len(self.stages), len(self.wbs)
        n = len(groups)

        def dma(i):
            st = self.stages[i % ns]
            sk = (self.name + "st", i % ns)
            for (src, k0, K, c0, nn) in groups[i]:
                p.dma("sp", st[:, k0:k0 + K, c0:c0 + nn], src.rearrange("(kc p) n -> p kc n", p=128), w=[sk])

        def cast(i):
            st = self.stages[i % ns]
            sk = (self.name + "st", i % ns)
            wb = self.wbs[i % nw]
            wk = (self.name + "wb", i % nw)
            for (src, k0, K, c0, nn) in groups[i]:
                eng = self.cast_engs[self.ci % len(self.cast_engs)]
                self.ci += 1
                if eng == "act":
                    p.op("act", lambda e, st=st, wb=wb, k0=k0, K=K, c0=c0, nn=nn: e.copy(
                        out=wb[:, k0:k0 + K, c0:c0 + nn], in_=st[:, k0:k0 + K, c0:c0 + nn]), r=[sk], w=[wk])
                else:
                    p.op(eng, lambda e, st=st, wb=wb, k0=k0, K=K, c0=c0, nn=nn: e.tensor_copy(
                        out=wb[:, k0:k0 + K, c0:c0 + nn], in_=st[:, k0:k0 + K, c0:c0 + nn]), r=[sk], w=[wk])

        for i in range(min(ns - 1, n)):
            dma(i)
        if n:
            cast(0)
        for i in range(n):
            if i + ns - 1 < n:
                dma(i + ns - 1)
            if i + 1 < n:
                cast(i + 1)
            compute(i, self.wbs[i % nw], (self.name + "wb", i % nw))
```

```python
import math
from contextlib import ExitStack
import numpy as np
import ml_dtypes
import concourse.bass as bass
import concourse.mybir as mybir
from concourse.bass_utils import run_bass_kernel_spmd

F32 = mybir.dt.float32
BF16 = mybir.dt.bfloat16
AF = mybir.ActivationFunctionType
ALU = mybir.AluOpType

D = 2048
T = 2560
NT = 5
TP = 512
TS = 2048
PAST = 512
TK = T + PAST
L = 16
NQ = T // L
EPS = 1e-6
HID = 8192
DEPTH = 2
SEGS = [(0, 256), (256, 256), (512, 2048)]
IN_OFF = dict(q=0, kv=512, kr=768, cu=800, cb=1312, cc=1824, su=2336, pu=2848, g=3360)
IN_COLS = 11552


class Prog:
    ENGS = ("pe", "act", "dve", "pool", "sp")
    KQ = 8

    def __init__(self, nc):
        self.nc = nc
        self.ops = []
        self.lw = {}
        self.rd = {}
        self.last_on = {e: None for e in self.ENGS}
        self.pend = {e: set() for e in self.ENGS}
        self.dma_since = []

    def _add(self, eng, fn, r, w, is_dma):
        idx = len(self.ops)
        deps = set(self.pend[eng])
        self.pend[eng] = set()
        for k in r:
            x = self.lw.get(k)
            if x is not None:
                deps.add(x)
        for k in w:
            x = self.lw.get(k)
            if x is not None:
                deps.add(x)
            for y in self.rd.get(k, ()):
                deps.add(y)
        for k in r:
            self.rd.setdefault(k, []).append(idx)
        for k in w:
            self.lw[k] = idx
            self.rd[k] = []
        deps.discard(idx)
        self.ops.append((eng, fn, deps, is_dma))
        self.last_on[eng] = idx
        if is_dma:
            self.dma_since.append(idx)
        return idx

    def op(self, eng, fn, r=(), w=()):
        return self._add(eng, fn, r, w, False)

    def dma(self, q, out, in_, r=(), w=(), slow=False):
        if slow:
            return self._add(q, lambda e: e.dma_start(out=out, in_=in_, allow_slow_non_contiguous=True), r, w, True)
        return self._add(q, lambda e: e.dma_start(out=out, in_=in_), r, w, True)

    def barrier(self):
        bar = set(self.dma_since)
        for e in self.ENGS:
            if self.last_on[e] is not None:
                bar.add(self.last_on[e])
        for e in self.ENGS:
            self.pend[e] |= bar
        self.dma_since = []

    def emit(self, es):
        nc = self.nc
        ops = self.ops
        needed = [False] * len(ops)
        for (eng_, _, deps, _) in ops:
            for d in deps:
                if eng_ == "pe" and ops[d][0] == "pe" and not ops[d][3]:
                    continue
                needed[d] = True
        esem = {e: es.enter_context(nc.semaphore("s_" + e)) for e in self.ENGS}
        qsem = {e: [es.enter_context(nc.semaphore("q_%s%d" % (e, i))) for i in range(self.KQ)]
                for e in ("sp", "pool", "act")}
        ev = [None] * len(ops)
        cnt = {e: 0 for e in self.ENGS}
        qcnt = {e: 0 for e in qsem}
        pre = [None] * len(ops)
        for i, (eng, fn, deps, is_dma) in enumerate(ops):
            if is_dma:
                n = qcnt[eng]
                qcnt[eng] += 1
                s = qsem[eng][n % self.KQ]
                ev[i] = (s, 16 * (n // self.KQ + 1))
                if n >= self.KQ:
                    pre[i] = (s, 16 * (n // self.KQ))
            elif needed[i]:
                cnt[eng] += 1
                ev[i] = (esem[eng], cnt[eng])
        streams = {e: [] for e in self.ENGS}
        for i, o in enumerate(ops):
            streams[o[0]].append(i)
        block = es.enter_context(nc.Block())

        def run(eng_name, e):
            seen = {}
            for i in streams[eng_name]:
                _, fn, deps, is_dma = ops[i]
                waits = {}
                if pre[i] is not None:
                    waits[pre[i][0]] = pre[i][1]
                for d in deps:
                    if eng_name == "pe" and ops[d][0] == "pe" and not ops[d][3]:
                        continue
                    s, v = ev[d]
                    if waits.get(s, 0) < v:
                        waits[s] = v
                for s, v in waits.items():
                    if seen.get(s, 0) < v:
                        e.wait_ge(s, v)
                        seen[s] = v
                ins = fn(e)
                if is_dma:
                    ins.then_inc(ev[i][0], 16)
                elif ev[i] is not None:
                    ins.then_inc(ev[i][0], 1)
            if eng_name in qsem:
                n = qcnt[eng_name]
                for k in range(min(n, self.KQ)):
                    tot = (n - k + self.KQ - 1) // self.KQ
                    e.wait_ge(qsem[eng_name][k], 16 * tot)

        @block.tensor
        def _(e):
            run("pe", e)

        @block.scalar
        def _(e):
            run("act", e)

        @block.vector
        def _(e):
            run("dve", e)

        @block.gpsimd
        def _(e):
            run("pool", e)

        @block.sync
        def _(e):
            run("sp", e)


def _feat_layout(v, nchunk):
    return np.ascontiguousarray(v.reshape(nchunk, 128).T)


def host_consts():
    c = {}
    c["ident_f"] = np.eye(128, dtype=np.float32)
    half = 16
    inv_freq = (10000.0 ** (-np.arange(0, half, 2, dtype=np.float32) / half)).astype(np.float32)
    rows = TS // 64
    row_pos = np.repeat(np.arange(rows, dtype=np.float32), 64)
    col_pos = np.tile(np.arange(64, dtype=np.float32), rows)
    ang = np.zeros((32, TS), np.float32)
    for r in range(32):
        pos = row_pos if r < 16 else col_pos
        ang[r] = pos * inv_freq[r % 8]
    c["rope_cos"] = np.cos(ang).astype(np.float32)
    c["rope_sin"] = np.sin(ang).astype(np.float32)
    prot = np.zeros((128, 96), np.float32)
    for mm in range(32):
        m = 64 + mm
        if mm % 16 < 8:
            prot[m + 8, m] = -1.0
        else:
            prot[m - 8, m] = 1.0
    c["prot"] = prot
    inv = np.zeros((4, 128, T), np.float32)
    for gi, w in enumerate((2, 4, 8, 16)):
        for (s0, n) in SEGS:
            t = np.arange(n)
            lo = np.clip(t - w // 2, 0, n)
            hi = np.clip(t + w // 2, 0, n)
            inv[gi, :, s0:s0 + n] = (1.0 / (hi - lo).astype(np.float32))[None, :]
    c["pool_inv"] = inv
    mF = np.zeros((2, 128, 256), np.float32)
    mB = np.zeros((2, 128, 256), np.float32)
    dI = np.zeros((2, 128, 256), np.float32)
    for ch in range(2):
        for pp in range(128):
            j = ch * 8 + pp // 16
            cc = pp % 16
            for i in range(16):
                if i >= j:
                    mF[ch, pp, i * 16:(i + 1) * 16] = 1.0
                if j >= i:
                    mB[ch, pp, i * 16:(i + 1) * 16] = 1.0
            dI[ch, pp, j * 16 + cc] = 1.0
    c["ssm_mF"] = mF
    c["ssm_mB"] = mB
    c["ssm_dI"] = dI
    return c


class K:
    pass


def build(debug=(), stop_after=None):
    nc = bass.Bass("TRN2", target_bir_lowering=False)
    k = K()
    k.nc = nc
    k.stop_after = stop_after
    k.debug = debug
    p = Prog(nc)
    k.p = p

    def din(name, shape, dt=F32):
        return nc.dram_tensor(name, list(shape), dt, kind="ExternalInput").ap()

    def dout(name, shape, dt=F32):
        return nc.dram_tensor(name, list(shape), dt, kind="ExternalOutput").ap()

    def dscr(name, shape, dt):
        kind = "ExternalOutput" if name in debug else "Internal"
        return nc.dram_tensor(name, list(shape), dt, kind=kind).ap()

    xin = din("xin", [T, D])
    cvecT = din("cvecT", [128, 16, 2])
    cache_ckv = din("cache_ckv", [DEPTH, PAST, 256])
    cache_kr = din("cache_kr", [DEPTH, PAST, 32])
    state_in = din("state_in", [DEPTH, 2, 2, 32, 64])
    w_ada = din("w_ada", [DEPTH, D, 6 * D])
    b_adaT = din("b_adaT", [DEPTH, 128, 96])
    n1gT = din("n1gT", [DEPTH, 128, 16])
    n2gT = din("n2gT", [DEPTH, 128, 16])
    w_in = din("w_in", [DEPTH, D, IN_COLS])
    qagT = din("qagT", [DEPTH, 128, 4])
    kvgT = din("kvgT", [DEPTH, 128, 2])
    w_uq = din("w_uq", [DEPTH, 512, 768])
    w_ukv = din("w_ukv", [DEPTH, 256, 1024])
    qng = din("qng", [DEPTH, 96, 1])
    kng = din("kng", [DEPTH, 96, 1])
    w_mla_o = din("w_mla_o", [DEPTH, 512, D])
    conv_wT = din("conv_wT", [DEPTH, 128, 4, 3])
    conv_bT = din("conv_bT", [DEPTH, 128, 4])
    w_conv_o = din("w_conv_o", [DEPTH, 512, D])
    ssm_lam_re = din("ssm_lam_re", [DEPTH, 2, 32, 64])
    ssm_lam_im = din("ssm_lam_im", [DEPTH, 2, 32, 64])
    ssm_log_step = din("ssm_log_step", [DEPTH, 2, 32])
    ssm_b_re = din("ssm_b_re", [DEPTH, 2, 32, 64, 16])
    ssm_b_im = din("ssm_b_im", [DEPTH, 2, 32, 64, 16])
    ssm_c_re = din("ssm_c_re", [DEPTH, 2, 32, 16, 64])
    ssm_c_im = din("ssm_c_im", [DEPTH, 2, 32, 16, 64])
    ssm_d = din("ssm_d", [DEPTH, 512])
    w_glu = din("w_glu", [DEPTH, 512, 2 * D])
    pool_w = din("pool_w", [DEPTH, 4, 128, 128])
    pool_sT = din("pool_sT", [DEPTH, 128, 4])
    w_pool_o = din("w_pool_o", [DEPTH, 512, D])
    w_o = din("w_o", [DEPTH, D, D])
    w_mlp1 = din("w_mlp1", [DEPTH, D, HID])
    w_mlp2 = din("w_mlp2", [DEPTH, HID, D])
    c_ident = din("ident_f", [128, 128])
    c_cos = din("rope_cos", [32, TS])
    c_sin = din("rope_sin", [32, TS])
    c_prot = din("prot", [128, 96])
    c_pinv = din("pool_inv", [4, 128, T])
    c_mF = din("ssm_mF", [2, 128, 256])
    c_mB = din("ssm_mB", [2, 128, 256])
    c_dI = din("ssm_dI", [2, 128, 256])
    c_dvec = din("ssm_dvec", [128, DEPTH, 32])
    yout = dout("yout", [T, D])
    o_ckv = dout("o_ckv", [2, DEPTH, 256, 256])
    o_kr = dout("o_kr", [2, DEPTH, 256, 32])
    o_ssm = dout("o_ssm", [2, DEPTH, 2, 2, 32, 64])
    XT = [dscr("XT%d" % i, [16, 128, T], F32) for i in range(2)]
    PROJ = dscr("PROJ", [20, 128, T], BF16)
    KV32 = dscr("KV32", [3, 128, T], F32)
    UTOK = dscr("UTOK", [T, 512], BF16)
    YTOK = dscr("YTOK", [T, 512], BF16)
    GS = dscr("GS", [64, 128, T], BF16)
    MT = dscr("MT", [16, 128, T], BF16)
    AT = dscr("AT", [64, 128, T], BF16)
    W2B = dscr("W2B", [16, 128, 64 * 128], BF16)
    SSM_BLT = dscr("SSM_BLT", [DEPTH, 2, 32, 2, 128, 128], BF16)
    SSM_ML = dscr("SSM_ML", [DEPTH, 2, 32, 2, 128, 256], BF16)
    SSM_CS = dscr("SSM_CS", [DEPTH, 2, 32, 64, 2, 256], BF16)
    if "DBG_BI" in debug:
        k.DBG_BI = dscr("DBG_BI", [16, 128, T], BF16)

    es = ExitStack()
    k.es = es

    def sb(name, shape, dt):
        return es.enter_context(nc.sbuf_tensor("sb_" + name, list(shape), dt))

    PSD = [es.enter_context(nc.psum_tensor("psd%d" % i, [128, 1024], F32)) for i in range(4)]
    PS = [PSD[i // 2][:, (i % 2) * 512:(i % 2 + 1) * 512] for i in range(8)]
    psc = [0]

    def nextps():
        i = psc[0] % 8
        psc[0] += 1
        return i

    ident_f = sb("ident_f", [128, 128], F32)
    ident_b = sb("ident_b", [128, 128], BF16)
    ones_b = sb("ones_b", [128, 128], BF16)
    ones_f = sb("ones_f", [128, 128], F32)
    epsc = sb("epsc", [128, 1], F32)
    modv = sb("modv", [128, DEPTH, 96, 2], F32)
    A1 = sb("A1", [128, DEPTH, 16, 2], F32)
    A2 = sb("A2", [128, DEPTH, 16, 2], F32)
    n1g = sb("n1g", [128, DEPTH, 16], F32)
    n2g = sb("n2g", [128, DEPTH, 16], F32)
    CAA = sb("CAA", [128, DEPTH, 2, 32], F32)
    CAB = sb("CAB", [128, DEPTH, 2, 32], F32)

    p.dma("sp", ident_f[:], c_ident[:, :], w=["ident_f"])
    p.op("dve", lambda e: e.tensor_copy(out=ident_b[:], in_=ident_f[:]), r=["ident_f"], w=["ident_b"])
    p.op("pool", lambda e: e.memset(ones_b[:], 1.0), w=["ones_b"])
    p.op("pool", lambda e: e.memset(ones_f[:], 1.0), w=["ones_f"])
    p.op("pool", lambda e: e.memset(epsc[:], EPS), w=["epsc"])
    for l in range(DEPTH):
        p.dma("sp", n1g[:, l, :], n1gT[l], w=["n1g"])
        p.dma("sp", n2g[:, l, :], n2gT[l], w=["n2g"])

    def A(buf, c, t0, n):
        return buf[:, c * T + t0: c * T + t0 + n]

    with ExitStack() as ph:
        sT = ph.enter_context(nc.sbuf_tensor("sT", [128, 16, 2], F32))
        sg = ph.enter_context(nc.sbuf_tensor("sgT", [128, 16, 2], F32))
        wab = [ph.enter_context(nc.sbuf_tensor("wab%d" % i, [128, 16, 512], F32)) for i in range(3)]
        wabb = [ph.enter_context(nc.sbuf_tensor("wabb%d" % i, [128, 16, 512], BF16)) for i in range(2)]
        sTb = ph.enter_context(nc.sbuf_tensor("sTb", [128, 16, 2], BF16))
        bad = ph.enter_context(nc.sbuf_tensor("bad", [128, DEPTH, 96], F32))
        p.dma("sp", sT[:], cvecT[:, :, :], w=["sT"])
        p.op("act", lambda e: e.activation(out=sg[:], in_=sT[:], func=AF.Sigmoid), r=["sT"], w=["sg"])
        p.op("dve", lambda e: e.tensor_tensor(out=sTb[:], in0=sT[:], in1=sg[:], op=ALU.mult), r=["sT", "sg"], w=["sTb"])
        for l in range(DEPTH):
            p.dma("sp", bad[:, l, :], b_adaT[l], w=["bad"])
        gl = [(l, g) for l in range(DEPTH) for g in range(24)]

        def ada_dma(i):
            l, g = gl[i]
            src = w_ada[l, :, g * 512:(g + 1) * 512].rearrange("(kc p) n -> p kc n", p=128)
            p.dma("sp", wab[i % 3][:], src, w=[("wab", i % 3)])

        def ada_cast(i):
            wf = wab[i % 3]; wb = wabb[i % 2]
            p.op("act", lambda e, wf=wf, wb=wb: e.copy(out=wb[:, 0:6, :], in_=wf[:, 0:6, :]), r=[("wab", i % 3)], w=[("wabb", i % 2)])
            p.op("pool", lambda e, wf=wf, wb=wb: e.tensor_copy(out=wb[:, 6:10, :], in_=wf[:, 6:10, :]), r=[("wab", i % 3)], w=[("wabb", i % 2)])
            p.op("dve", lambda e, wf=wf, wb=wb: e.tensor_copy(out=wb[:, 10:16, :], in_=wf[:, 10:16, :]), r=[("wab", i % 3)], w=[("wabb", i % 2)])
        ada_dma(0); ada_dma(1); ada_cast(0)
        for i, (l, g) in enumerate(gl):
            if i + 2 < len(gl):
                ada_dma(i + 2)
            if i + 1 < len(gl):
                ada_cast(i + 1)
            wb = wabb[i % 2]; wk = ("wabb", i % 2)
            pi = 7
            for fc in range(4):
                col = (g * 4 + fc) * 2
                for kc in range(16):
                    p.op("pe", lambda e, wb=wb, kc=kc, fc=fc, col=col: e.matmul(
                        PS[pi][:, col:col + 2], wb[:, kc, fc * 128:(fc + 1) * 128], sTb[:, kc, :],
                        start=(kc == 0), stop=(kc == 15)), r=[wk, "sTb"], w=[("ps", pi)])
            if g != 23:
                continue
            p.op("dve", lambda e, l=l: e.tensor_tensor(
                out=modv[:, l, :, :], in0=PS[7][:, 0:192].rearrange("p (c j) -> p c j", j=2),
                in1=bad[:, l, :].unsqueeze(2).to_broadcast([128, 96, 2]), op=ALU.add),
                r=[("ps", 7), "bad"], w=["modv"])
            for (Ax, ng, c0) in ((A1, n1g, 16), (A2, n2g, 64)):
                p.op("dve", lambda e, Ax=Ax, ng=ng, c0=c0, l=l: e.scalar_tensor_tensor(
                    out=Ax[:, l, :, :], in0=modv[:, l, c0:c0 + 16, :], scalar=1.0,
                    in1=ng[:, l, :].unsqueeze(2).to_broadcast([128, 16, 2]), op0=ALU.add, op1=ALU.mult),
                    r=["modv", "n1g", "n2g"], w=["A1A2"])
    p.barrier()
    for l in range(DEPTH):
        ssm_gen(k, locals(), l)
    BUFA = sb("BUFA", [128, 16 * T], BF16)
    BUFB = sb("BUFB", [128, 16 * T], BF16)
    build_layers(k, locals())
    return k


def build_layers(k, g):
    nc = k.nc
    p = k.p
    PS = g["PS"]; nextps = g["nextps"]; A = g["A"]
    BUFA = g["BUFA"]; BUFB = g["BUFB"]
    ident_f = g["ident_f"]; ident_b = g["ident_b"]; ones_b = g["ones_b"]
    modv = g["modv"]; A1 = g["A1"]; A2 = g["A2"]
    XT = g["XT"]; PROJ = g["PROJ"]; KV32 = g["KV32"]; UTOK = g["UTOK"]; YTOK = g["YTOK"]
    GS = g["GS"]; MT = g["MT"]; AT = g["AT"]
    uid = [0]

    def scoped(ph, name, shape, dt):
        uid[0] += 1
        return ph.enter_context(nc.sbuf_tensor("t%d_%s" % (uid[0], name), list(shape), dt))

    def mod_j(tt):
        return 0 if tt == 0 else 1

    with ExitStack() as ph:
        xt = [scoped(ph, "xt%d" % i, [128, D], F32) for i in range(2)]
        st = [scoped(ph, "st%d" % i, [128, 4, 128], F32) for i in range(3)]
        si = 0
        for ti in range(T // 128):
            xb = xt[ti % 2]; xk = ("xt", ti % 2)
            p.dma("sp", xb[:], g["xin"][ti * 128:(ti + 1) * 128, :], w=[xk])
            for fg in range(4):
                pi = nextps()
                for f4 in range(4):
                    fc = fg * 4 + f4
                    p.op("pe", lambda e, pi=pi, f4=f4, fc=fc, xb=xb: e.transpose(
                        PS[pi][:, f4 * 128:(f4 + 1) * 128], xb[:, fc * 128:(fc + 1) * 128], ident_f[:]),
                        r=[xk, "ident_f"], w=[("ps", pi)])
                sb_ = st[si % 3]; sk = ("st", si % 3); si += 1
                eng = "dve" if fg % 2 == 0 else "act"
                if eng == "dve":
                    p.op("dve", lambda e, pi=pi, sb_=sb_: e.tensor_copy(
                        out=sb_[:], in_=PS[pi][:, :].rearrange("p (a b) -> p a b", a=4)), r=[("ps", pi)], w=[sk])
                else:
                    p.op("act", lambda e, pi=pi, sb_=sb_: e.copy(
                        out=sb_[:], in_=PS[pi][:, :].rearrange("p (a b) -> p a b", a=4)), r=[("ps", pi)], w=[sk])
                dst = XT[0][fg * 4:(fg + 1) * 4, :, ti * 128:(ti + 1) * 128].rearrange("c p t -> p c t")
                p.dma("sp", dst, sb_[:], r=[sk], w=["XT0"])
    p.barrier()

    def norm(XTd, xkey, Amod, l, shift_c0, dst):
        with ExitStack() as ph:
            xc = [scoped(ph, "xc%d" % i, [128, T], F32) for i in range(2)]
            sq = [scoped(ph, "sq%d" % i, [128, T], BF16) for i in range(2)]
            RB = scoped(ph, "RB", [128, T], F32)
            pss = [nextps() for _ in range(NT)]
            for fc in range(16):
                xb = xc[fc % 2]; xk = ("xc", fc % 2)
                p.dma("sp", xb[:], XTd[fc], r=[xkey], w=[xk])
                sb_ = sq[fc % 2]; sk = ("sq", fc % 2)
                p.op("act", lambda e, xb=xb, sb_=sb_: e.activation(out=sb_[:], in_=xb[:], func=AF.Square),
                     r=[xk], w=[sk])
                for tt in range(NT):
                    p.op("pe", lambda e, tt=tt, sb_=sb_, fc=fc: e.matmul(
                        PS[pss[tt]][:, :], ones_b[:, :], sb_[:, tt * 512:(tt + 1) * 512],
                        start=(fc == 0), stop=(fc == 15)), r=[sk, "ones_b"], w=[("ps", pss[tt])])
            for tt in range(NT):
                sl = RB[:, tt * 512:(tt + 1) * 512]
                p.op("act", lambda e, tt=tt, sl=sl: e.activation(out=sl, in_=PS[pss[tt]][:, :], func=AF.Ln, scale=1.0 / D, bias=g["epsc"][:, 0:1]),
                     r=[("ps", pss[tt]), "epsc"], w=[("RB", tt)])
                p.op("act", lambda e, sl=sl: e.activation(out=sl, in_=sl, func=AF.Exp, scale=-0.5), r=[("RB", tt)], w=[("RB", tt)])
            for fc in range(16):
                xb = xc[fc % 2]; xk = ("xc", fc % 2)
                p.dma("sp", xb[:], XTd[fc], r=[xkey], w=[xk])
                for tt in range(NT):
                    j = mod_j(tt)
                    sl = xb[:, tt * 512:(tt + 1) * 512]
                    p.op("dve", lambda e, sl=sl, tt=tt, fc=fc, j=j: e.scalar_tensor_tensor(
                        out=sl, in0=sl, scalar=Amod[:, l, fc, j:j + 1], in1=RB[:, tt * 512:(tt + 1) * 512],
                        op0=ALU.mult, op1=ALU.mult), r=[xk, ("RB", tt), "A1A2"], w=[xk])
                    p.op("act", lambda e, sl=sl, tt=tt, fc=fc, j=j: e.activation(
                        out=A(dst, fc, tt * 512, 512), in_=sl, func=AF.Identity,
                        bias=modv[:, l, shift_c0 + fc, j:j + 1], scale=1.0), r=[xk, "modv"], w=["BUF"])
        p.barrier()

    k.norm = norm
    for l in range(DEPTH):
        layer(k, g, l, scoped, norm)
        if k.stop_after:
            break
    with ExitStack() as ph:
        xc = [scoped(ph, "oc%d" % i, [128, 4, 512], F32) for i in range(2)]
        ot = [scoped(ph, "ot%d" % i, [128, 4, 512], F32) for i in range(2)]
        it = 0
        for tt in range(NT):
            for fg in range(4):
                xb = xc[it % 2]; xk = ("oc", it % 2)
                src = XT[0][fg * 4:(fg + 1) * 4, :, tt * 512:(tt + 1) * 512].rearrange("c p t -> p c t")
                p.dma("sp", xb[:], src, r=["XT0"], w=[xk])
                ob = ot[it % 2]; ok = ("ot", it % 2); it += 1
                for t4 in range(4):
                    pi = nextps()
                    for f4 in range(4):
                        p.op("pe", lambda e, pi=pi, f4=f4, t4=t4, xb=xb: e.transpose(
                            PS[pi][:, f4 * 128:(f4 + 1) * 128], xb[:, f4, t4 * 128:(t4 + 1) * 128], ident_f[:]),
                            r=[xk, "ident_f"], w=[("ps", pi)])
                    if t4 % 2 == 0:
                        p.op("dve", lambda e, pi=pi, ob=ob, t4=t4: e.tensor_copy(out=ob[:, t4, :], in_=PS[pi][:, :]),
                             r=[("ps", pi)], w=[ok])
                    else:
                        p.op("act", lambda e, pi=pi, ob=ob, t4=t4: e.copy(out=ob[:, t4, :], in_=PS[pi][:, :]),
                             r=[("ps", pi)], w=[ok])
                dst = g["yout"][tt * 512:(tt + 1) * 512, fg * 512:(fg + 1) * 512].rearrange("(a p) f -> p a f", p=128)
                p.dma("sp", dst, ob[:], r=[ok], w=["yout"])
    p.barrier()
    p.emit(k.es)


def rot(lst):
    st = [0]

    def nxt():
        v = lst[st[0] % len(lst)]
        st[0] += 1
        return v
    return nxt


def layer(k, g, l, scoped, norm):
    nc = k.nc
    p = k.p
    PS = g["PS"]; A = g["A"]
    BUFA = g["BUFA"]; BUFB = g["BUFB"]
    ident_f = g["ident_f"]; ident_b = g["ident_b"]; ones_b = g["ones_b"]
    modv = g["modv"]; A1 = g["A1"]; A2 = g["A2"]
    XT = g["XT"]; PROJ = g["PROJ"]; KV32 = g["KV32"]; UTOK = g["UTOK"]; YTOK = g["YTOK"]
    GS = g["GS"]; MT = g["MT"]; AT = g["AT"]
    w_in = g["w_in"][l]

    def mod_j(tt):
        return 0 if tt == 0 else 1

    def wload(wb, wk, src, K, n, c0=0):
        p.dma("pool", wb[:, 0:K, c0:c0 + n], src.rearrange("(kc p) n -> p kc n", p=128), w=[wk])

    norm(XT[0], "XT0", A1, l, 0, BUFA)

    with ExitStack() as ph:
        wbs = [scoped(ph, "wb%d" % i, [128, 16, 256], BF16) for i in range(3)]
        stg = [scoped(ph, "stg%d" % i, [128, T], BF16) for i in range(2)]
        s32 = [scoped(ph, "s32_%d" % i, [128, 512], F32) for i in range(2)]
        stages = [BUFB[:, i * 8192:(i + 1) * 8192].bitcast(F32).rearrange("p (k n) -> p k n", k=16) for i in range(4)]
        sti = [0]; s3i = [0]
        psr = rot(list(range(8)))
        groups = []
        for j in range(2):
            groups.append(("bf", 0 + j * 256, 256, 0 + 2 * j))
        for (c0, d0) in ((800, 4), (1312, 8), (1824, 12), (2848, 16)):
            for j in range(2):
                groups.append(("bf", c0 + j * 256, 256, d0 + 2 * j))
        groups.append(("kv", 512, 256, 0))
        groups.append(("kr", 768, 32, 2))
        for j in range(32):
            groups.append(("gate", 3360 + j * 256, 256, 2 * j))
        for half in range(2):
            groups.append(("ssm", 2336 + half * 256, 256, half))
        pieces = []
        for (kind, c0, n, d0) in groups:
            if kind == "kr":
                pieces.append([(w_in[:, c0:c0 + 32], 0, 16, 64, 32)])
            else:
                pieces.append([(w_in[:, c0:c0 + n], 0, 16, 0, n)])
        ws = WStream(p, "win", stages, wbs)
        kr_idx = [i for i, g_ in enumerate(groups) if g_[0] == "kr"][0]
        kr_wb = wbs[kr_idx % 3]
        orig_cast_hook = {"done": False}

        def compute(gi_, wb, wk):
            kind, c0, n, d0 = groups[gi_]
            if gi_ + 1 == kr_idx:
                p.op("dve", lambda e: e.memset(kr_wb[:, :, 0:64], 0.0), w=[("winwb", kr_idx % 3)])
            if kind == "ssm":
                half = d0
                for ti in range(T // 128):
                    pi = psr()
                    for kc in range(16):
                        p.op("pe", lambda e, pi=pi, wb=wb, kc=kc, ti=ti: e.matmul(
                            PS[pi][:, 0:256], A(BUFA, kc, ti * 128, 128), wb[:, kc, 0:256],
                            start=(kc == 0), stop=(kc == 15)), r=[wk, "BUF"], w=[("ps", pi)])
                    sb_ = stg[sti[0] % 2]; sk = ("stg", sti[0] % 2); sti[0] += 1
                    p.op("dve", lambda e, pi=pi, sb_=sb_: e.tensor_copy(out=sb_[:, 0:256], in_=PS[pi][:, 0:256]),
                         r=[("ps", pi)], w=[sk])
                    p.dma("sp", UTOK[ti * 128:(ti + 1) * 128, half * 256:(half + 1) * 256], sb_[:, 0:256], r=[sk], w=["UTOK"])
                return
            subs = [(0, 96)] if kind == "kr" else [(0, 128), (128, 128)]
            for si, (sc0, sn) in enumerate(subs):
                if kind in ("bf", "gate"):
                    sb_ = stg[sti[0] % 2]; sk = ("stg", sti[0] % 2); sti[0] += 1
                for tt in range(NT):
                    pi = psr()
                    for kc in range(16):
                        p.op("pe", lambda e, pi=pi, wb=wb, kc=kc, sc0=sc0, sn=sn, tt=tt: e.matmul(
                            PS[pi][0:sn, :], wb[:, kc, sc0:sc0 + sn], A(BUFA, kc, tt * 512, 512),
                            start=(kc == 0), stop=(kc == 15)), r=[wk, "BUF"], w=[("ps", pi)])
                    if kind == "bf":
                        p.op("dve", lambda e, pi=pi, sb_=sb_, tt=tt: e.tensor_copy(
                            out=sb_[:, tt * 512:(tt + 1) * 512], in_=PS[pi][:, :]), r=[("ps", pi)], w=[sk])
                    elif kind == "gate":
                        p.op("act", lambda e, pi=pi, sb_=sb_, tt=tt: e.activation(
                            out=sb_[:, tt * 512:(tt + 1) * 512], in_=PS[pi][:, :], func=AF.Sigmoid),
                            r=[("ps", pi)], w=[sk])
                    else:
                        s3 = s32[s3i[0] % 2]; s3k = ("s32", s3i[0] % 2); s3i[0] += 1
                        r0 = 64 if kind == "kr" else 0
                        r1 = 96 if kind == "kr" else 128
                        p.op("dve", lambda e, pi=pi, s3=s3, r0=r0, r1=r1: e.tensor_copy(
                            out=s3[r0:r1, :], in_=PS[pi][r0:r1, :]), r=[("ps", pi)], w=[s3k])
                        p.dma("sp", KV32[d0 + si, r0:r1, tt * 512:(tt + 1) * 512], s3[r0:r1, :], r=[s3k], w=["KV32"])
                if kind == "bf":
                    p.dma("sp", PROJ[d0 + si], sb_[:], r=[sk], w=["PROJ"])
                elif kind == "gate":
                    p.dma("sp", GS[d0 + si], sb_[:], r=[sk], w=["GS"])
        ws.run(pieces, compute)
    p.barrier()
    if k.stop_after == "P2":
        return
    mixers(k, g, l, scoped)
    p.barrier()
    if k.stop_after == "P3":
        return
    tail(k, g, l, scoped, norm)


def make_in_maps(inp):
    f = np.float32
    consts = host_consts()
    shared = dict(consts)

    def featl(a, n):
        return np.ascontiguousarray(a.reshape(DEPTH, n, 128).transpose(0, 2, 1))
    shared["w_ada"] = inp["w_ada"]
    shared["b_adaT"] = featl(inp["b_ada"], 96)
    shared["n1gT"] = featl(inp["norm1_g"], 16)
    shared["n2gT"] = featl(inp["norm2_g"], 16)
    shared["w_in"] = inp["w_in"]
    shared["qagT"] = featl(inp["q_a_norm_g"], 4)
    shared["kvgT"] = featl(inp["kv_a_norm_g"], 2)
    shared["w_uq"] = inp["w_uq"]
    shared["w_ukv"] = inp["w_ukv"]
    shared["qng"] = np.ascontiguousarray(inp["q_norm_g"].reshape(DEPTH, 96, 1))
    shared["kng"] = np.ascontiguousarray(inp["k_norm_g"].reshape(DEPTH, 96, 1))
    shared["w_mla_o"] = inp["w_mla_o"]
    shared["conv_wT"] = np.ascontiguousarray(inp["conv_w"].reshape(DEPTH, 3, 4, 128).transpose(0, 3, 2, 1))
    shared["conv_bT"] = featl(inp["conv_b"], 4)
    shared["w_conv_o"] = inp["w_conv_o"]
    for nm in ("ssm_lam_re", "ssm_lam_im", "ssm_log_step", "ssm_b_re", "ssm_b_im", "ssm_c_re", "ssm_c_im",
               "ssm_d", "w_glu", "pool_w", "w_pool_o", "w_o", "w_mlp1", "w_mlp2"):
        shared[nm] = inp[nm]
    shared["pool_sT"] = featl(inp["pool_scale"], 4)
    dv = inp["ssm_d"].reshape(DEPTH, 32, 16)
    shared["ssm_dvec"] = np.ascontiguousarray(np.tile(dv.transpose(2, 0, 1)[None], (8, 1, 1, 1)).reshape(128, DEPTH, 32))
    maps = []
    for c in range(8):
        m = dict(shared)
        m["xin"] = np.ascontiguousarray(np.concatenate(
            [inp["x_prompt"][2 * c], inp["x_prompt"][2 * c + 1], inp["x_sample"][c]], axis=0))
        cv = np.stack([inp["c_ctx"], inp["c"][c]], axis=0)
        m["cvecT"] = np.ascontiguousarray(cv.reshape(2, 16, 128).transpose(2, 1, 0))
        m["cache_ckv"] = np.ascontiguousarray(inp["cache_ckv"][c])
        m["cache_kr"] = np.ascontiguousarray(inp["cache_krope"][c])
        m["state_in"] = np.ascontiguousarray(inp["state_ssm"][c])
        maps.append(m)
    return maps


_CACHE = {}


def kernel(**inputs):
    inp = {k_: np.asarray(v) for k_, v in inputs.items()}
    if "k" not in _CACHE:
        _CACHE["k"] = build()
    k = _CACHE["k"]
    maps = make_in_maps(inp)
    res = run_bass_kernel_spmd(k.nc, maps, core_ids=list(range(8)))
    R = res.results
    y_prompt = np.zeros((16, 256, D), np.float32)
    y_sample = np.zeros((8, 2048, D), np.float32)
    new_ckv = np.zeros((16, DEPTH, 256, 256), np.float32)
    new_kr = np.zeros((16, DEPTH, 256, 32), np.float32)
    new_ssm = np.zeros((16, DEPTH, 2, 2, 32, 64), np.float32)
    for c in range(8):
        y = R[c]["yout"]
        y_prompt[2 * c] = y[0:256]
        y_prompt[2 * c + 1] = y[256:512]
        y_sample[c] = y[512:]
        new_ckv[2 * c:2 * c + 2] = R[c]["o_ckv"]
        new_kr[2 * c:2 * c + 2] = R[c]["o_kr"]
        new_ssm[2 * c:2 * c + 2] = R[c]["o_ssm"]
    return (y_prompt, y_sample, new_ckv, new_kr, new_ssm)


def mixers(k, g, l, scoped):
    nc = k.nc
    p = k.p
    PS = g["PS"]; A = g["A"]
    BUFA = g["BUFA"]; BUFB = g["BUFB"]
    ident_f = g["ident_f"]; ident_b = g["ident_b"]; ones_b = g["ones_b"]; ones_f = g["ones_f"]; PSD = g["PSD"]; epsc = g["epsc"]
    PROJ = g["PROJ"]; KV32 = g["KV32"]
    QN, CKV, QH, KH, VE, VO, KPE = 0, 10240, 16384, 18944, 22016, 25088, 28160
    SCALE = 96 ** -0.5

    def rstd_from_ps(pi, rows, n, dim, rs, rk):
        p.op("act", lambda e: e.activation(out=rs[0:rows, 0:n], in_=PS[pi][0:rows, 0:n], func=AF.Ln, scale=1.0 / dim, bias=epsc[0:rows, 0:1]),
             r=[("ps", pi), "epsc"], w=[rk])
        p.op("act", lambda e: e.activation(out=rs[0:rows, 0:n], in_=rs[0:rows, 0:n], func=AF.Exp, scale=-0.5), r=[rk], w=[rk])

    with ExitStack() as ph:
        wuq = scoped(ph, "wuq", [128, 4, 768], BF16)
        wukv = scoped(ph, "wukv", [128, 2, 1024], BF16)
        qag = scoped(ph, "qag", [128, 4], F32)
        kvg = scoped(ph, "kvg", [128, 2], F32)
        qng = scoped(ph, "qng", [96, 1], F32)
        kng = scoped(ph, "kng", [96, 1], F32)
        protf = scoped(ph, "protf", [128, 96], F32)
        f32t = [scoped(ph, "f32t0", [128, 2, 512], F32)] * 2
        esum = [f32t[0][:, 0, :], f32t[0][:, 1, :]]
        sq3 = [scoped(ph, "sq3_%d" % i, [128, 512], BF16) for i in range(3)]
        rs3 = [scoped(ph, "rs3_%d" % i, [128, 512], F32) for i in range(3)]
        tb3 = [scoped(ph, "tb3_%d" % i, [128, 512], F32) for i in range(3)]
        ta2 = [scoped(ph, "ta2_%d" % i, [128, 512], F32) for i in range(2)]
        otile = scoped(ph, "otile", [128, 4, 256], F32)
        Et = [scoped(ph, "Et%d" % i, [128, 1024], BF16) for i in range(2)]
        Et.append(otile[:].rearrange("p a b -> p (a b)").bitcast(BF16)[:, 0:1024])
        etflag = [False]
        esflag = [False]
        dblr = rot([1, 2, 3])
        krt = scoped(ph, "krt", [128, 4, 32], F32)
        cct = otile
        ckr = tb3[1][:, 0:384].rearrange("p (a b) -> p a b", a=4)
        kpf = tb3[0]
        sqi = [0]; rsi = [0]; fi = [0]; ei = [0]
        psr = rot([4, 5, 6, 7])

        st_uq = BUFB[:, 4 * T:4 * T + 6144].bitcast(F32).rearrange("p (k n) -> p k n", k=4)
        st_ukv = BUFB[:, 4 * T + 6144:4 * T + 10240].bitcast(F32).rearrange("p (k n) -> p k n", k=2)
        st_cos = BUFB[:, 4 * T + 10240:4 * T + 14336].bitcast(F32)
        st_sin = BUFB[:, 4 * T + 14336:4 * T + 18432].bitcast(F32)
        p.dma("sp", st_uq, g["w_uq"][l].rearrange("(kc p) n -> p kc n", p=128), w=["st_uq"])
        p.dma("sp", st_ukv, g["w_ukv"][l].rearrange("(kc p) n -> p kc n", p=128), w=["st_ukv"])
        p.op("act", lambda e: e.copy(out=wuq[:], in_=st_uq), r=["st_uq"], w=["wuq"])
        p.op("pool", lambda e: e.tensor_copy(out=wukv[:], in_=st_ukv), r=["st_ukv"], w=["wukv"])
        p.dma("sp", qag[:], g["qagT"][l], w=["qag"])
        p.dma("sp", kvg[:], g["kvgT"][l], w=["kvg"])
        p.dma("sp", qng[:], g["qng"][l], w=["qng"])
        p.dma("sp", kng[:], g["kng"][l], w=["kng"])
        p.dma("sp", st_cos[64:96, :], g["c_cos"][:, :], w=["st_cos"])
        p.dma("sp", st_sin[64:96, :], g["c_sin"][:, :], w=["st_sin"])
        p.op("pool", lambda e: e.tensor_copy(out=BUFA[64:96, 31232:33280], in_=st_cos[64:96, :]), r=["st_cos"], w=["cos"])
        p.op("pool", lambda e: e.tensor_copy(out=BUFA[64:96, 33280:35328], in_=st_sin[64:96, :]), r=["st_sin"], w=["sin"])
        p.dma("sp", protf[:], g["c_prot"][:, :], w=["prot"])
        p.op("pool", lambda e: e.memset(BUFA[:, VE:VE + 3072].rearrange("p (j c) -> p j c", c=128)[:, :, 64:128], 1.0), w=["V"])
        p.op("pool", lambda e: e.memset(BUFA[:, VO:VO + 3072].rearrange("p (j c) -> p j c", c=128)[:, :, 0:64], 1.0), w=["V"])
        p.op("pool", lambda e: e.memset(ckr[:], 0.0), w=[("tb3", 1)])
        for c in range(4):
            p.dma("sp", A(BUFA, c, 0, T), PROJ[c], r=["PROJ"], w=[("qn", c)])
        for tt in range(NT):
            pi = psr()
            for c in range(4):
                sq = sq3[sqi[0] % 3]; sk = ("sq3", sqi[0] % 3); sqi[0] += 1
                p.op("act", lambda e, sq=sq, c=c, tt=tt: e.activation(out=sq[:], in_=A(BUFA, c, tt * 512, 512), func=AF.Square),
                     r=[("qn", c)], w=[sk])
                p.op("pe", lambda e, sq=sq, c=c, pi=pi: e.matmul(PS[pi][:, :], ones_b[:, :], sq[:], start=(c == 0), stop=(c == 3)),
                     r=[sk, "ones_b"], w=[("ps", pi)])
            rs = rs3[rsi[0] % 3]; rk = ("rs3", rsi[0] % 3); rsi[0] += 1
            rstd_from_ps(pi, 128, 512, 512.0, rs, rk)
            for c in range(4):
                p.op("dve", lambda e, c=c, tt=tt, rs=rs: e.scalar_tensor_tensor(
                    out=A(BUFA, c, tt * 512, 512), in0=A(BUFA, c, tt * 512, 512), scalar=qag[:, c:c + 1], in1=rs[:, :],
                    op0=ALU.mult, op1=ALU.mult), r=[("qn", c), rk, "qag"], w=[("qn", c)])
        for tt in range(NT):
            ft = f32t[0]; fk = ("f32t", 0); fi[0] += 1
            p.dma("sp", ft[:], KV32[0:2, :, tt * 512:(tt + 1) * 512].rearrange("c p t -> p c t"), r=["KV32"], w=[fk])
            pi = psr()
            for c in range(2):
                sq = sq3[sqi[0] % 3]; sk = ("sq3", sqi[0] % 3); sqi[0] += 1
                p.op("act", lambda e, sq=sq, c=c, ft=ft: e.activation(out=sq[:], in_=ft[:, c, :], func=AF.Square), r=[fk], w=[sk])
                p.op("pe", lambda e, sq=sq, c=c, pi=pi: e.matmul(PS[pi][:, :], ones_b[:, :], sq[:], start=(c == 0), stop=(c == 1)),
                     r=[sk, "ones_b"], w=[("ps", pi)])
            rs = rs3[rsi[0] % 3]; rk = ("rs3", rsi[0] % 3); rsi[0] += 1
            rstd_from_ps(pi, 128, 512, 256.0, rs, rk)
            for c in range(2):
                p.op("dve", lambda e, c=c, ft=ft, rs=rs: e.scalar_tensor_tensor(
                    out=ft[:, c, :], in0=ft[:, c, :], scalar=kvg[:, c:c + 1], in1=rs[:, :], op0=ALU.mult, op1=ALU.mult),
                    r=[fk, rk, "kvg"], w=[fk])
                p.op("act", lambda e, c=c, ft=ft, tt=tt: e.copy(
                    out=BUFA[:, CKV + c * TK + tt * 512: CKV + c * TK + (tt + 1) * 512], in_=ft[:, c, :]), r=[fk], w=["ckvT"])
            if tt == 0:
                for t4 in range(4):
                    pi2 = psr()
                    for c in range(2):
                        p.op("pe", lambda e, pi2=pi2, c=c, t4=t4, ft=ft: e.transpose(
                            PS[pi2][:, c * 128:(c + 1) * 128], ft[:, c, t4 * 128:(t4 + 1) * 128], ident_f[:]),
                            r=[fk, "ident_f"], w=[("ps", pi2)])
                    p.op("dve", lambda e, pi2=pi2, t4=t4: e.tensor_copy(out=otile[:, t4, :], in_=PS[pi2][:, 0:256]),
                         r=[("ps", pi2)], w=["otile"])
                for b_ in range(2):
                    p.dma("sp", g["o_ckv"][b_, l, :, :].rearrange("(h p) f -> p h f", p=128), otile[:, 2 * b_:2 * b_ + 2, :], r=["otile"], w=["o_ckv"])
        for tt in range(NT):
            p.dma("sp", kpf[64:96, :], KV32[2, 64:96, tt * 512:(tt + 1) * 512], r=["KV32"], w=[("tb3", 0)])
            p.op("dve", lambda e, tt=tt: e.tensor_copy(out=BUFA[64:96, KPE + tt * 512: KPE + (tt + 1) * 512], in_=kpf[64:96, :]),
                 r=[("tb3", 0)], w=["kpe"])
            if tt == 0:
                pi2 = psr()
                for t4 in range(4):
                    p.op("pe", lambda e, pi2=pi2, t4=t4: e.transpose(
                        PS[pi2][:, t4 * 32:(t4 + 1) * 32], kpf[64:96, t4 * 128:(t4 + 1) * 128], ident_f[64:96, 64:96]),
                        r=[("tb3", 0), "ident_f"], w=[("ps", pi2)])
                p.op("dve", lambda e, pi2=pi2: e.tensor_copy(out=krt[:], in_=PS[pi2][:, 0:128].rearrange("p (a b) -> p a b", a=4)),
                     r=[("ps", pi2)], w=["krt"])
                for b_ in range(2):
                    p.dma("sp", g["o_kr"][b_, l, :, :].rearrange("(h p) f -> p h f", p=128), krt[:, 2 * b_:2 * b_ + 2, :], r=["krt"], w=["o_kr"])
        p.dma("sp", cct[:], g["cache_ckv"][l].rearrange("(a p) f -> p a f", p=128), w=["otile"])
        p.dma("sp", ckr[:, :, 64:96], g["cache_kr"][l].rearrange("(a p) f -> p a f", p=128), r=[], w=[("tb3", 1)])
        for c in range(2):
            pi2 = psr()
            for t4 in range(4):
                p.op("pe", lambda e, pi2=pi2, c=c, t4=t4: e.transpose(
                    PS[pi2][:, t4 * 128:(t4 + 1) * 128], cct[:, t4, c * 128:(c + 1) * 128], ident_f[:]),
                    r=["otile", "ident_f"], w=[("ps", pi2)])
            p.op("act", lambda e, pi2=pi2, c=c: e.copy(out=BUFA[:, CKV + c * TK + T: CKV + c * TK + T + 512], in_=PS[pi2][:, :]),
                 r=[("ps", pi2)], w=["ckvT"])
        pi2 = psr()
        for t4 in range(4):
            p.op("pe", lambda e, pi2=pi2, t4=t4: e.transpose(
                PS[pi2][0:96, t4 * 128:(t4 + 1) * 128], ckr[:, t4, :], ident_f[:]), r=[("tb3", 1), "ident_f"], w=[("ps", pi2)])
        p.op("dve", lambda e, pi2=pi2: e.tensor_copy(out=BUFA[64:96, KPE + T: KPE + T + 512], in_=PS[pi2][64:96, :]),
             r=[("ps", pi2)], w=["kpe"])

        COSO, SINO, KPS = 31232, 33280, 35328
        for ti in range(6):
            p.op("act", lambda e, ti=ti: e.activation(out=BUFA[64:96, KPS + ti * 512:KPS + (ti + 1) * 512],
                                                     in_=BUFA[64:96, KPE + ti * 512:KPE + (ti + 1) * 512], func=AF.Square),
                 r=["kpe"], w=["kps"])
        psr8 = rot(list(range(8)))
        ucnt = [0]

        class U_:
            pass

        def stageA(u):
            pi = psr8(); u.pi = pi
            i3 = ucnt[0] % 3; ucnt[0] += 1
            u.i3 = i3
            sq = sq3[i3]; sk = ("sq3", i3)
            if u.kind == "q":
                for kc in range(4):
                    p.op("pe", lambda e, pi=pi, kc=kc, ti=u.ti, h=u.h: e.matmul(
                        PS[pi][0:96, :], wuq[:, kc, h * 96:(h + 1) * 96], A(BUFA, kc, ti * 512, 512),
                        start=(kc == 0), stop=(kc == 3)), r=["wuq", ("qn", kc)], w=[("ps", pi)])
                u.rows = 96
            else:
                for kc in range(2):
                    p.op("pe", lambda e, pi=pi, kc=kc, ti=u.ti, h=u.h: e.matmul(
                        PS[pi][0:64, :], wukv[:, kc, h * 128:h * 128 + 64],
                        BUFA[:, CKV + kc * TK + ti * 512: CKV + kc * TK + (ti + 1) * 512],
                        start=(kc == 0), stop=(kc == 1)), r=["wukv", "ckvT"], w=[("ps", pi)])
                u.rows = 64
            rows = u.rows
            p.op("act", lambda e, pi=pi, sq=sq, rows=rows: e.activation(out=sq[0:rows, :], in_=PS[pi][0:rows, :], func=AF.Square),
                 r=[("ps", pi)], w=[sk])

        def stageB(u):
            pi, i3 = u.pi, u.i3
            sq = sq3[i3]; sk = ("sq3", i3)
            rs = rs3[i3]; rk = ("rs3", i3)
            pj = psr8()
            if u.kind == "q":
                p.op("pe", lambda e, sq=sq, pj=pj: e.matmul(PS[pj][0:96, :], ones_b[0:96, 0:96], sq[0:96, :], start=True, stop=True),
                     r=[sk, "ones_b"], w=[("ps", pj)])
                gv, gkey, dst, dk = qng, "qng", QH, "qkq"
                do_rope = u.ti >= 1
            else:
                p.op("pe", lambda e, sq=sq, pj=pj: e.matmul(PS[pj][0:96, :], ones_b[0:64, 0:96], sq[0:64, :], start=True, stop=False),
                     r=[sk, "ones_b"], w=[("ps", pj)])
                p.op("pe", lambda e, pj=pj, ti=u.ti: e.matmul(PS[pj][0:96, :], ones_b[64:96, 0:96],
                                                              BUFA[64:96, KPS + ti * 512:KPS + (ti + 1) * 512], start=False, stop=True),
                     r=["kps", "ones_b"], w=[("ps", pj)])
                gv, gkey, dst, dk = kng, "kng", KH, "qkk"
                do_rope = 1 <= u.ti <= 4
            rstd_from_ps(pj, 96, 512, 96.0, rs, rk)
            u.do_rope = do_rope
            u.dcol = dst + u.ti * 512
            u.dk = dk
            dcol = u.dcol
            if do_rope:
                tb = tb3[i3]; tk_ = ("tb3", i3)
                outf = lambda r0, r1, tb=tb: tb[r0:r1, :]
                wkeys = [tk_]
            else:
                outf = lambda r0, r1, dcol=dcol: BUFA[r0:r1, dcol:dcol + 512]
                wkeys = [dk]
            if u.kind == "q":
                p.op("dve", lambda e, rs=rs, gv=gv, pi=pi, outf=outf: e.scalar_tensor_tensor(
                    out=outf(0, 96), in0=PS[pi][0:96, :], scalar=gv[0:96, 0:1], in1=rs[0:96, :],
                    op0=ALU.mult, op1=ALU.mult), r=[("ps", pi), rk, gkey], w=wkeys)
            else:
                p.op("dve", lambda e, rs=rs, gv=gv, pi=pi, outf=outf: e.scalar_tensor_tensor(
                    out=outf(0, 64), in0=PS[pi][0:64, :], scalar=gv[0:64, 0:1], in1=rs[0:64, :],
                    op0=ALU.mult, op1=ALU.mult), r=[("ps", pi), rk, gkey], w=wkeys)
                p.op("dve", lambda e, rs=rs, gv=gv, ti=u.ti, outf=outf: e.scalar_tensor_tensor(
                    out=outf(64, 96), in0=BUFA[64:96, KPE + ti * 512:KPE + (ti + 1) * 512], scalar=gv[64:96, 0:1], in1=rs[64:96, :],
                    op0=ALU.mult, op1=ALU.mult), r=["kpe", rk, gkey], w=wkeys)

        def stageC(u):
            if not u.do_rope:
                return
            i3 = u.i3
            tb = tb3[i3]; tk_ = ("tb3", i3)
            ta = ta2[i3 % 2]; tak = ("ta2", i3 % 2)
            ci = u.ti - 1
            dcol = u.dcol
            dk = u.dk
            pr = psr8()
            p.op("pe", lambda e, pr=pr, tb=tb: e.matmul(PS[pr][0:96, :], protf[64:96, 0:96], tb[64:96, :], start=True, stop=True),
                 r=[tk_, "prot"], w=[("ps", pr)])
            p.op("act", lambda e, dcol=dcol, tb=tb: e.copy(out=BUFA[0:64, dcol:dcol + 512], in_=tb[0:64, :]), r=[tk_], w=[dk])
            p.op("dve", lambda e, ci=ci, tb=tb, ta=ta: e.tensor_tensor(out=ta[64:96, :], in0=tb[64:96, :],
                                                                     in1=BUFA[64:96, COSO + ci * 512:COSO + (ci + 1) * 512], op=ALU.mult),
                 r=[tk_, "cos"], w=[tak])
            p.op("dve", lambda e, ci=ci, pr=pr, tb=tb: e.tensor_tensor(out=tb[64:96, :], in0=PS[pr][64:96, :],
                                                                     in1=BUFA[64:96, SINO + ci * 512:SINO + (ci + 1) * 512], op=ALU.mult),
                 r=[("ps", pr), "sin", tk_], w=[tk_])
            p.op("dve", lambda e, dcol=dcol, ta=ta, tb=tb: e.tensor_tensor(out=BUFA[64:96, dcol:dcol + 512], in0=ta[64:96, :], in1=tb[64:96, :], op=ALU.add),
                 r=[tak, tk_], w=[dk])

        for h in range(8):
            units = []
            for (kind, ti) in [("q", tt) for tt in range(NT)] + [("k", kt) for kt in range(6)]:
                u = U_(); u.kind = kind; u.ti = ti; u.h = h
                units.append(u)
            nU = len(units)
            for st_ in range(nU + 2):
                if st_ < nU:
                    stageA(units[st_])
                if 0 <= st_ - 1 < nU:
                    stageB(units[st_ - 1])
                if 0 <= st_ - 2 < nU:
                    stageC(units[st_ - 2])
            voff = (VE if h % 2 == 0 else VO)
            c0 = (h % 2) * 64
            for g3 in range(3):
                pi = psr()
                for j in range(8):
                    kt = g3 * 8 + j
                    for kc in range(2):
                        p.op("pe", lambda e, pi=pi, j=j, kt=kt, kc=kc, h=h: e.matmul(
                            PS[pi][:, j * 64:(j + 1) * 64], BUFA[:, CKV + kc * TK + kt * 128: CKV + kc * TK + (kt + 1) * 128],
                            wukv[:, kc, h * 128 + 64:h * 128 + 128], start=(kc == 0), stop=(kc == 1)),
                            r=["wukv", "ckvT"], w=[("ps", pi)])
                dstv = BUFA[:, voff + g3 * 1024: voff + (g3 + 1) * 1024].rearrange("p (j c) -> p j c", c=128)[:, :, c0:c0 + 64]
                p.op("act", lambda e, pi=pi, dstv=dstv: e.copy(out=dstv, in_=PS[pi][:, :].rearrange("p (j c) -> p j c", c=64)),
                     r=[("ps", pi)], w=["V"])
            r0 = c0
            jobs = [(0, 256, [0, 1]), (256, 256, [2, 3])] + [(512 * tt, 512, list(range(4, 24))) for tt in range(1, 5)]
            for ji, (q0, nq, kts) in enumerate(jobs):
                po, pd = (0, 1)
                pairs = [(kts[i], kts[i + 1]) for i in range(0, len(kts), 2)]
                npair = len(pairs)

                def emitS(pi_, q0=q0, nq=nq, pairs=pairs):
                    dbl = dblr()
                    for hf in range(2):
                        kt = pairs[pi_][hf]
                        p.op("pe", lambda e, dbl=dbl, hf=hf, kt=kt, q0=q0, nq=nq: e.matmul(
                            PS[2 * dbl + hf][:, 0:nq], BUFA[0:96, KH + kt * 128: KH + (kt + 1) * 128], BUFA[0:96, QH + q0: QH + q0 + nq],
                            start=True, stop=True), r=["qkq", "qkk"], w=[("ps", 2 * dbl + hf)])
                    return dbl
                pend = [emitS(0)]
                if npair > 1:
                    pend.append(emitS(1))
                if npair > 2:
                    pend.append(emitS(2))
                first = {"dve": True, "pool": True}
                used = []
                for pi_ in range(npair):
                    dbl = pend[pi_]
                    E = Et[ei[0] % 3]; ek = ("E", ei[0] % 3); ei[0] += 1
                    extra = ["otile"] if (E is Et[2] and not etflag[0]) else []
                    if extra:
                        etflag[0] = True
                    p.op("act", lambda e, dbl=dbl, E=E, nq=nq: e.activation(
                        out=E[:, :].rearrange("p (h c) -> p h c", h=2)[:, :, 0:nq],
                        in_=PSD[dbl][:, :].rearrange("p (h c) -> p h c", h=2)[:, :, 0:nq], func=AF.Exp, scale=SCALE),
                        r=[("ps", 2 * dbl), ("ps", 2 * dbl + 1)], w=[ek] + extra)
                    for hf in range(2):
                        kt = pairs[pi_][hf]
                        first_mm = (pi_ == 0 and hf == 0)
                        last_mm = (pi_ == npair - 1 and hf == 1)
                        p.op("pe", lambda e, E=E, kt=kt, nq=nq, po=po, hf=hf, first_mm=first_mm, last_mm=last_mm, voff=voff: e.matmul(
                            PS[po][:, 0:nq], BUFA[:, voff + kt * 128: voff + (kt + 1) * 128], E[:, hf * 512:hf * 512 + nq],
                            start=first_mm, stop=last_mm), r=[ek, "V"], w=[("ps", po)])
                    if pi_ + 3 < npair:
                        pend.append(emitS(pi_ + 3))
                oh = 64 - r0
                dsb = tb3[ji % 3]; dk_ = ("tb3", ji % 3)
                p.op("act", lambda e, dsb=dsb, po=po, oh=oh, nq=nq: e.copy(out=dsb[oh:oh + 64, 0:nq], in_=PS[po][oh:oh + 64, 0:nq]),
                     r=[("ps", po)], w=[dk_])
                p.op("pe", lambda e, dsb=dsb, pd=pd, oh=oh, r0=r0, nq=nq: e.matmul(
                    PS[pd][r0:r0 + 64, 0:nq], ident_f[oh:oh + 64, oh:oh + 64], dsb[oh:oh + 64, 0:nq], start=True, stop=True),
                    r=[dk_, "ident_f"], w=[("ps", pd)])
                rs = rs3[rsi[0] % 3]; rk = ("rs3", rsi[0] % 3); rsi[0] += 1
                p.op("act", lambda e, rs=rs, pd=pd, nq=nq, r0=r0: e.activation(out=rs[r0:r0 + 64, 0:nq], in_=PS[pd][r0:r0 + 64, 0:nq], func=AF.Ln),
                     r=[("ps", pd)], w=[rk])
                p.op("act", lambda e, rs=rs, nq=nq, r0=r0: e.activation(out=rs[r0:r0 + 64, 0:nq], in_=rs[r0:r0 + 64, 0:nq], func=AF.Exp, scale=-1.0),
                     r=[rk], w=[rk])
                p.op("dve", lambda e, rs=rs, po=po, nq=nq, r0=r0, q0=q0, h=h: e.tensor_tensor(
                    out=BUFB[r0:r0 + 64, (h // 2) * T + q0:(h // 2) * T + q0 + nq], in0=PS[po][r0:r0 + 64, 0:nq],
                    in1=rs[r0:r0 + 64, 0:nq], op=ALU.mult), r=[("ps", po), rk], w=["BI"])
    p.barrier()
    if k.stop_after == "MLA":
        return
    mixers2(k, g, l, scoped)


PADW = T + 64


def pcol(t):
    for si, (s0, n) in enumerate(SEGS):
        if s0 <= t < s0 + n:
            return t + 16 * si + 8
    raise ValueError


def mixers2(k, g, l, scoped):
    nc = k.nc
    p = k.p
    PS = g["PS"]; A = g["A"]
    BUFA = g["BUFA"]; BUFB = g["BUFB"]
    PROJ = g["PROJ"]
    psr = rot(list(range(8)))
    with ExitStack() as ph:
        cw = scoped(ph, "cw", [128, 4, 3], F32)
        cb = scoped(ph, "cb", [128, 4], F32)
        ub = [scoped(ph, "cu0", [128, 3, T], BF16)] * 2
        v = scoped(ph, "cv", [128, T], F32)
        acc = scoped(ph, "cacc", [128, T], F32)
        p.dma("sp", cw[:], g["conv_wT"][l], w=["cw"])
        p.dma("sp", cb[:], g["conv_bT"][l], w=["cb"])
        for j in range(4):
            u = ub[0]; uk = ("cu", 0)
            for i3, c in enumerate((4 + j, 8 + j, 12 + j)):
                p.dma("sp", u[:, i3, :], PROJ[c], r=["PROJ"], w=[uk])
            p.op("dve", lambda e, u=u: e.tensor_tensor(out=v[:], in0=u[:, 2, :], in1=u[:, 0, :], op=ALU.mult), r=[uk], w=["cv"])
            p.op("dve", lambda e, j=j: e.tensor_scalar(out=acc[:], in0=v[:], scalar1=cw[:, j, 1:2], scalar2=cb[:, j:j + 1],
                                                      op0=ALU.mult, op1=ALU.add), r=["cv", "cw", "cb"], w=["cacc"])
            for (s0, n) in SEGS:
                p.op("dve", lambda e, j=j, s0=s0, n=n: e.scalar_tensor_tensor(
                    out=acc[:, s0 + 1:s0 + n], in0=v[:, s0:s0 + n - 1], scalar=cw[:, j, 0:1], in1=acc[:, s0 + 1:s0 + n],
                    op0=ALU.mult, op1=ALU.add), r=["cv", "cacc", "cw"], w=["cacc"])
                p.op("dve", lambda e, j=j, s0=s0, n=n: e.scalar_tensor_tensor(
                    out=acc[:, s0:s0 + n - 1], in0=v[:, s0 + 1:s0 + n], scalar=cw[:, j, 2:3], in1=acc[:, s0:s0 + n - 1],
                    op0=ALU.mult, op1=ALU.add), r=["cv", "cacc", "cw"], w=["cacc"])
            p.op("dve", lambda e, j=j, u=u: e.tensor_tensor(out=A(BUFB, 4 + j, 0, T), in0=acc[:], in1=u[:, 1, :], op=ALU.mult),
                 r=["cacc", uk], w=["BI"])
    p.barrier()
    with ExitStack() as ph:
        pu = scoped(ph, "pu", [128, PADW], BF16)
        w2 = scoped(ph, "pw2", [128, PADW], F32)
        w4 = scoped(ph, "pw4", [128, PADW], F32)
        inv = scoped(ph, "pinv", [128, T], F32)
        pm = scoped(ph, "pm", [128, T], BF16)
        pw = scoped(ph, "pw", [128, 4, 128], BF16)
        psc = scoped(ph, "psc", [128, 4], F32)
        p.dma("pool", pw[:], g["pool_w"][l].rearrange("g c d -> c g d"), w=["pw"])
        p.dma("sp", psc[:], g["pool_sT"][l], w=["psc"])
        p.op("pool", lambda e: e.memset(pu[:], 0.0), w=["pu"])
        p.op("pool", lambda e: e.memset(w2[:], 0.0), w=["pw2"])
        p.op("pool", lambda e: e.memset(w4[:], 0.0), w=["pw4"])
        W_ = PADW
        for gi in range(4):
            for (s0, n) in SEGS:
                p.dma("sp", pu[:, pcol(s0):pcol(s0) + n], PROJ[16 + gi, :, s0:s0 + n], r=["PROJ"], w=["pu"])
            p.dma("sp", inv[:], g["c_pinv"][gi], w=["pinv"])
            p.op("dve", lambda e: e.tensor_tensor(out=w2[:, 1:W_], in0=pu[:, 0:W_ - 1], in1=pu[:, 1:W_], op=ALU.add), r=["pu", "pw4"], w=["pw2"])
            cur, curk, oth, othk = w2, "pw2", w4, "pw4"
            sh = 1
            for lev in range(gi):
                p.op("dve", lambda e, cur=cur, oth=oth, sh=sh: e.tensor_tensor(
                    out=oth[:, sh:W_ - sh], in0=cur[:, 0:W_ - 2 * sh], in1=cur[:, 2 * sh:W_], op=ALU.add), r=[curk], w=[othk])
                cur, curk, oth, othk = oth, othk, cur, curk
                sh *= 2
            for (s0, n) in SEGS:
                c0 = pcol(s0)
                p.op("dve", lambda e, cur=cur, s0=s0, n=n, c0=c0: e.tensor_tensor(
                    out=cur[:, c0:c0 + n], in0=cur[:, c0:c0 + n], in1=inv[:, s0:s0 + n], op=ALU.mult), r=[curk, "pinv"], w=[curk])
                p.op("dve", lambda e, cur=cur, s0=s0, n=n, c0=c0: e.tensor_tensor(
                    out=pm[:, s0:s0 + n], in0=cur[:, c0:c0 + n], in1=pu[:, c0:c0 + n], op=ALU.subtract), r=[curk, "pu"], w=["pm"])
            for tt in range(NT):
                pi = psr()
                p.op("pe", lambda e, pi=pi, gi=gi, tt=tt: e.matmul(PS[pi][:, :], pw[:, gi, :], pm[:, tt * 512:(tt + 1) * 512], start=True, stop=True),
                     r=["pw", "pm"], w=[("ps", pi)])
                p.op("act", lambda e, pi=pi, gi=gi, tt=tt: e.activation(
                    out=A(BUFB, 12 + gi, tt * 512, 512), in_=PS[pi][:, :], func=AF.Copy, scale=psc[:, gi:gi + 1]),
                    r=[("ps", pi), "psc"], w=["BI"])
    p.barrier()
    ssm(k, g, l, scoped)
    if "DBG_BI" in k.debug:
        p.barrier()
        for c in range(16):
            p.dma("sp", k.DBG_BI[c], A(BUFB, c, 0, T), r=["BI"], w=["DBG"])


def ssm(k, g, l, scoped):
    nc = k.nc
    p = k.p
    PS = g["PS"]
    BUFA = g["BUFA"]; BUFB = g["BUFB"]
    ident_b = g["ident_b"]; CAA = g["CAA"]; CAB = g["CAB"]
    UTOK = g["UTOK"]; YTOK = g["YTOK"]
    SSM_BLT = g["SSM_BLT"]; SSM_ML = g["SSM_ML"]; SSM_CS = g["SSM_CS"]
    psr = rot(list(range(8)))
    SEQ = [(0, 16, 0), (17, 16, 16), (34, 128, 32)]
    NCOL = 163
    UL0 = 8 * T
    Sv = BUFA[:, 0:20864].bitcast(F32).rearrange("p (r d g c) -> p r d g c", r=2, d=2, g=16)

    def UL(gg, ch, q0, n):
        o = UL0 + (gg * 2 + ch) * 160 + q0
        return BUFB[:, o:o + n]

    with ExitStack() as ph:
        Sbf = scoped(ph, "Sbf", [128, 2, 2, 16, NCOL], BF16)
        blt = [scoped(ph, "sblt%d" % i, [128, 2, 128], BF16) for i in range(4)]
        mlw = [scoped(ph, "mlw%d" % i, [128, 2, 2, 256], BF16) for i in range(2)]
        csw = [scoped(ph, "csw%d" % i, [128, 2, 2, 256], BF16) for i in range(2)]
        yti = [scoped(ph, "yti%d" % i, [128, 512], BF16) for i in range(3)]
        tf = [scoped(ph, "tf%d" % i, [128, 2, 16, 2], F32) for i in range(2)]
        tb = [scoped(ph, "tb%d" % i, [128, 2, 16, 2], F32) for i in range(2)]
        p.dma("sp", BUFA[0:32, 0:8192], UTOK[0:512, :].rearrange("(q i) f -> q (i f)", i=16), r=["UTOK"], w=["UQ"])
        p.dma("sp", BUFA[:, 8192:16384], UTOK[512:2560, :].rearrange("(q i) f -> q (i f)", i=16), r=["UTOK"], w=["UQ"])
        p.op("dve", lambda e: e.tensor_copy(
            out=BUFA[0:32, 16384:24576].rearrange("p (g i c) -> p g i c", g=32, i=16),
            in_=BUFA[0:32, 0:8192].rearrange("p (i g c) -> p g i c", i=16, g=32)), r=["UQ"], w=["UQ2"])
        p.op("pool", lambda e: e.tensor_copy(
            out=BUFA[:, 24576:32768].rearrange("p (g i c) -> p g i c", g=32, i=16),
            in_=BUFA[:, 8192:16384].rearrange("p (i g c) -> p g i c", i=16, g=32)), r=["UQ"], w=["UQ2"])
        n_ = 0
        for gg in range(32):
            for ch in range(2):
                pi = psr()
                psb = PS[pi][:, 0:80].bitcast(BF16)
                o_ = gg * 256 + ch * 128
                p.op("pe", lambda e, psb=psb, o_=o_: e.transpose(psb[:, 0:32], BUFA[0:32, 16384 + o_:16384 + o_ + 128], ident_b[0:32, 0:32]),
                     r=["UQ2", "ident_b"], w=[("ps", pi)])
                p.op("pe", lambda e, psb=psb, o_=o_: e.transpose(psb[:, 32:160], BUFA[:, 24576 + o_:24576 + o_ + 128], ident_b[:, :]),
                     r=["UQ2", "ident_b"], w=[("ps", pi)])
                if n_ % 2 == 0:
                    p.op("dve", lambda e, psb=psb, gg=gg, ch=ch: e.tensor_copy(out=UL(gg, ch, 0, 160), in_=psb[:, 0:160]), r=[("ps", pi)], w=["UL"])
                else:
                    p.op("act", lambda e, psb=psb, gg=gg, ch=ch: e.copy(out=UL(gg, ch, 0, 160), in_=psb[:, 0:160]), r=[("ps", pi)], w=["UL"])
                n_ += 1
        p.op("pool", lambda e: e.memset(BUFA[:, 0:20864], 0.0), w=["UQ", "UQ2", ("S", 0), ("S", 1)])
        for d in range(2):
            col = 34 if d == 0 else 162
            for reim in range(2):
                for gh in range(2):
                    p.dma("sp", Sv[gh * 64:(gh + 1) * 64, reim, d, :, col],
                          g["state_in"][l, d, reim, gh * 16:(gh + 1) * 16, :].rearrange("g n -> n g"), w=[("S", d)], slow=True)
        bi_ = 0
        for d in range(2):
            for gg in range(32):
                gh, g16 = gg // 16, gg % 16
                r0 = gh * 64
                bt = blt[bi_ % 4]; bk = ("sblt", bi_ % 4); bi_ += 1
                p.dma("sp", bt[:], SSM_BLT[l, d, gg].rearrange("c p f -> p c f"), r=["SSM_BLT"], w=[bk])
                pi = psr()
                for reim in range(2):
                    for ch in range(2):
                        p.op("pe", lambda e, pi=pi, r0=r0, reim=reim, ch=ch, bt=bt, gg=gg: e.matmul(
                            PS[pi][r0:r0 + 64, reim * 160:(reim + 1) * 160], bt[:, ch, reim * 64:(reim + 1) * 64], UL(gg, ch, 0, 160),
                            start=(ch == 0), stop=(ch == 1)), r=[bk, "UL"], w=[("ps", pi)])
                for (b0, Q, q0) in SEQ:
                    c0 = b0 + (1 - d)
                    p.op("dve", lambda e, pi=pi, r0=r0, d=d, g16=g16, c0=c0, Q=Q, q0=q0: e.tensor_copy(
                        out=Sv[r0:r0 + 64, :, d, g16, c0:c0 + Q],
                        in_=PS[pi][r0:r0 + 64, 0:320].rearrange("p (r q) -> p r q", r=2)[:, :, q0:q0 + Q]),
                        r=[("ps", pi)], w=[("S", d)])
        if "SSMD" in k.debug and l == 0:
            d1 = nc.dram_tensor("D_SIN", [128, 10432], F32, kind="ExternalOutput").ap()
            p.dma("sp", d1, BUFA[:, 0:20864].bitcast(F32), r=[("S", 0), ("S", 1)], w=["D_SIN"])
            d2 = nc.dram_tensor("D_UL", [128, 10240], BF16, kind="ExternalOutput").ap()
            p.dma("sp", d2, BUFB[:, UL0:UL0 + 10240], r=["UL"], w=["D_UL"])
        for d, eng, tmp in ((0, "dve", tf), (1, "pool", tb)):
            key = ("S", d)
            fs = slice(d * 16, (d + 1) * 16)
            ca4 = CAA[:, l, :, fs].unsqueeze(3).to_broadcast([128, 2, 16, 2])
            cb4 = CAB[:, l, :, fs].unsqueeze(3).to_broadcast([128, 2, 16, 2])
            ca3 = CAA[:, l, :, fs]
            cb3 = CAB[:, l, :, fs]
            t1, t2 = tmp
            tk = "tmp%d" % d
            steps = range(16) if d == 0 else range(15, -1, -1)
            for q in steps:
                pc = q if d == 0 else q + 1
                ncl = q + 1 if d == 0 else q
                prev = Sv[:, :, d, :, pc:pc + 18:17]
                prsw = Sv[:, ::-1, d, :, pc:pc + 18:17]
                new = Sv[:, :, d, :, ncl:ncl + 18:17]
                p.op(eng, lambda e, prev=prev, t1=t1, ca4=ca4: e.tensor_tensor(out=t1[:], in0=prev, in1=ca4, op=ALU.mult), r=[key], w=[tk + "a"])
                p.op(eng, lambda e, prsw=prsw, t2=t2, cb4=cb4: e.tensor_tensor(out=t2[:], in0=prsw, in1=cb4, op=ALU.mult), r=[key], w=[tk + "b"])
                p.op(eng, lambda e, t1=t1, t2=t2: e.tensor_tensor(out=t1[:], in0=t1[:], in1=t2[:], op=ALU.add), r=[tk + "a", tk + "b"], w=[tk + "a"])
                p.op(eng, lambda e, new=new, t1=t1: e.tensor_tensor(out=new, in0=new, in1=t1[:], op=ALU.add), r=[tk + "a", key], w=[key])
            steps = range(128) if d == 0 else range(127, -1, -1)
            for q in steps:
                pc = 34 + (q if d == 0 else q + 1)
                ncl = 34 + (q + 1 if d == 0 else q)
                prev = Sv[:, :, d, :, pc]
                prsw = Sv[:, ::-1, d, :, pc]
                new = Sv[:, :, d, :, ncl]
                p.op(eng, lambda e, prev=prev, t1=t1, ca3=ca3: e.tensor_tensor(out=t1[:, :, :, 0], in0=prev, in1=ca3, op=ALU.mult), r=[key], w=[tk + "a"])
                p.op(eng, lambda e, prsw=prsw, t2=t2, cb3=cb3: e.tensor_tensor(out=t2[:, :, :, 0], in0=prsw, in1=cb3, op=ALU.mult), r=[key], w=[tk + "b"])
                p.op(eng, lambda e, t1=t1, t2=t2: e.tensor_tensor(out=t1[:, :, :, 0], in0=t1[:, :, :, 0], in1=t2[:, :, :, 0], op=ALU.add), r=[tk + "a", tk + "b"], w=[tk + "a"])
                p.op(eng, lambda e, new=new, t1=t1: e.tensor_tensor(out=new, in0=new, in1=t1[:, :, :, 0], op=ALU.add), r=[tk + "a", key], w=[key])
        if "SSMD" in k.debug and l == 0:
            d4 = nc.dram_tensor("D_CAA", [128, DEPTH, 2, 32], F32, kind="ExternalOutput").ap()
            p.dma("sp", d4, CAA[:], w=["D_CAA"])
            d5 = nc.dram_tensor("D_CAB", [128, DEPTH, 2, 32], F32, kind="ExternalOutput").ap()
            p.dma("sp", d5, CAB[:], w=["D_CAB"])
            d3 = nc.dram_tensor("D_S", [128, 10432], F32, kind="ExternalOutput").ap()
            p.dma("sp", d3, BUFA[:, 0:20864].bitcast(F32), r=[("S", 0), ("S", 1)], w=["D_S"])
        for b_ in range(2):
            b0 = SEQ[b_][0]
            for d in range(2):
                col = b0 + 16 if d == 0 else b0
                for reim in range(2):
                    for gh in range(2):
                        p.dma("sp", g["o_ssm"][b_, l, d, reim, gh * 16:(gh + 1) * 16, :].rearrange("g n -> n g"),
                              Sv[gh * 64:(gh + 1) * 64, reim, d, :, col], r=[("S", d)], w=["o_ssm"], slow=True)
        p.op("act", lambda e: e.copy(out=Sbf[:].rearrange("p r d g c -> p (r d g c)"), in_=BUFA[:, 0:20864].bitcast(F32)),
             r=[("S", 0), ("S", 1)], w=["Sbf"])
        p.barrier()
        n_ = 0
        for gg in range(32):
            gh, g16 = gg // 16, gg % 16
            r0 = gh * 64
            mw = mlw[gg % 2]; mk = ("mlw", gg % 2)
            cw = csw[gg % 2]; ck = ("csw", gg % 2)
            for d in range(2):
                p.dma("sp", mw[:, d], SSM_ML[l, d, gg].rearrange("c p f -> p c f"), r=["SSM_ML"], w=[mk])
                p.dma("sp", cw[r0:r0 + 64, d], SSM_CS[l, d, gg], r=["SSM_CS"], w=[ck])
            for (rows, q0, ybase, sc0) in ((16, 0, 0, 0), (16, 16, 16384, 17), (128, 32, 8192, 34)):
                pi = psr()
                mms = []
                for d in range(2):
                    for ch in range(2):
                        mms.append((UL(gg, ch, q0, rows), mw[:, d, ch, :], [mk, "UL"]))
                    for reim in range(2):
                        cc0 = sc0 + d
                        lh = Sbf[r0:r0 + 64, reim, d, g16, cc0:cc0 + rows]
                        mms.append((lh, cw[r0:r0 + 64, d, reim, :], [ck, "Sbf"]))
                for mi, (lh, rh, rk) in enumerate(mms):
                    p.op("pe", lambda e, pi=pi, lh=lh, rh=rh, mi=mi, rows=rows: e.matmul(
                        PS[pi][0:rows, 0:256], lh, rh, start=(mi == 0), stop=(mi == len(mms) - 1)), r=rk, w=[("ps", pi)])
                dst = BUFA[0:rows, ybase:ybase + 8192].rearrange("p (i g c) -> p i g c", i=16, g=32)[:, :, gg, :]
                src = PS[pi][0:rows, 0:256].rearrange("p (i c) -> p i c", i=16)
                if n_ % 2 == 0:
                    p.op("dve", lambda e, dst=dst, src=src: e.tensor_copy(out=dst, in_=src), r=[("ps", pi)], w=["YQ"])
                else:
                    p.op("act", lambda e, dst=dst, src=src: e.copy(out=dst, in_=src), r=[("ps", pi)], w=["YQ"])
                n_ += 1
        p.dma("sp", YTOK[0:256, :].rearrange("(q i) f -> q (i f)", i=16), BUFA[0:16, 0:8192], r=["YQ"], w=["YTOK"])
        p.dma("sp", YTOK[256:512, :].rearrange("(q i) f -> q (i f)", i=16), BUFA[0:16, 16384:24576], r=["YQ"], w=["YTOK"])
        p.dma("sp", YTOK[512:2560, :].rearrange("(q i) f -> q (i f)", i=16), BUFA[:, 8192:16384], r=["YQ"], w=["YTOK"])
        for ti in range(T // 128):
            yt = yti[ti % 3]; yk = ("yti", ti % 3)
            p.dma("sp", yt[:], YTOK[ti * 128:(ti + 1) * 128, :], r=["YTOK"], w=[yk])
            pi = psr()
            psb = PS[pi][:, 0:256].bitcast(BF16)
            for c in range(4):
                p.op("pe", lambda e, psb=psb, c=c, yt=yt: e.transpose(psb[:, c * 128:(c + 1) * 128], yt[:, c * 128:(c + 1) * 128], ident_b[:, :]),
                     r=[yk, "ident_b"], w=[("ps", pi)])
            dst = BUFB[:, 8 * T:12 * T].rearrange("p (c t) -> p c t", c=4)[:, :, ti * 128:(ti + 1) * 128]
            src = psb.rearrange("p (c t) -> p c t", c=4)
            if ti % 2 == 0:
                p.op("dve", lambda e, dst=dst, src=src: e.tensor_copy(out=dst, in_=src), r=[("ps", pi)], w=["BI", "UL"])
            else:
                p.op("act", lambda e, dst=dst, src=src: e.copy(out=dst, in_=src), r=[("ps", pi)], w=["BI", "UL"])
    p.barrier()


def tail(k, g, l, scoped, norm):
    nc = k.nc
    p = k.p
    PS = g["PS"]; A = g["A"]
    BUFA = g["BUFA"]; BUFB = g["BUFB"]
    modv = g["modv"]; A2 = g["A2"]
    XT = g["XT"]; GS = g["GS"]; MT = g["MT"]; AT = g["AT"]
    psr = rot(list(range(8)))

    def mod_j(tt):
        return 0 if tt == 0 else 1

    def kcp(src):
        return src.rearrange("(kc p) n -> p kc n", p=128)

    with ExitStack() as ph:
        wm = [scoped(ph, "wm%d" % i, [128, 20, 256], BF16) for i in range(2)]
        sgt = scoped(ph, "sgt", [128, 512], F32)
        acc = [scoped(ph, "macc%d" % i, [128, 512], F32) for i in range(2)]
        tmp = [scoped(ph, "mtmp%d" % i, [128, 512], F32) for i in range(3)]
        mstg = [scoped(ph, "mstg%d" % i, [128, T], BF16) for i in range(2)]
        ti_ = [0]

        def T_():
            i = ti_[0] % 3
            ti_[0] += 1
            return tmp[i], ("mtmp", i)
        stages = [BUFA[:, 20480 + i * 10240: 20480 + (i + 1) * 10240].bitcast(F32).rearrange("p (k n) -> p k n", k=20) for i in range(2)]
        pieces = []
        for fg in range(8):
            c0 = fg * 256
            pcs = []
            for bi, src in enumerate((g["w_mla_o"][l][:, c0:c0 + 256], g["w_conv_o"][l][:, c0:c0 + 256],
                                      g["w_pool_o"][l][:, c0:c0 + 256], g["w_glu"][l][:, c0:c0 + 256],
                                      g["w_glu"][l][:, 2048 + c0:2048 + c0 + 256])):
                pcs.append((src, bi * 4, 4, 0, 256))
            pieces.append(pcs)

        def compute(fg, w, wk):
            for f2 in range(2):
                fc = fg * 2 + f2
                gb = fc % 2
                gk = ("gbuf", gb)
                for i in range(4):
                    p.dma("sp", BUFA[:, (gb * 4 + i) * T:(gb * 4 + i + 1) * T], GS[i * 16 + fc], r=["GS"], w=[gk])

                def G(i, tt, gb=gb):
                    return BUFA[:, (gb * 4 + i) * T + tt * 512:(gb * 4 + i) * T + (tt + 1) * 512]
                ms = mstg[fc % 2]; mk = ("mstg", fc % 2)
                cs = f2 * 128
                for tt in range(NT):
                    banks = [psr() for _ in range(5)]
                    specs = [(0, 0), (4, 4), (12, 8), (16, 8), (8, 12)]
                    for bnk, (wi_, bch) in zip(banks, specs):
                        for kc in range(4):
                            p.op("pe", lambda e, bnk=bnk, wi_=wi_, bch=bch, kc=kc, tt=tt, w=w, cs=cs: e.matmul(
                                PS[bnk][:, :], w[:, wi_ + kc, cs:cs + 128], A(BUFB, bch + kc, tt * 512, 512),
                                start=(kc == 0), stop=(kc == 3)), r=[wk, "BI"], w=[("ps", bnk)])
                    pa, pb, pga, pgg, pd = banks
                    p.op("act", lambda e, pgg=pgg: e.activation(out=sgt[:], in_=PS[pgg][:, :], func=AF.Sigmoid),
                         r=[("ps", pgg)], w=["sgt"])
                    ac = acc[tt % 2]; ak = ("macc", tt % 2)
                    p.op("dve", lambda e, ac=ac, pa=pa, tt=tt, G=G: e.tensor_tensor(out=ac[:], in0=PS[pa][:, :], in1=G(0, tt), op=ALU.mult),
                         r=[("ps", pa), gk], w=[ak])
                    t1, t1k = T_()
                    p.op("dve", lambda e, t1=t1, pb=pb, tt=tt, G=G: e.tensor_tensor(out=t1[:], in0=PS[pb][:, :], in1=G(1, tt), op=ALU.mult),
                         r=[("ps", pb), gk], w=[t1k])
                    p.op("pool", lambda e, ac=ac, t1=t1: e.tensor_tensor(out=ac[:], in0=ac[:], in1=t1[:], op=ALU.add), r=[ak, t1k], w=[ak])
                    t2, t2k = T_()
                    p.op("dve", lambda e, t2=t2, pga=pga: e.tensor_tensor(out=t2[:], in0=PS[pga][:, :], in1=sgt[:], op=ALU.mult),
                         r=[("ps", pga), "sgt"], w=[t2k])
                    p.op("pool", lambda e, t2=t2, tt=tt, G=G: e.tensor_tensor(out=t2[:], in0=t2[:], in1=G(2, tt), op=ALU.mult), r=[t2k, gk], w=[t2k])
                    p.op("pool", lambda e, ac=ac, t2=t2: e.tensor_tensor(out=ac[:], in0=ac[:], in1=t2[:], op=ALU.add), r=[ak, t2k], w=[ak])
                    t3, t3k = T_()
                    p.op("dve", lambda e, t3=t3, pd=pd, tt=tt, G=G: e.tensor_tensor(out=t3[:], in0=PS[pd][:, :], in1=G(3, tt), op=ALU.mult),
                         r=[("ps", pd), gk], w=[t3k])
                    p.op("pool", lambda e, ac=ac, t3=t3, ms=ms, tt=tt: e.tensor_tensor(out=ms[:, tt * 512:(tt + 1) * 512], in0=ac[:], in1=t3[:], op=ALU.add),
                         r=[ak, t3k], w=[mk])
                p.dma("sp", MT[fc], ms[:], r=[mk], w=["MT"])
        WStream(p, "wm", stages, wm, cast_engs=("act",)).run(pieces, compute)
    p.barrier()
    if "DBG_MT" in k.debug:
        return
    with ExitStack() as ph:
        wbs = [scoped(ph, "wo%d" % i, [128, 16, 256], BF16) for i in range(2)]
        xs = [scoped(ph, "xs%d" % i, [128, T], F32) for i in range(2)]
        for c in range(16):
            p.dma("sp", A(BUFA, c, 0, T), MT[c], r=["MT"], w=["BUF"])
        stages = [BUFB[:, i * 8192:(i + 1) * 8192].bitcast(F32).rearrange("p (k n) -> p k n", k=16) for i in range(4)]
        pieces = [[(g["w_o"][l][:, fg * 256:(fg + 1) * 256], 0, 16, 0, 256)] for fg in range(8)]

        def compute(fg, wb, wk):
            for f2 in range(2):
                fc = fg * 2 + f2
                x = xs[fc % 2]; xk = ("xs", fc % 2)
                p.dma("sp", x[:], XT[0][fc], r=["XT0"], w=[xk])
                for tt in range(NT):
                    pi = psr()
                    j = mod_j(tt)
                    for kc in range(16):
                        p.op("pe", lambda e, pi=pi, wb=wb, kc=kc, tt=tt, f2=f2: e.matmul(
                            PS[pi][:, :], wb[:, kc, f2 * 128:(f2 + 1) * 128], A(BUFA, kc, tt * 512, 512),
                            start=(kc == 0), stop=(kc == 15)), r=[wk, "BUF"], w=[("ps", pi)])
                    p.op("dve", lambda e, pi=pi, x=x, tt=tt, fc=fc, j=j: e.scalar_tensor_tensor(
                        out=x[:, tt * 512:(tt + 1) * 512], in0=PS[pi][:, :], scalar=modv[:, l, 32 + fc, j:j + 1],
                        in1=x[:, tt * 512:(tt + 1) * 512], op0=ALU.mult, op1=ALU.add), r=[("ps", pi), xk, "modv"], w=[xk])
                p.dma("sp", XT[1][fc], x[:], r=[xk], w=["XT1"])
        WStream(p, "wo", stages, wbs).run(pieces, compute)
    p.barrier()
    if "DBG_X1" in k.debug:
        return
    norm(XT[1], "XT1", A2, l, 48, BUFB)
    with ExitStack() as ph:
        wbs = [scoped(ph, "w1_%d" % i, [128, 16, 256], BF16) for i in range(3)]
        stg = [scoped(ph, "astg%d" % i, [128, T], BF16) for i in range(2)]
        rl = [scoped(ph, "rl%d" % i, [128, 512], F32) for i in range(2)]
        ri = [0]
        stages = [BUFA[:, i * 8192:(i + 1) * 8192].bitcast(F32).rearrange("p (k n) -> p k n", k=16) for i in range(4)]
        pieces = [[(g["w_mlp1"][l][:, hg * 256:(hg + 1) * 256], 0, 16, 0, 256)] for hg in range(32)]

        def compute(hg, wb, wk):
            for f2 in range(2):
                hc = hg * 2 + f2
                sb_ = stg[hc % 2]; sk = ("astg", hc % 2)
                for tt in range(NT):
                    pi = psr()
                    for kc in range(16):
                        p.op("pe", lambda e, pi=pi, wb=wb, kc=kc, tt=tt, f2=f2: e.matmul(
                            PS[pi][:, :], wb[:, kc, f2 * 128:(f2 + 1) * 128], A(BUFB, kc, tt * 512, 512),
                            start=(kc == 0), stop=(kc == 15)), r=[wk, "BUF"], w=[("ps", pi)])
                    r_ = rl[ri[0] % 2]; rk = ("rl", ri[0] % 2); ri[0] += 1
                    p.op("act", lambda e, pi=pi, r_=r_: e.activation(out=r_[:], in_=PS[pi][:, :], func=AF.Relu), r=[("ps", pi)], w=[rk])
                    p.op("dve", lambda e, r_=r_, sb_=sb_, tt=tt: e.tensor_tensor(out=sb_[:, tt * 512:(tt + 1) * 512], in0=r_[:], in1=r_[:], op=ALU.mult),
                         r=[rk], w=[sk])
                p.dma("sp", AT[hc], sb_[:], r=[sk], w=["AT"])
        WStream(p, "w1", stages, wbs, cast_engs=("pool", "act")).run(pieces, compute)
    p.barrier()
    W2B = g["W2B"]
    with ExitStack() as ph:
        x1t = [scoped(ph, "x1t%d" % i, [128, 512], F32) for i in range(3)]
        xi = [0]
        wbf = [BUFB[:, 32768:40960].rearrange("p (k n) -> p k n", k=64), BUFA[:, 32768:40960].rearrange("p (k n) -> p k n", k=64)]
        stg2 = [BUFB[:, i * 16384:(i + 1) * 16384].bitcast(F32).rearrange("p (k n) -> p k n", k=64) for i in range(2)]
        for tt in range(NT):
            j = mod_j(tt)
            p.dma("sp", BUFA[:, 0:64 * 512].rearrange("p (c t) -> p c t", t=512),
                  AT[:, :, tt * 512:(tt + 1) * 512].rearrange("c p t -> p c t"), r=["AT"], w=["atile"])

            def compute(fc, wv, wk, tt=tt, j=j):
                if tt == 0:
                    p.dma("sp", W2B[fc], wv.rearrange("p k n -> p (k n)"), r=[wk], w=[("W2B", fc)])
                xt_ = x1t[xi[0] % 3]; xk = ("x1t", xi[0] % 3); xi[0] += 1
                p.dma("sp", xt_[:], XT[1][fc][:, tt * 512:(tt + 1) * 512], r=["XT1"], w=[xk])
                pi = psr()
                for kc in range(64):
                    p.op("pe", lambda e, pi=pi, wv=wv, kc=kc: e.matmul(
                        PS[pi][:, :], wv[:, kc, :], BUFA[:, kc * 512:(kc + 1) * 512],
                        start=(kc == 0), stop=(kc == 63)), r=[wk, "atile"], w=[("ps", pi)])
                p.op("dve", lambda e, pi=pi, xt_=xt_, fc=fc, j=j: e.scalar_tensor_tensor(
                    out=xt_[:], in0=PS[pi][:, :], scalar=modv[:, l, 80 + fc, j:j + 1], in1=xt_[:],
                    op0=ALU.mult, op1=ALU.add), r=[("ps", pi), xk, "modv"], w=[xk])
                p.dma("sp", XT[0][fc][:, tt * 512:(tt + 1) * 512], xt_[:], r=[xk], w=["XT0"])
            if tt == 0:
                pieces = [[(g["w_mlp2"][l][:, fc * 128:(fc + 1) * 128], 0, 64, 0, 128)] for fc in range(16)]
                WStream(p, "w2", stg2, wbf, cast_engs=("act", "pool")).run(pieces, compute)
            else:
                def ld(fc):
                    p.dma("sp", wbf[fc % 2].rearrange("p k n -> p (k n)"), W2B[fc], r=[("W2B", fc)], w=[("w2wb", fc % 2)])
                ld(0)
                for fc in range(16):
                    if fc + 1 < 16:
                        ld(fc + 1)
                    compute(fc, wbf[fc % 2], ("w2wb", fc % 2))
    p.barrier()


def ssm_gen(k, g, l):
    nc = k.nc
    p = k.p
    PS = g["PS"]
    ident_f = g["ident_f"]
    CAA = g["CAA"]; CAB = g["CAB"]
    TWO_PI = 2.0 * math.pi
    uid = [0]
    with ExitStack() as ph:
        def t_(name, shape, dt=F32):
            uid[0] += 1
            return ph.enter_context(nc.sbuf_tensor("g%d_%d_%s" % (l, uid[0], name), list(shape), dt))
        lr = t_("lr", [128, 32]); li = t_("li", [128, 32]); ls = t_("ls", [128, 32])
        Bre = t_("Bre", [128, 32, 16]); Bim = t_("Bim", [128, 32, 16])
        Cre = t_("Cre", [128, 32, 16]); Cim = t_("Cim", [128, 32, 16])
        mF = t_("mF", [128, 2, 256]); mB = t_("mB", [128, 2, 256]); dI = t_("dI", [128, 2, 256])
        dvec = t_("dvec", [128, DEPTH, 32])
        for d in range(2):
            for gh in range(2):
                rows = slice(gh * 64, (gh + 1) * 64)
                gs = slice(gh * 16, (gh + 1) * 16)
                fs = slice(d * 16, (d + 1) * 16)
                p.dma("sp", lr[rows, fs], g["ssm_lam_re"][l, d, gs, :].rearrange("g n -> n g"), w=["lr"], slow=True)
                p.dma("sp", li[rows, fs], g["ssm_lam_im"][l, d, gs, :].rearrange("g n -> n g"), w=["li"], slow=True)
                p.dma("sp", ls[rows, fs], g["ssm_log_step"][l, d, gs].partition_broadcast(64), w=["ls"])
                p.dma("sp", Bre[rows, fs, :], g["ssm_b_re"][l, d, gs, :, :].rearrange("g n c -> n g c"), w=["Bre"])
                p.dma("sp", Bim[rows, fs, :], g["ssm_b_im"][l, d, gs, :, :].rearrange("g n c -> n g c"), w=["Bim"])
                for g16 in range(16):
                    gg = gh * 16 + g16
                    p.dma("sp", Cre[rows, d * 16 + g16, :], g["ssm_c_re"][l, d, gg, :, :].rearrange("c n -> n c"), w=["Cre"], slow=True)
                    p.dma("sp", Cim[rows, d * 16 + g16, :], g["ssm_c_im"][l, d, gg, :, :].rearrange("c n -> n c"), w=["Cim"], slow=True)
        p.dma("sp", mF[:], g["c_mF"].rearrange("c p f -> p c f"), w=["mF"])
        p.dma("sp", mB[:], g["c_mB"].rearrange("c p f -> p c f"), w=["mB"])
        p.dma("sp", dI[:], g["c_dI"].rearrange("c p f -> p c f"), w=["dI"])
        p.dma("sp", dvec[:], g["c_dvec"][:, :, :], w=["dvec"])

        K_ = ["gen"]

        def V(fn):
            p.op("dve", fn, r=K_ + ["lr", "li", "ls", "Bre", "Bim", "Cre", "Cim"], w=K_)

        def ACT(fn):
            p.op("act", fn, r=K_ + ["lr", "li", "ls", "Bre", "Bim", "Cre", "Cim"], w=K_)

        def tt(out, a, b, op):
            V(lambda e: e.tensor_tensor(out=out, in0=a, in1=b, op=op))

        step = t_("step", [128, 32]); a_ = t_("a", [128, 32]); th = t_("th", [128, 32])
        mag = t_("mag", [128, 32]); imag = t_("imag", [128, 32])
        r_ = t_("r", [128, 32]); ri = t_("ri", [128, 32], mybir.dt.int32); rf = t_("rf", [128, 32])
        f_ = t_("f", [128, 32]); fc = t_("fc", [128, 32]); m_ = t_("m", [128, 32])
        sinv = t_("sinv", [128, 32]); cosv = t_("cosv", [128, 32])
        ACT(lambda e: e.activation(out=step[:], in_=ls[:], func=AF.Exp))
        tt(a_[:], lr[:], step[:], ALU.mult)
        tt(th[:], li[:], step[:], ALU.mult)
        ACT(lambda e: e.activation(out=mag[:], in_=a_[:], func=AF.Exp))
        ACT(lambda e: e.activation(out=imag[:], in_=a_[:], func=AF.Exp, scale=-1.0))
        V(lambda e: e.tensor_scalar(out=r_[:], in0=th[:], scalar1=1.0 / TWO_PI, scalar2=None, op0=ALU.mult))
        V(lambda e: e.tensor_copy(out=ri[:], in_=r_[:]))
        V(lambda e: e.tensor_copy(out=rf[:], in_=ri[:]))
        tt(f_[:], r_[:], rf[:], ALU.subtract)
        V(lambda e: e.tensor_scalar(out=fc[:], in0=f_[:], scalar1=0.25, scalar2=None, op0=ALU.add))
        V(lambda e: e.tensor_scalar(out=m_[:], in0=fc[:], scalar1=0.5, scalar2=None, op0=ALU.is_ge))
        tt(fc[:], fc[:], m_[:], ALU.subtract)
        ACT(lambda e: e.activation(out=sinv[:], in_=f_[:], func=AF.Sin, scale=TWO_PI))
        ACT(lambda e: e.activation(out=cosv[:], in_=fc[:], func=AF.Sin, scale=TWO_PI))
        PPr = t_("PPr", [128, 32, 17]); PPi = t_("PPi", [128, 32, 17])
        PNr = t_("PNr", [128, 32, 17]); PNi = t_("PNi", [128, 32, 17])
        tA = t_("tA", [128, 16 * 256]); tB = t_("tB", [128, 16 * 256])

        def cmul(outr, outi, xr, xi, yr, yi, shape):
            n = 1
            for s_ in shape[1:]:
                n *= s_
            pat = {2: None, 3: "p (a b) -> p a b", 4: "p (a b c) -> p a b c"}[len(shape)]

            def view(t):
                v = t[:, 0:n]
                if len(shape) == 3:
                    return v.rearrange(pat, a=shape[1])
                if len(shape) == 4:
                    return v.rearrange(pat, a=shape[1], b=shape[2])
                return v
            ta = view(tA); tb = view(tB)
            tt(ta, xr, yr, ALU.mult)
            tt(tb, xi, yi, ALU.mult)
            tt(outr, ta, tb, ALU.subtract)
            tt(ta, xr, yi, ALU.mult)
            tt(tb, xi, yr, ALU.mult)
            tt(outi, ta, tb, ALU.add)

        for (Pr, Pi, br, bi_, sgn) in ((PPr, PPi, mag, mag, 1.0), (PNr, PNi, imag, imag, -1.0)):
            V(lambda e, Pr=Pr: e.memset(Pr[:, :, 0:1], 1.0))
            V(lambda e, Pi=Pi: e.memset(Pi[:, :, 0:1], 0.0))
            tt(Pr[:, :, 1], br[:], cosv[:], ALU.mult)
            tt(Pi[:, :, 1], bi_[:], sinv[:], ALU.mult)
            if sgn < 0:
                V(lambda e, Pi=Pi: e.tensor_scalar(out=Pi[:, :, 1], in0=Pi[:, :, 1], scalar1=-1.0, scalar2=None, op0=ALU.mult))
            for (o0, n_, s0, k0) in ((2, 1, 1, 1), (3, 2, 1, 2), (5, 4, 1, 4), (9, 8, 1, 8)):
                cmul(Pr[:, :, o0:o0 + n_], Pi[:, :, o0:o0 + n_], Pr[:, :, s0:s0 + n_], Pi[:, :, s0:s0 + n_],
                     Pr[:, :, k0:k0 + 1].to_broadcast([128, 32, n_]), Pi[:, :, k0:k0 + 1].to_broadcast([128, 32, n_]), [128, 32, n_])
        V(lambda e: e.tensor_copy(out=CAA[:, l, 0, :], in_=PPr[:, :, 16]))
        V(lambda e: e.tensor_copy(out=CAA[:, l, 1, :], in_=PPr[:, :, 16]))
        V(lambda e: e.tensor_copy(out=CAB[:, l, 1, :], in_=PPi[:, :, 16]))
        V(lambda e: e.tensor_scalar(out=CAB[:, l, 0, :], in0=PPi[:, :, 16], scalar1=-1.0, scalar2=None, op0=ALU.mult))
        nre = t_("nre", [128, 32]); den = t_("den", [128, 32]); cre = t_("cre", [128, 32]); cim = t_("cim", [128, 32])
        t1 = t_("t1", [128, 32]); t2 = t_("t2", [128, 32])
        V(lambda e: e.tensor_scalar(out=nre[:], in0=PPr[:, :, 1], scalar1=-1.0, scalar2=None, op0=ALU.add))
        tt(t1[:], lr[:], lr[:], ALU.mult)
        tt(t2[:], li[:], li[:], ALU.mult)
        tt(den[:], t1[:], t2[:], ALU.add)
        V(lambda e: e.reciprocal(out=den[:], in_=den[:]))
        tt(t1[:], nre[:], lr[:], ALU.mult)
        tt(t2[:], PPi[:, :, 1], li[:], ALU.mult)
        tt(cre[:], t1[:], t2[:], ALU.add)
        tt(cre[:], cre[:], den[:], ALU.mult)
        tt(t1[:], PPi[:, :, 1], lr[:], ALU.mult)
        tt(t2[:], nre[:], li[:], ALU.mult)
        tt(cim[:], t1[:], t2[:], ALU.subtract)
        tt(cim[:], cim[:], den[:], ALU.mult)
        BBr = t_("BBr", [128, 32, 16]); BBi = t_("BBi", [128, 32, 16])
        cmul(BBr[:], BBi[:], cre[:].unsqueeze(2).to_broadcast([128, 32, 16]), cim[:].unsqueeze(2).to_broadcast([128, 32, 16]),
             Bre[:], Bim[:], [128, 32, 16])
        PCr = t_("PCr", [128, 32, 16]); PCi = t_("PCi", [128, 32, 16])
        PQr = t_("PQr", [128, 32, 16]); PQi = t_("PQi", [128, 32, 16])
        V(lambda e: e.tensor_copy(out=PCr[:, 0:16, :], in_=PPr[:, 0:16, 1:17]))
        V(lambda e: e.tensor_copy(out=PCi[:, 0:16, :], in_=PPi[:, 0:16, 1:17]))
        V(lambda e: e.tensor_copy(out=PQr[:, 0:16, :], in_=PNr[:, 0:16, 1:17]))
        V(lambda e: e.tensor_copy(out=PQi[:, 0:16, :], in_=PNi[:, 0:16, 1:17]))
        cmul(PCr[:, 16:32, :], PCi[:, 16:32, :], PNr[:, 16:32, 0:16], PNi[:, 16:32, 0:16],
             PPr[:, 16:32, 16:17].to_broadcast([128, 16, 16]), PPi[:, 16:32, 16:17].to_broadcast([128, 16, 16]), [128, 16, 16])
        cmul(PQr[:, 16:32, :], PQi[:, 16:32, :], PPr[:, 16:32, 0:16], PPi[:, 16:32, 0:16],
             PNr[:, 16:32, 16:17].to_broadcast([128, 16, 16]), PNi[:, 16:32, 16:17].to_broadcast([128, 16, 16]), [128, 16, 16])
        if "GEN" in k.debug and l == 0:
            for nm, t, shp in (("lr", lr, [128, 32]), ("li", li, [128, 32]), ("ls", ls, [128, 32]), ("step", step, [128, 32]),
                               ("th", th, [128, 32]), ("f", f_, [128, 32]), ("fc", fc, [128, 32]),
                               ("sinv", sinv, [128, 32]), ("cosv", cosv, [128, 32]), ("mag", mag, [128, 32]),
                               ("PPr", PPr, [128, 32, 17]), ("PPi", PPi, [128, 32, 17]), ("PNr", PNr, [128, 32, 17]), ("PNi", PNi, [128, 32, 17]),
                               ("BBr", BBr, [128, 32, 16]), ("BBi", BBi, [128, 32, 16]), ("Cre", Cre, [128, 32, 16]), ("Bre", Bre, [128, 32, 16]),
                               ("PCr", PCr, [128, 32, 16]), ("PCi", PCi, [128, 32, 16]), ("PQr", PQr, [128, 32, 16]), ("PQi", PQi, [128, 32, 16])):
                dt_ = nc.dram_tensor("G_" + nm, shp, F32, kind="ExternalOutput").ap()
                p.dma("sp", dt_, t[:], r=K_ + ["lr", "li", "ls", "Bre", "Bim", "Cre", "Cim"], w=["GDBG"])
        Xr = t_("Xr", [128, 16, 256]); Xi = t_("Xi", [128, 16, 256])
        Qr = t_("Qr", [128, 16, 256]); Qi = t_("Qi", [128, 16, 256])
        BLr = t_("BLr", [128, 16, 256]); BLi = t_("BLi", [128, 16, 256])
        CSb = t_("CSb", [128, 16, 2, 256], BF16)
        mlt = [t_("mlt%d" % i, [128, 256], BF16) for i in range(2)]
        mtmp = t_("mtmp", [128, 256])
        blt = [t_("blt%d" % i, [128, 128], BF16) for i in range(2)]
        cnt = [0]
        sh4 = [128, 16, 16, 16]

        def v4(t):
            return t[:].rearrange("p g (i c) -> p g i c", i=16)
        for d in range(2):
            fs = slice(d * 16, (d + 1) * 16)
            cmul(v4(Xr), v4(Xi), PCr[:, fs, :].unsqueeze(3).to_broadcast(sh4), PCi[:, fs, :].unsqueeze(3).to_broadcast(sh4),
                 Cre[:, fs, :].unsqueeze(2).to_broadcast(sh4), Cim[:, fs, :].unsqueeze(2).to_broadcast(sh4), sh4)
            V(lambda e: e.tensor_scalar(out=Xi[:], in0=Xi[:], scalar1=-1.0, scalar2=None, op0=ALU.mult))
            cmul(v4(Qr), v4(Qi), PQr[:, fs, :].unsqueeze(3).to_broadcast(sh4), PQi[:, fs, :].unsqueeze(3).to_broadcast(sh4),
                 BBr[:, fs, :].unsqueeze(2).to_broadcast(sh4), BBi[:, fs, :].unsqueeze(2).to_broadcast(sh4), sh4)
            cmul(BLr[:], BLi[:], Qr[:], Qi[:], PPr[:, fs, 16:17].to_broadcast([128, 16, 256]), PPi[:, fs, 16:17].to_broadcast([128, 16, 256]),
                 [128, 16, 256])
            ACT(lambda e: e.copy(out=CSb[:, :, 0, :], in_=Xr[:]))
            ACT(lambda e: e.copy(out=CSb[:, :, 1, :], in_=Xi[:]))
            for gh in range(2):
                p.dma("sp", g["SSM_CS"][l, d, gh * 16:(gh + 1) * 16].rearrange("g n r f -> n g (r f)"),
                      CSb[gh * 64:(gh + 1) * 64].rearrange("p g r f -> p g (r f)"), r=K_, w=["SSM_CS"])
            for gg in range(32):
                gh, g16 = gg // 16, gg % 16
                r0 = gh * 64
                for ch in range(2):
                    cs = slice(ch * 128, (ch + 1) * 128)
                    i_ = cnt[0] % 2
                    cnt[0] += 1
                    pi = 4 + (cnt[0] % 2) * 2
                    p.op("pe", lambda e, pi=pi, g16=g16, cs=cs, r0=r0: e.matmul(
                        PS[pi][:, 0:256], Qr[r0:r0 + 64, g16, cs], Xr[r0:r0 + 64, g16, :], start=True, stop=False), r=K_, w=[("ps", pi)])
                    p.op("pe", lambda e, pi=pi, g16=g16, cs=cs, r0=r0: e.matmul(
                        PS[pi][:, 0:256], Qi[r0:r0 + 64, g16, cs], Xi[r0:r0 + 64, g16, :], start=False, stop=True), r=K_, w=[("ps", pi)])
                    msk = (mF if d == 0 else mB)
                    ml = mlt[i_]; mk = ("mlt", i_)
                    if d == 0:
                        p.op("dve", lambda e, pi=pi, ch=ch, msk=msk: e.tensor_tensor(out=mtmp[:], in0=PS[pi][:, 0:256], in1=msk[:, ch, :], op=ALU.mult),
                             r=[("ps", pi), "mF", "mB"], w=["mtmp"])
                        p.op("dve", lambda e, ch=ch, gg=gg, ml=ml: e.scalar_tensor_tensor(
                            out=ml[:], in0=dI[:, ch, :], scalar=dvec[:, l, gg:gg + 1], in1=mtmp[:], op0=ALU.mult, op1=ALU.add),
                            r=["mtmp", "dI", "dvec"], w=[mk])
                    else:
                        p.op("dve", lambda e, pi=pi, ch=ch, msk=msk, ml=ml: e.tensor_tensor(out=ml[:], in0=PS[pi][:, 0:256], in1=msk[:, ch, :], op=ALU.mult),
                             r=[("ps", pi), "mF", "mB"], w=[mk])
                    p.dma("sp", g["SSM_ML"][l, d, gg, ch], ml[:], r=[mk], w=["SSM_ML"])
                    pj = pi + 1
                    p.op("pe", lambda e, pj=pj, g16=g16, cs=cs, r0=r0: e.transpose(
                        PS[pj][:, 0:64], BLr[r0:r0 + 64, g16, cs], ident_f[r0:r0 + 64, r0:r0 + 64]), r=K_ + ["ident_f"], w=[("ps", pj)])
                    p.op("pe", lambda e, pj=pj, g16=g16, cs=cs, r0=r0: e.transpose(
                        PS[pj][:, 64:128], BLi[r0:r0 + 64, g16, cs], ident_f[r0:r0 + 64, r0:r0 + 64]), r=K_ + ["ident_f"], w=[("ps", pj)])
                    bl = blt[i_]; bk = ("blt", i_)
                    p.op("act", lambda e, pj=pj, bl=bl: e.copy(out=bl[:], in_=PS[pj][:, 0:128]), r=[("ps", pj)], w=[bk])
                    p.dma("sp", g["SSM_BLT"][l, d, gg, ch], bl[:], r=[bk], w=["SSM_BLT"])
    p.barrier()


class WStream:
    def __init__(self, p, name, stages, wbs, cast_engs=("act", "pool")):
        self.p = p
        self.name = name
        self.stages = stages
        self.wbs = wbs
        self.cast_engs = cast_engs
        self.ci = 0

    def run(self, groups, compute):
        p = self.p
        ns, nw = len(self.stages), len(self.wbs)
        n = len(groups)

        def dma(i):
            st = self.stages[i % ns]
            sk = (self.name + "st", i % ns)
            for (src, k0, K, c0, nn) in groups[i]:
                p.dma("sp", st[:, k0:k0 + K, c0:c0 + nn], src.rearrange("(kc p) n -> p kc n", p=128), w=[sk])

        def cast(i):
            st = self.stages[i % ns]
            sk = (self.name + "st", i % ns)
            wb = self.wbs[i % nw]
            wk = (self.name + "wb", i % nw)
            for (src, k0, K, c0, nn) in groups[i]:
                eng = self.cast_engs[self.ci % len(self.cast_engs)]
                self.ci += 1
                if eng == "act":
                    p.op("act", lambda e, st=st, wb=wb, k0=k0, K=K, c0=c0, nn=nn: e.copy(
                        out=wb[:, k0:k0 + K, c0:c0 + nn], in_=st[:, k0:k0 + K, c0:c0 + nn]), r=[sk], w=[wk])
                else:
                    p.op(eng, lambda e, st=st, wb=wb, k0=k0, K=K, c0=c0, nn=nn: e.tensor_copy(
                        out=wb[:, k0:k0 + K, c0:c0 + nn], in_=st[:, k0:k0 + K, c0:c0 + nn]), r=[sk], w=[wk])

        for i in range(min(ns - 1, n)):
            dma(i)
        if n:
            cast(0)
        for i in range(n):
            if i + ns - 1 < n:
                dma(i + ns - 1)
            if i + 1 < n:
                cast(i + 1)
            compute(i, self.wbs[i % nw], (self.name + "wb", i % nw))
```

```python
import math
from contextlib import ExitStack
import numpy as np
import ml_dtypes
import concourse.bass as bass
import concourse.mybir as mybir
from concourse.bass_utils import run_bass_kernel_spmd

F32 = mybir.dt.float32
BF16 = mybir.dt.bfloat16
AF = mybir.ActivationFunctionType
ALU = mybir.AluOpType

D = 2048
T = 2560
NT = 5
TP = 512
TS = 2048
PAST = 512
TK = T + PAST
L = 16
NQ = T // L
EPS = 1e-6
HID = 8192
DEPTH = 2
SEGS = [(0, 256), (256, 256), (512, 2048)]
IN_OFF = dict(q=0, kv=512, kr=768, cu=800, cb=1312, cc=1824, su=2336, pu=2848, g=3360)
IN_COLS = 11552


class Prog:
    ENGS = ("pe", "act", "dve", "pool", "sp")
    KQ = 8

    def __init__(self, nc):
        self.nc = nc
        self.ops = []
        self.lw = {}
        self.rd = {}
        self.last_on = {e: None for e in self.ENGS}
        self.pend = {e: set() for e in self.ENGS}
        self.dma_since = []

    def _add(self, eng, fn, r, w, is_dma):
        idx = len(self.ops)
        deps = set(self.pend[eng])
        self.pend[eng] = set()
        for k in r:
            x = self.lw.get(k)
            if x is not None:
                deps.add(x)
        for k in w:
            x = self.lw.get(k)
            if x is not None:
                deps.add(x)
            for y in self.rd.get(k, ()):
                deps.add(y)
        for k in r:
            self.rd.setdefault(k, []).append(idx)
        for k in w:
            self.lw[k] = idx
            self.rd[k] = []
        deps.discard(idx)
        self.ops.append((eng, fn, deps, is_dma))
        self.last_on[eng] = idx
        if is_dma:
            self.dma_since.append(idx)
        return idx

    def op(self, eng, fn, r=(), w=()):
        return self._add(eng, fn, r, w, False)

    def dma(self, q, out, in_, r=(), w=(), slow=False):
        if slow:
            return self._add(q, lambda e: e.dma_start(out=out, in_=in_, allow_slow_non_contiguous=True), r, w, True)
        return self._add(q, lambda e: e.dma_start(out=out, in_=in_), r, w, True)

    def barrier(self):
        bar = set(self.dma_since)
        for e in self.ENGS:
            if self.last_on[e] is not None:
                bar.add(self.last_on[e])
        for e in self.ENGS:
            self.pend[e] |= bar
        self.dma_since = []

    def emit(self, es):
        nc = self.nc
        ops = self.ops
        needed = [False] * len(ops)
        for (eng_, _, deps, _) in ops:
            for d in deps:
                if eng_ == "pe" and ops[d][0] == "pe" and not ops[d][3]:
                    continue
                needed[d] = True
        esem = {e: es.enter_context(nc.semaphore("s_" + e)) for e in self.ENGS}
        qsem = {e: [es.enter_context(nc.semaphore("q_%s%d" % (e, i))) for i in range(self.KQ)]
                for e in ("sp", "pool", "act")}
        ev = [None] * len(ops)
        cnt = {e: 0 for e in self.ENGS}
        qcnt = {e: 0 for e in qsem}
        pre = [None] * len(ops)
        for i, (eng, fn, deps, is_dma) in enumerate(ops):
            if is_dma:
                n = qcnt[eng]
                qcnt[eng] += 1
                s = qsem[eng][n % self.KQ]
                ev[i] = (s, 16 * (n // self.KQ + 1))
                if n >= self.KQ:
                    pre[i] = (s, 16 * (n // self.KQ))
            elif needed[i]:
                cnt[eng] += 1
                ev[i] = (esem[eng], cnt[eng])
        streams = {e: [] for e in self.ENGS}
        for i, o in enumerate(ops):
            streams[o[0]].append(i)
        block = es.enter_context(nc.Block())

        def run(eng_name, e):
            seen = {}
            for i in streams[eng_name]:
                _, fn, deps, is_dma = ops[i]
                waits = {}
                if pre[i] is not None:
                    waits[pre[i][0]] = pre[i][1]
                for d in deps:
                    if eng_name == "pe" and ops[d][0] == "pe" and not ops[d][3]:
                        continue
                    s, v = ev[d]
                    if waits.get(s, 0) < v:
                        waits[s] = v
                for s, v in waits.items():
                    if seen.get(s, 0) < v:
                        e.wait_ge(s, v)
                        seen[s] = v
                ins = fn(e)
                if is_dma:
                    ins.then_inc(ev[i][0], 16)
                elif ev[i] is not None:
                    ins.then_inc(ev[i][0], 1)
            if eng_name in qsem:
                n = qcnt[eng_name]
                for k in range(min(n, self.KQ)):
                    tot = (n - k + self.KQ - 1) // self.KQ
                    e.wait_ge(qsem[eng_name][k], 16 * tot)

        @block.tensor
        def _(e):
            run("pe", e)

        @block.scalar
        def _(e):
            run("act", e)

        @block.vector
        def _(e):
            run("dve", e)

        @block.gpsimd
        def _(e):
            run("pool", e)

        @block.sync
        def _(e):
            run("sp", e)


def _feat_layout(v, nchunk):
    return np.ascontiguousarray(v.reshape(nchunk, 128).T)


def host_consts():
    c = {}
    c["ident_f"] = np.eye(128, dtype=np.float32)
    half = 16
    inv_freq = (10000.0 ** (-np.arange(0, half, 2, dtype=np.float32) / half)).astype(np.float32)
    rows = TS // 64
    row_pos = np.repeat(np.arange(rows, dtype=np.float32), 64)
    col_pos = np.tile(np.arange(64, dtype=np.float32), rows)
    ang = np.zeros((32, TS), np.float32)
    for r in range(32):
        pos = row_pos if r < 16 else col_pos
        ang[r] = pos * inv_freq[r % 8]
    c["rope_cos"] = np.cos(ang).astype(np.float32)
    c["rope_sin"] = np.sin(ang).astype(np.float32)
    prot = np.zeros((128, 96), np.float32)
    for mm in range(32):
        m = 64 + mm
        if mm % 16 < 8:
            prot[m + 8, m] = -1.0
        else:
            prot[m - 8, m] = 1.0
    c["prot"] = prot
    inv = np.zeros((4, 128, T), np.float32)
    for gi, w in enumerate((2, 4, 8, 16)):
        for (s0, n) in SEGS:
            t = np.arange(n)
            lo = np.clip(t - w // 2, 0, n)
            hi = np.clip(t + w // 2, 0, n)
            inv[gi, :, s0:s0 + n] = (1.0 / (hi - lo).astype(np.float32))[None, :]
    c["pool_inv"] = inv
    mF = np.zeros((2, 128, 256), np.float32)
    mB = np.zeros((2, 128, 256), np.float32)
    dI = np.zeros((2, 128, 256), np.float32)
    for ch in range(2):
        for pp in range(128):
            j = ch * 8 + pp // 16
            cc = pp % 16
            for i in range(16):
                if i >= j:
                    mF[ch, pp, i * 16:(i + 1) * 16] = 1.0
                if j >= i:
                    mB[ch, pp, i * 16:(i + 1) * 16] = 1.0
            dI[ch, pp, j * 16 + cc] = 1.0
    c["ssm_mF"] = mF
    c["ssm_mB"] = mB
    c["ssm_dI"] = dI
    return c


class K:
    pass


def build(debug=(), stop_after=None):
    nc = bass.Bass("TRN2", target_bir_lowering=False)
    k = K()
    k.nc = nc
    k.stop_after = stop_after
    k.debug = debug
    p = Prog(nc)
    k.p = p

    def din(name, shape, dt=F32):
        return nc.dram_tensor(name, list(shape), dt, kind="ExternalInput").ap()

    def dout(name, shape, dt=F32):
        return nc.dram_tensor(name, list(shape), dt, kind="ExternalOutput").ap()

    def dscr(name, shape, dt):
        kind = "ExternalOutput" if name in debug else "Internal"
        return nc.dram_tensor(name, list(shape), dt, kind=kind).ap()

    xin = din("xin", [T, D])
    cvecT = din("cvecT", [128, 16, 2])
    cache_ckv = din("cache_ckv", [DEPTH, PAST, 256])
    cache_kr = din("cache_kr", [DEPTH, PAST, 32])
    state_in = din("state_in", [DEPTH, 2, 2, 32, 64])
    w_ada = din("w_ada", [DEPTH, D, 6 * D])
    b_adaT = din("b_adaT", [DEPTH, 128, 96])
    n1gT = din("n1gT", [DEPTH, 128, 16])
    n2gT = din("n2gT", [DEPTH, 128, 16])
    w_in = din("w_in", [DEPTH, D, IN_COLS])
    qagT = din("qagT", [DEPTH, 128, 4])
    kvgT = din("kvgT", [DEPTH, 128, 2])
    w_uq = din("w_uq", [DEPTH, 512, 768])
    w_ukv = din("w_ukv", [DEPTH, 256, 1024])
    qng = din("qng", [DEPTH, 96, 1])
    kng = din("kng", [DEPTH, 96, 1])
    w_mla_o = din("w_mla_o", [DEPTH, 512, D])
    conv_wT = din("conv_wT", [DEPTH, 128, 4, 3])
    conv_bT = din("conv_bT", [DEPTH, 128, 4])
    w_conv_o = din("w_conv_o", [DEPTH, 512, D])
    ssm_lam_re = din("ssm_lam_re", [DEPTH, 2, 32, 64])
    ssm_lam_im = din("ssm_lam_im", [DEPTH, 2, 32, 64])
    ssm_log_step = din("ssm_log_step", [DEPTH, 2, 32])
    ssm_b_re = din("ssm_b_re", [DEPTH, 2, 32, 64, 16])
    ssm_b_im = din("ssm_b_im", [DEPTH, 2, 32, 64, 16])
    ssm_c_re = din("ssm_c_re", [DEPTH, 2, 32, 16, 64])
    ssm_c_im = din("ssm_c_im", [DEPTH, 2, 32, 16, 64])
    ssm_d = din("ssm_d", [DEPTH, 512])
    w_glu = din("w_glu", [DEPTH, 512, 2 * D])
    pool_w = din("pool_w", [DEPTH, 4, 128, 128])
    pool_sT = din("pool_sT", [DEPTH, 128, 4])
    w_pool_o = din("w_pool_o", [DEPTH, 512, D])
    w_o = din("w_o", [DEPTH, D, D])
    w_mlp1 = din("w_mlp1", [DEPTH, D, HID])
    w_mlp2 = din("w_mlp2", [DEPTH, HID, D])
    c_ident = din("ident_f", [128, 128])
    c_cos = din("rope_cos", [32, TS])
    c_sin = din("rope_sin", [32, TS])
    c_prot = din("prot", [128, 96])
    c_pinv = din("pool_inv", [4, 128, T])
    c_mF = din("ssm_mF", [2, 128, 256])
    c_mB = din("ssm_mB", [2, 128, 256])
    c_dI = din("ssm_dI", [2, 128, 256])
    c_dvec = din("ssm_dvec", [128, DEPTH, 32])
    yout = dout("yout", [T, D])
    o_ckv = dout("o_ckv", [2, DEPTH, 256, 256])
    o_kr = dout("o_kr", [2, DEPTH, 256, 32])
    o_ssm = dout("o_ssm", [2, DEPTH, 2, 2, 32, 64])
    XT = [dscr("XT%d" % i, [16, 128, T], F32) for i in range(2)]
    PROJ = dscr("PROJ", [20, 128, T], BF16)
    KV32 = dscr("KV32", [3, 128, T], F32)
    UTOK = dscr("UTOK", [T, 512], BF16)
    YTOK = dscr("YTOK", [T, 512], BF16)
    GS = dscr("GS", [64, 128, T], BF16)
    MT = dscr("MT", [16, 128, T], BF16)
    AT = dscr("AT", [64, 128, T], BF16)
    W2B = dscr("W2B", [16, 128, 64 * 128], BF16)
    SSM_BLT = dscr("SSM_BLT", [DEPTH, 2, 32, 2, 128, 128], BF16)
    SSM_ML = dscr("SSM_ML", [DEPTH, 2, 32, 2, 128, 256], BF16)
    SSM_CS = dscr("SSM_CS", [DEPTH, 2, 32, 64, 2, 256], BF16)
    if "DBG_BI" in debug:
        k.DBG_BI = dscr("DBG_BI", [16, 128, T], BF16)

    es = ExitStack()
    k.es = es

    def sb(name, shape, dt):
        return es.enter_context(nc.sbuf_tensor("sb_" + name, list(shape), dt))

    PSD = [es.enter_context(nc.psum_tensor("psd%d" % i, [128, 1024], F32)) for i in range(4)]
    PS = [PSD[i // 2][:, (i % 2) * 512:(i % 2 + 1) * 512] for i in range(8)]
    psc = [0]

    def nextps():
        i = psc[0] % 8
        psc[0] += 1
        return i

    ident_f = sb("ident_f", [128, 128], F32)
    ident_b = sb("ident_b", [128, 128], BF16)
    ones_b = sb("ones_b", [128, 128], BF16)
    ones_f = sb("ones_f", [128, 128], F32)
    epsc = sb("epsc", [128, 1], F32)
    modv = sb("modv", [128, DEPTH, 96, 2], F32)
    A1 = sb("A1", [128, DEPTH, 16, 2], F32)
    A2 = sb("A2", [128, DEPTH, 16, 2], F32)
    n1g = sb("n1g", [128, DEPTH, 16], F32)
    n2g = sb("n2g", [128, DEPTH, 16], F32)
    CAA = sb("CAA", [128, DEPTH, 2, 32], F32)
    CAB = sb("CAB", [128, DEPTH, 2, 32], F32)

    p.dma("sp", ident_f[:], c_ident[:, :], w=["ident_f"])
    p.op("dve", lambda e: e.tensor_copy(out=ident_b[:], in_=ident_f[:]), r=["ident_f"], w=["ident_b"])
    p.op("pool", lambda e: e.memset(ones_b[:], 1.0), w=["ones_b"])
    p.op("pool", lambda e: e.memset(ones_f[:], 1.0), w=["ones_f"])
    p.op("pool", lambda e: e.memset(epsc[:], EPS), w=["epsc"])
    for l in range(DEPTH):
        p.dma("sp", n1g[:, l, :], n1gT[l], w=["n1g"])
        p.dma("sp", n2g[:, l, :], n2gT[l], w=["n2g"])

    def A(buf, c, t0, n):
        return buf[:, c * T + t0: c * T + t0 + n]

    pro = ExitStack()
    gin = ssm_gen_alloc(nc, pro)
    with ExitStack() as ph:
        sT = ph.enter_context(nc.sbuf_tensor("sT", [128, 16, 2], F32))
        sg = ph.enter_context(nc.sbuf_tensor("sgT", [128, 16, 2], F32))
        wab = [ph.enter_context(nc.sbuf_tensor("wab%d" % i, [128, 16, 512], F32)) for i in range(3)]
        wabb = [ph.enter_context(nc.sbuf_tensor("wabb%d" % i, [128, 16, 512], BF16)) for i in range(2)]
        sTb = ph.enter_context(nc.sbuf_tensor("sTb", [128, 16, 2], BF16))
        bad = ph.enter_context(nc.sbuf_tensor("bad", [128, DEPTH, 96], F32))
        p.dma("sp", sT[:], cvecT[:, :, :], w=["sT"])
        p.op("act", lambda e: e.activation(out=sg[:], in_=sT[:], func=AF.Sigmoid), r=["sT"], w=["sg"])
        p.op("dve", lambda e: e.tensor_tensor(out=sTb[:], in0=sT[:], in1=sg[:], op=ALU.mult), r=["sT", "sg"], w=["sTb"])
        for l in range(DEPTH):
            p.dma("sp", bad[:, l, :], b_adaT[l], w=["bad"])
        gl = [(l, g) for l in range(DEPTH) for g in range(24)]

        def ada_dma(i):
            l, g = gl[i]
            src = w_ada[l, :, g * 512:(g + 1) * 512].rearrange("(kc p) n -> p kc n", p=128)
            p.dma("sp", wab[i % 3][:], src, w=[("wab", i % 3)])

        def ada_cast(i):
            wf = wab[i % 3]; wb = wabb[i % 2]
            p.op("act", lambda e, wf=wf, wb=wb: e.copy(out=wb[:, 0:8, :], in_=wf[:, 0:8, :]), r=[("wab", i % 3)], w=[("wabb", i % 2)])
            p.op("dve", lambda e, wf=wf, wb=wb: e.tensor_copy(out=wb[:, 8:16, :], in_=wf[:, 8:16, :]), r=[("wab", i % 3)], w=[("wabb", i % 2)])
        ada_dma(0); ada_dma(1); ada_cast(0)
        ssm_gen_loads(p, locals(), gin)
        for i, (l, g) in enumerate(gl):
            if i + 2 < len(gl):
                ada_dma(i + 2)
            if i + 1 < len(gl):
                ada_cast(i + 1)
            wb = wabb[i % 2]; wk = ("wabb", i % 2)
            pi = 7
            for fc in range(4):
                col = (g * 4 + fc) * 2
                for kc in range(16):
                    p.op("pe", lambda e, wb=wb, kc=kc, fc=fc, col=col: e.matmul(
                        PS[pi][:, col:col + 2], wb[:, kc, fc * 128:(fc + 1) * 128], sTb[:, kc, :],
                        start=(kc == 0), stop=(kc == 15)), r=[wk, "sTb"], w=[("ps", pi)])
            if g != 23:
                continue
            p.op("dve", lambda e, l=l: e.tensor_tensor(
                out=modv[:, l, :, :], in0=PS[7][:, 0:192].rearrange("p (c j) -> p c j", j=2),
                in1=bad[:, l, :].unsqueeze(2).to_broadcast([128, 96, 2]), op=ALU.add),
                r=[("ps", 7), "bad"], w=["modv"])
            for (Ax, ng, c0) in ((A1, n1g, 16), (A2, n2g, 64)):
                p.op("dve", lambda e, Ax=Ax, ng=ng, c0=c0, l=l: e.scalar_tensor_tensor(
                    out=Ax[:, l, :, :], in0=modv[:, l, c0:c0 + 16, :], scalar=1.0,
                    in1=ng[:, l, :].unsqueeze(2).to_broadcast([128, 16, 2]), op0=ALU.add, op1=ALU.mult),
                    r=["modv", "n1g", "n2g"], w=["A1A2"])
    p.barrier()
    for l in range(DEPTH):
        ssm_gen(k, locals(), l, gin)
    pro.close()
    BUFA = sb("BUFA", [128, 16 * T], BF16)
    BUFB = sb("BUFB", [128, 16 * T], BF16)
    build_layers(k, locals())
    return k


def build_layers(k, g):
    nc = k.nc
    p = k.p
    PS = g["PS"]; nextps = g["nextps"]; A = g["A"]
    BUFA = g["BUFA"]; BUFB = g["BUFB"]
    ident_f = g["ident_f"]; ident_b = g["ident_b"]; ones_b = g["ones_b"]
    modv = g["modv"]; A1 = g["A1"]; A2 = g["A2"]
    XT = g["XT"]; PROJ = g["PROJ"]; KV32 = g["KV32"]; UTOK = g["UTOK"]; YTOK = g["YTOK"]
    GS = g["GS"]; MT = g["MT"]; AT = g["AT"]
    uid = [0]

    def scoped(ph, name, shape, dt):
        uid[0] += 1
        return ph.enter_context(nc.sbuf_tensor("t%d_%s" % (uid[0], name), list(shape), dt))

    def mod_j(tt):
        return 0 if tt == 0 else 1

    with ExitStack() as ph:
        xt = [scoped(ph, "xt%d" % i, [128, D], F32) for i in range(2)]
        st = [scoped(ph, "st%d" % i, [128, 4, 128], F32) for i in range(3)]
        si = 0
        for ti in range(T // 128):
            xb = xt[ti % 2]; xk = ("xt", ti % 2)
            p.dma("sp", xb[:], g["xin"][ti * 128:(ti + 1) * 128, :], w=[xk])
            for fg in range(4):
                pi = nextps()
                for f4 in range(4):
                    fc = fg * 4 + f4
                    p.op("pe", lambda e, pi=pi, f4=f4, fc=fc, xb=xb: e.transpose(
                        PS[pi][:, f4 * 128:(f4 + 1) * 128], xb[:, fc * 128:(fc + 1) * 128], ident_f[:]),
                        r=[xk, "ident_f"], w=[("ps", pi)])
                sb_ = st[si % 3]; sk = ("st", si % 3); si += 1
                eng = "dve" if fg % 2 == 0 else "act"
                if eng == "dve":
                    p.op("dve", lambda e, pi=pi, sb_=sb_: e.tensor_copy(
                        out=sb_[:], in_=PS[pi][:, :].rearrange("p (a b) -> p a b", a=4)), r=[("ps", pi)], w=[sk])
                else:
                    p.op("act", lambda e, pi=pi, sb_=sb_: e.copy(
                        out=sb_[:], in_=PS[pi][:, :].rearrange("p (a b) -> p a b", a=4)), r=[("ps", pi)], w=[sk])
                dst = XT[0][fg * 4:(fg + 1) * 4, :, ti * 128:(ti + 1) * 128].rearrange("c p t -> p c t")
                p.dma("sp", dst, sb_[:], r=[sk], w=["XT0"])
    p.barrier()

    def norm(XTd, xkey, Amod, l, shift_c0, dst):
        with ExitStack() as ph:
            xc = [scoped(ph, "xc%d" % i, [128, T], F32) for i in range(2)]
            sq = [scoped(ph, "sq%d" % i, [128, T], BF16) for i in range(2)]
            RB = scoped(ph, "RB", [128, T], F32)
            pss = [nextps() for _ in range(NT)]
            for fc in range(16):
                xb = xc[fc % 2]; xk = ("xc", fc % 2)
                p.dma("sp", xb[:], XTd[fc], r=[xkey], w=[xk])
                sb_ = sq[fc % 2]; sk = ("sq", fc % 2)
                p.op("act", lambda e, xb=xb, sb_=sb_: e.activation(out=sb_[:], in_=xb[:], func=AF.Square),
                     r=[xk], w=[sk])
                for tt in range(NT):
                    p.op("pe", lambda e, tt=tt, sb_=sb_, fc=fc: e.matmul(
                        PS[pss[tt]][:, :], ones_b[:, :], sb_[:, tt * 512:(tt + 1) * 512],
                        start=(fc == 0), stop=(fc == 15)), r=[sk, "ones_b"], w=[("ps", pss[tt])])
            for tt in range(NT):
                sl = RB[:, tt * 512:(tt + 1) * 512]
                p.op("act", lambda e, tt=tt, sl=sl: e.activation(out=sl, in_=PS[pss[tt]][:, :], func=AF.Ln, scale=1.0 / D, bias=g["epsc"][:, 0:1]),
                     r=[("ps", pss[tt]), "epsc"], w=[("RB", tt)])
                p.op("act", lambda e, sl=sl: e.activation(out=sl, in_=sl, func=AF.Exp, scale=-0.5), r=[("RB", tt)], w=[("RB", tt)])
            for fc in range(16):
                xb = xc[fc % 2]; xk = ("xc", fc % 2)
                p.dma("sp", xb[:], XTd[fc], r=[xkey], w=[xk])
                for tt in range(NT):
                    j = mod_j(tt)
                    sl = xb[:, tt * 512:(tt + 1) * 512]
                    p.op("dve", lambda e, sl=sl, tt=tt, fc=fc, j=j: e.scalar_tensor_tensor(
                        out=sl, in0=sl, scalar=Amod[:, l, fc, j:j + 1], in1=RB[:, tt * 512:(tt + 1) * 512],
                        op0=ALU.mult, op1=ALU.mult), r=[xk, ("RB", tt), "A1A2"], w=[xk])
                    p.op("act", lambda e, sl=sl, tt=tt, fc=fc, j=j: e.activation(
                        out=A(dst, fc, tt * 512, 512), in_=sl, func=AF.Identity,
                        bias=modv[:, l, shift_c0 + fc, j:j + 1], scale=1.0), r=[xk, "modv"], w=["BUF"])
        p.barrier()

    k.norm = norm
    for l in range(DEPTH):
        layer(k, g, l, scoped, norm)
        if k.stop_after:
            break
    with ExitStack() as ph:
        xc = [scoped(ph, "oc%d" % i, [128, 4, 512], F32) for i in range(2)]
        ot = [scoped(ph, "ot%d" % i, [128, 4, 512], F32) for i in range(2)]
        it = 0
        for tt in range(NT):
            for fg in range(4):
                xb = xc[it % 2]; xk = ("oc", it % 2)
                src = XT[0][fg * 4:(fg + 1) * 4, :, tt * 512:(tt + 1) * 512].rearrange("c p t -> p c t")
                p.dma("sp", xb[:], src, r=["XT0"], w=[xk])
                ob = ot[it % 2]; ok = ("ot", it % 2); it += 1
                for t4 in range(4):
                    pi = nextps()
                    for f4 in range(4):
                        p.op("pe", lambda e, pi=pi, f4=f4, t4=t4, xb=xb: e.transpose(
                            PS[pi][:, f4 * 128:(f4 + 1) * 128], xb[:, f4, t4 * 128:(t4 + 1) * 128], ident_f[:]),
                            r=[xk, "ident_f"], w=[("ps", pi)])
                    if t4 % 2 == 0:
                        p.op("dve", lambda e, pi=pi, ob=ob, t4=t4: e.tensor_copy(out=ob[:, t4, :], in_=PS[pi][:, :]),
                             r=[("ps", pi)], w=[ok])
                    else:
                        p.op("act", lambda e, pi=pi, ob=ob, t4=t4: e.copy(out=ob[:, t4, :], in_=PS[pi][:, :]),
                             r=[("ps", pi)], w=[ok])
                dst = g["yout"][tt * 512:(tt + 1) * 512, fg * 512:(fg + 1) * 512].rearrange("(a p) f -> p a f", p=128)
                p.dma("sp", dst, ob[:], r=[ok], w=["yout"])
    p.barrier()
    p.emit(k.es)


def rot(lst):
    st = [0]

    def nxt():
        v = lst[st[0] % len(lst)]
        st[0] += 1
        return v
    return nxt


def layer(k, g, l, scoped, norm):
    nc = k.nc
    p = k.p
    PS = g["PS"]; A = g["A"]
    BUFA = g["BUFA"]; BUFB = g["BUFB"]
    ident_f = g["ident_f"]; ident_b = g["ident_b"]; ones_b = g["ones_b"]
    modv = g["modv"]; A1 = g["A1"]; A2 = g["A2"]
    XT = g["XT"]; PROJ = g["PROJ"]; KV32 = g["KV32"]; UTOK = g["UTOK"]; YTOK = g["YTOK"]
    GS = g["GS"]; MT = g["MT"]; AT = g["AT"]
    w_in = g["w_in"][l]

    def mod_j(tt):
        return 0 if tt == 0 else 1

    def wload(wb, wk, src, K, n, c0=0):
        p.dma("pool", wb[:, 0:K, c0:c0 + n], src.rearrange("(kc p) n -> p kc n", p=128), w=[wk])

    norm(XT[0], "XT0", A1, l, 0, BUFA)

    with ExitStack() as ph:
        wbs = [scoped(ph, "wb%d" % i, [128, 16, 256], BF16) for i in range(3)]
        stg = [scoped(ph, "stg%d" % i, [128, T], BF16) for i in range(2)]
        s32 = [scoped(ph, "s32_%d" % i, [128, 512], F32) for i in range(2)]
        stages = [BUFB[:, i * 8192:(i + 1) * 8192].bitcast(F32).rearrange("p (k n) -> p k n", k=16) for i in range(4)]
        sti = [0]; s3i = [0]
        psr = rot(list(range(8)))
        groups = []
        for j in range(2):
            groups.append(("bf", 0 + j * 256, 256, 0 + 2 * j))
        for (c0, d0) in ((800, 4), (1312, 8), (1824, 12), (2848, 16)):
            for j in range(2):
                groups.append(("bf", c0 + j * 256, 256, d0 + 2 * j))
        groups.append(("kv", 512, 256, 0))
        groups.append(("kr", 768, 32, 2))
        for j in range(32):
            groups.append(("gate", 3360 + j * 256, 256, 2 * j))
        for half in range(2):
            groups.append(("ssm", 2336 + half * 256, 256, half))
        pieces = []
        for (kind, c0, n, d0) in groups:
            if kind == "kr":
                pieces.append([(w_in[:, c0:c0 + 32], 0, 16, 64, 32)])
            else:
                pieces.append([(w_in[:, c0:c0 + n], 0, 16, 0, n)])
        ws = WStream(p, "win", stages, wbs)
        kr_idx = [i for i, g_ in enumerate(groups) if g_[0] == "kr"][0]
        kr_wb = wbs[kr_idx % 3]
        orig_cast_hook = {"done": False}

        def compute(gi_, wb, wk):
            kind, c0, n, d0 = groups[gi_]
            if gi_ + 1 == kr_idx:
                p.op("dve", lambda e: e.memset(kr_wb[:, :, 0:64], 0.0), w=[("winwb", kr_idx % 3)])
            if kind == "ssm":
                half = d0
                for ti in range(T // 128):
                    pi = psr()
                    for kc in range(16):
                        p.op("pe", lambda e, pi=pi, wb=wb, kc=kc, ti=ti: e.matmul(
                            PS[pi][:, 0:256], A(BUFA, kc, ti * 128, 128), wb[:, kc, 0:256],
                            start=(kc == 0), stop=(kc == 15)), r=[wk, "BUF"], w=[("ps", pi)])
                    sb_ = stg[sti[0] % 2]; sk = ("stg", sti[0] % 2); sti[0] += 1
                    p.op("dve", lambda e, pi=pi, sb_=sb_: e.tensor_copy(out=sb_[:, 0:256], in_=PS[pi][:, 0:256]),
                         r=[("ps", pi)], w=[sk])
                    p.dma("sp", UTOK[ti * 128:(ti + 1) * 128, half * 256:(half + 1) * 256], sb_[:, 0:256], r=[sk], w=["UTOK"])
                return
            subs = [(0, 96)] if kind == "kr" else [(0, 128), (128, 128)]
            for si, (sc0, sn) in enumerate(subs):
                if kind in ("bf", "gate"):
                    sb_ = stg[sti[0] % 2]; sk = ("stg", sti[0] % 2); sti[0] += 1
                for tt in range(NT):
                    pi = psr()
                    for kc in range(16):
                        p.op("pe", lambda e, pi=pi, wb=wb, kc=kc, sc0=sc0, sn=sn, tt=tt: e.matmul(
                            PS[pi][0:sn, :], wb[:, kc, sc0:sc0 + sn], A(BUFA, kc, tt * 512, 512),
                            start=(kc == 0), stop=(kc == 15)), r=[wk, "BUF"], w=[("ps", pi)])
                    if kind == "bf":
                        p.op("dve", lambda e, pi=pi, sb_=sb_, tt=tt: e.tensor_copy(
                            out=sb_[:, tt * 512:(tt + 1) * 512], in_=PS[pi][:, :]), r=[("ps", pi)], w=[sk])
                    elif kind == "gate":
                        p.op("act", lambda e, pi=pi, sb_=sb_, tt=tt: e.activation(
                            out=sb_[:, tt * 512:(tt + 1) * 512], in_=PS[pi][:, :], func=AF.Sigmoid),
                            r=[("ps", pi)], w=[sk])
                    else:
                        s3 = s32[s3i[0] % 2]; s3k = ("s32", s3i[0] % 2); s3i[0] += 1
                        r0 = 64 if kind == "kr" else 0
                        r1 = 96 if kind == "kr" else 128
                        p.op("dve", lambda e, pi=pi, s3=s3, r0=r0, r1=r1: e.tensor_copy(
                            out=s3[r0:r1, :], in_=PS[pi][r0:r1, :]), r=[("ps", pi)], w=[s3k])
                        p.dma("sp", KV32[d0 + si, r0:r1, tt * 512:(tt + 1) * 512], s3[r0:r1, :], r=[s3k], w=["KV32"])
                if kind == "bf":
                    p.dma("sp", PROJ[d0 + si], sb_[:], r=[sk], w=["PROJ"])
                elif kind == "gate":
                    p.dma("sp", GS[d0 + si], sb_[:], r=[sk], w=["GS"])
        ws.run(pieces, compute)
    p.barrier()
    if k.stop_after == "P2":
        return
    mixers(k, g, l, scoped)
    p.barrier()
    if k.stop_after == "P3":
        return
    tail(k, g, l, scoped, norm)


def make_in_maps(inp):
    f = np.float32
    consts = host_consts()
    shared = dict(consts)

    def featl(a, n):
        return np.ascontiguousarray(a.reshape(DEPTH, n, 128).transpose(0, 2, 1))
    shared["w_ada"] = inp["w_ada"]
    shared["b_adaT"] = featl(inp["b_ada"], 96)
    shared["n1gT"] = featl(inp["norm1_g"], 16)
    shared["n2gT"] = featl(inp["norm2_g"], 16)
    shared["w_in"] = inp["w_in"]
    shared["qagT"] = featl(inp["q_a_norm_g"], 4)
    shared["kvgT"] = featl(inp["kv_a_norm_g"], 2)
    shared["w_uq"] = inp["w_uq"]
    shared["w_ukv"] = inp["w_ukv"]
    shared["qng"] = np.ascontiguousarray(inp["q_norm_g"].reshape(DEPTH, 96, 1))
    shared["kng"] = np.ascontiguousarray(inp["k_norm_g"].reshape(DEPTH, 96, 1))
    shared["w_mla_o"] = inp["w_mla_o"]
    shared["conv_wT"] = np.ascontiguousarray(inp["conv_w"].reshape(DEPTH, 3, 4, 128).transpose(0, 3, 2, 1))
    shared["conv_bT"] = featl(inp["conv_b"], 4)
    shared["w_conv_o"] = inp["w_conv_o"]
    for nm in ("ssm_lam_re", "ssm_lam_im", "ssm_log_step", "ssm_b_re", "ssm_b_im", "ssm_c_re", "ssm_c_im",
               "ssm_d", "w_glu", "pool_w", "w_pool_o", "w_o", "w_mlp1", "w_mlp2"):
        shared[nm] = inp[nm]
    shared["pool_sT"] = featl(inp["pool_scale"], 4)
    dv = inp["ssm_d"].reshape(DEPTH, 32, 16)
    shared["ssm_dvec"] = np.ascontiguousarray(np.tile(dv.transpose(2, 0, 1)[None], (8, 1, 1, 1)).reshape(128, DEPTH, 32))
    maps = []
    for c in range(8):
        m = dict(shared)
        m["xin"] = np.ascontiguousarray(np.concatenate(
            [inp["x_prompt"][2 * c], inp["x_prompt"][2 * c + 1], inp["x_sample"][c]], axis=0))
        cv = np.stack([inp["c_ctx"], inp["c"][c]], axis=0)
        m["cvecT"] = np.ascontiguousarray(cv.reshape(2, 16, 128).transpose(2, 1, 0))
        m["cache_ckv"] = np.ascontiguousarray(inp["cache_ckv"][c])
        m["cache_kr"] = np.ascontiguousarray(inp["cache_krope"][c])
        m["state_in"] = np.ascontiguousarray(inp["state_ssm"][c])
        maps.append(m)
    return maps


_CACHE = {}


def kernel(**inputs):
    inp = {k_: np.asarray(v) for k_, v in inputs.items()}
    if "k" not in _CACHE:
        _CACHE["k"] = build()
    k = _CACHE["k"]
    maps = make_in_maps(inp)
    res = run_bass_kernel_spmd(k.nc, maps, core_ids=list(range(8)))
    R = res.results
    y_prompt = np.zeros((16, 256, D), np.float32)
    y_sample = np.zeros((8, 2048, D), np.float32)
    new_ckv = np.zeros((16, DEPTH, 256, 256), np.float32)
    new_kr = np.zeros((16, DEPTH, 256, 32), np.float32)
    new_ssm = np.zeros((16, DEPTH, 2, 2, 32, 64), np.float32)
    for c in range(8):
        y = R[c]["yout"]
        y_prompt[2 * c] = y[0:256]
        y_prompt[2 * c + 1] = y[256:512]
        y_sample[c] = y[512:]
        new_ckv[2 * c:2 * c + 2] = R[c]["o_ckv"]
        new_kr[2 * c:2 * c + 2] = R[c]["o_kr"]
        new_ssm[2 * c:2 * c + 2] = R[c]["o_ssm"]
    return (y_prompt, y_sample, new_ckv, new_kr, new_ssm)


def mixers(k, g, l, scoped):
    nc = k.nc
    p = k.p
    PS = g["PS"]; A = g["A"]
    BUFA = g["BUFA"]; BUFB = g["BUFB"]
    ident_f = g["ident_f"]; ident_b = g["ident_b"]; ones_b = g["ones_b"]; ones_f = g["ones_f"]; PSD = g["PSD"]; epsc = g["epsc"]
    PROJ = g["PROJ"]; KV32 = g["KV32"]
    QN, CKV, QH, KH, VE, VO, KPE = 0, 10240, 16384, 18944, 22016, 25088, 28160
    SCALE = 96 ** -0.5

    def rstd_from_ps(pi, rows, n, dim, rs, rk):
        p.op("act", lambda e: e.activation(out=rs[0:rows, 0:n], in_=PS[pi][0:rows, 0:n], func=AF.Ln, scale=1.0 / dim, bias=epsc[0:rows, 0:1]),
             r=[("ps", pi), "epsc"], w=[rk])
        p.op("act", lambda e: e.activation(out=rs[0:rows, 0:n], in_=rs[0:rows, 0:n], func=AF.Exp, scale=-0.5), r=[rk], w=[rk])

    with ExitStack() as ph:
        wuq = scoped(ph, "wuq", [128, 4, 768], BF16)
        wukv = scoped(ph, "wukv", [128, 2, 1024], BF16)
        qag = scoped(ph, "qag", [128, 4], F32)
        kvg = scoped(ph, "kvg", [128, 2], F32)
        qng = scoped(ph, "qng", [96, 1], F32)
        kng = scoped(ph, "kng", [96, 1], F32)
        protf = scoped(ph, "protf", [128, 96], F32)
        f32t = [scoped(ph, "f32t0", [128, 2, 512], F32)] * 2
        esum = [f32t[0][:, 0, :], f32t[0][:, 1, :]]
        sq3 = [scoped(ph, "sq3_%d" % i, [128, 512], BF16) for i in range(3)]
        rs3 = [scoped(ph, "rs3_%d" % i, [128, 512], F32) for i in range(3)]
        tb3 = [scoped(ph, "tb3_%d" % i, [128, 512], F32) for i in range(3)]
        ta2 = [scoped(ph, "ta2_%d" % i, [128, 512], F32) for i in range(2)]
        otile = scoped(ph, "otile", [128, 4, 256], F32)
        Et = [scoped(ph, "Et%d" % i, [128, 1024], BF16) for i in range(2)]
        Et.append(otile[:].rearrange("p a b -> p (a b)").bitcast(BF16)[:, 0:1024])
        etflag = [False]
        esflag = [False]
        dblr = rot([1, 2, 3])
        krt = scoped(ph, "krt", [128, 4, 32], F32)
        cct = otile
        ckr = tb3[1][:, 0:384].rearrange("p (a b) -> p a b", a=4)
        kpf = tb3[0]
        sqi = [0]; rsi = [0]; fi = [0]; ei = [0]
        psr = rot([4, 5, 6, 7])

        st_uq = BUFB[:, 4 * T:4 * T + 6144].bitcast(F32).rearrange("p (k n) -> p k n", k=4)
        st_ukv = BUFB[:, 4 * T + 6144:4 * T + 10240].bitcast(F32).rearrange("p (k n) -> p k n", k=2)
        st_cos = BUFB[:, 4 * T + 10240:4 * T + 14336].bitcast(F32)
        st_sin = BUFB[:, 4 * T + 14336:4 * T + 18432].bitcast(F32)
        p.dma("sp", st_uq, g["w_uq"][l].rearrange("(kc p) n -> p kc n", p=128), w=["st_uq"])
        p.dma("sp", st_ukv, g["w_ukv"][l].rearrange("(kc p) n -> p kc n", p=128), w=["st_ukv"])
        p.op("act", lambda e: e.copy(out=wuq[:], in_=st_uq), r=["st_uq"], w=["wuq"])
        p.op("pool", lambda e: e.tensor_copy(out=wukv[:], in_=st_ukv), r=["st_ukv"], w=["wukv"])
        p.dma("sp", qag[:], g["qagT"][l], w=["qag"])
        p.dma("sp", kvg[:], g["kvgT"][l], w=["kvg"])
        p.dma("sp", qng[:], g["qng"][l], w=["qng"])
        p.dma("sp", kng[:], g["kng"][l], w=["kng"])
        p.dma("sp", st_cos[64:96, :], g["c_cos"][:, :], w=["st_cos"])
        p.dma("sp", st_sin[64:96, :], g["c_sin"][:, :], w=["st_sin"])
        p.op("pool", lambda e: e.tensor_copy(out=BUFA[64:96, 31232:33280], in_=st_cos[64:96, :]), r=["st_cos"], w=["cos"])
        p.op("pool", lambda e: e.tensor_copy(out=BUFA[64:96, 33280:35328], in_=st_sin[64:96, :]), r=["st_sin"], w=["sin"])
        p.dma("sp", protf[:], g["c_prot"][:, :], w=["prot"])
        p.op("pool", lambda e: e.memset(BUFA[:, VE:VE + 3072].rearrange("p (j c) -> p j c", c=128)[:, :, 64:128], 1.0), w=["V"])
        p.op("pool", lambda e: e.memset(BUFA[:, VO:VO + 3072].rearrange("p (j c) -> p j c", c=128)[:, :, 0:64], 1.0), w=["V"])
        p.op("pool", lambda e: e.memset(ckr[:], 0.0), w=[("tb3", 1)])
        for c in range(4):
            p.dma("sp", A(BUFA, c, 0, T), PROJ[c], r=["PROJ"], w=[("qn", c)])
        for tt in range(NT):
            pi = psr()
            for c in range(4):
                sq = sq3[sqi[0] % 3]; sk = ("sq3", sqi[0] % 3); sqi[0] += 1
                p.op("act", lambda e, sq=sq, c=c, tt=tt: e.activation(out=sq[:], in_=A(BUFA, c, tt * 512, 512), func=AF.Square),
                     r=[("qn", c)], w=[sk])
                p.op("pe", lambda e, sq=sq, c=c, pi=pi: e.matmul(PS[pi][:, :], ones_b[:, :], sq[:], start=(c == 0), stop=(c == 3)),
                     r=[sk, "ones_b"], w=[("ps", pi)])
            rs = rs3[rsi[0] % 3]; rk = ("rs3", rsi[0] % 3); rsi[0] += 1
            rstd_from_ps(pi, 128, 512, 512.0, rs, rk)
            for c in range(4):
                p.op("dve", lambda e, c=c, tt=tt, rs=rs: e.scalar_tensor_tensor(
                    out=A(BUFA, c, tt * 512, 512), in0=A(BUFA, c, tt * 512, 512), scalar=qag[:, c:c + 1], in1=rs[:, :],
                    op0=ALU.mult, op1=ALU.mult), r=[("qn", c), rk, "qag"], w=[("qn", c)])
        for tt in range(NT):
            ft = f32t[0]; fk = ("f32t", 0); fi[0] += 1
            p.dma("sp", ft[:], KV32[0:2, :, tt * 512:(tt + 1) * 512].rearrange("c p t -> p c t"), r=["KV32"], w=[fk])
            pi = psr()
            for c in range(2):
                sq = sq3[sqi[0] % 3]; sk = ("sq3", sqi[0] % 3); sqi[0] += 1
                p.op("act", lambda e, sq=sq, c=c, ft=ft: e.activation(out=sq[:], in_=ft[:, c, :], func=AF.Square), r=[fk], w=[sk])
                p.op("pe", lambda e, sq=sq, c=c, pi=pi: e.matmul(PS[pi][:, :], ones_b[:, :], sq[:], start=(c == 0), stop=(c == 1)),
                     r=[sk, "ones_b"], w=[("ps", pi)])
            rs = rs3[rsi[0] % 3]; rk = ("rs3", rsi[0] % 3); rsi[0] += 1
            rstd_from_ps(pi, 128, 512, 256.0, rs, rk)
            for c in range(2):
                p.op("dve", lambda e, c=c, ft=ft, rs=rs: e.scalar_tensor_tensor(
                    out=ft[:, c, :], in0=ft[:, c, :], scalar=kvg[:, c:c + 1], in1=rs[:, :], op0=ALU.mult, op1=ALU.mult),
                    r=[fk, rk, "kvg"], w=[fk])
                p.op("act", lambda e, c=c, ft=ft, tt=tt: e.copy(
                    out=BUFA[:, CKV + c * TK + tt * 512: CKV + c * TK + (tt + 1) * 512], in_=ft[:, c, :]), r=[fk], w=["ckvT"])
            if tt == 0:
                for t4 in range(4):
                    pi2 = psr()
                    for c in range(2):
                        p.op("pe", lambda e, pi2=pi2, c=c, t4=t4, ft=ft: e.transpose(
                            PS[pi2][:, c * 128:(c + 1) * 128], ft[:, c, t4 * 128:(t4 + 1) * 128], ident_f[:]),
                            r=[fk, "ident_f"], w=[("ps", pi2)])
                    p.op("dve", lambda e, pi2=pi2, t4=t4: e.tensor_copy(out=otile[:, t4, :], in_=PS[pi2][:, 0:256]),
                         r=[("ps", pi2)], w=["otile"])
                for b_ in range(2):
                    p.dma("sp", g["o_ckv"][b_, l, :, :].rearrange("(h p) f -> p h f", p=128), otile[:, 2 * b_:2 * b_ + 2, :], r=["otile"], w=["o_ckv"])
        for tt in range(NT):
            p.dma("sp", kpf[64:96, :], KV32[2, 64:96, tt * 512:(tt + 1) * 512], r=["KV32"], w=[("tb3", 0)])
            p.op("dve", lambda e, tt=tt: e.tensor_copy(out=BUFA[64:96, KPE + tt * 512: KPE + (tt + 1) * 512], in_=kpf[64:96, :]),
                 r=[("tb3", 0)], w=["kpe"])
            if tt == 0:
                pi2 = psr()
                for t4 in range(4):
                    p.op("pe", lambda e, pi2=pi2, t4=t4: e.transpose(
                        PS[pi2][:, t4 * 32:(t4 + 1) * 32], kpf[64:96, t4 * 128:(t4 + 1) * 128], ident_f[64:96, 64:96]),
                        r=[("tb3", 0), "ident_f"], w=[("ps", pi2)])
                p.op("dve", lambda e, pi2=pi2: e.tensor_copy(out=krt[:], in_=PS[pi2][:, 0:128].rearrange("p (a b) -> p a b", a=4)),
                     r=[("ps", pi2)], w=["krt"])
                for b_ in range(2):
                    p.dma("sp", g["o_kr"][b_, l, :, :].rearrange("(h p) f -> p h f", p=128), krt[:, 2 * b_:2 * b_ + 2, :], r=["krt"], w=["o_kr"])
        p.dma("sp", cct[:], g["cache_ckv"][l].rearrange("(a p) f -> p a f", p=128), w=["otile"])
        p.dma("sp", ckr[:, :, 64:96], g["cache_kr"][l].rearrange("(a p) f -> p a f", p=128), r=[], w=[("tb3", 1)])
        for c in range(2):
            pi2 = psr()
            for t4 in range(4):
                p.op("pe", lambda e, pi2=pi2, c=c, t4=t4: e.transpose(
                    PS[pi2][:, t4 * 128:(t4 + 1) * 128], cct[:, t4, c * 128:(c + 1) * 128], ident_f[:]),
                    r=["otile", "ident_f"], w=[("ps", pi2)])
            p.op("act", lambda e, pi2=pi2, c=c: e.copy(out=BUFA[:, CKV + c * TK + T: CKV + c * TK + T + 512], in_=PS[pi2][:, :]),
                 r=[("ps", pi2)], w=["ckvT"])
        pi2 = psr()
        for t4 in range(4):
            p.op("pe", lambda e, pi2=pi2, t4=t4: e.transpose(
                PS[pi2][0:96, t4 * 128:(t4 + 1) * 128], ckr[:, t4, :], ident_f[:]), r=[("tb3", 1), "ident_f"], w=[("ps", pi2)])
        p.op("dve", lambda e, pi2=pi2: e.tensor_copy(out=BUFA[64:96, KPE + T: KPE + T + 512], in_=PS[pi2][64:96, :]),
             r=[("ps", pi2)], w=["kpe"])

        COSO, SINO, KPS = 31232, 33280, 35328
        for ti in range(6):
            p.op("act", lambda e, ti=ti: e.activation(out=BUFA[64:96, KPS + ti * 512:KPS + (ti + 1) * 512],
                                                     in_=BUFA[64:96, KPE + ti * 512:KPE + (ti + 1) * 512], func=AF.Square),
                 r=["kpe"], w=["kps"])
        psr8 = rot(list(range(8)))
        ucnt = [0]

        class U_:
            pass

        def stageA(u):
            pi = psr8(); u.pi = pi
            i3 = ucnt[0] % 3; ucnt[0] += 1
            u.i3 = i3
            sq = sq3[i3]; sk = ("sq3", i3)
            if u.kind == "q":
                for kc in range(4):
                    p.op("pe", lambda e, pi=pi, kc=kc, ti=u.ti, h=u.h: e.matmul(
                        PS[pi][0:96, :], wuq[:, kc, h * 96:(h + 1) * 96], A(BUFA, kc, ti * 512, 512),
                        start=(kc == 0), stop=(kc == 3)), r=["wuq", ("qn", kc)], w=[("ps", pi)])
                u.rows = 96
            else:
                for kc in range(2):
                    p.op("pe", lambda e, pi=pi, kc=kc, ti=u.ti, h=u.h: e.matmul(
                        PS[pi][0:64, :], wukv[:, kc, h * 128:h * 128 + 64],
                        BUFA[:, CKV + kc * TK + ti * 512: CKV + kc * TK + (ti + 1) * 512],
                        start=(kc == 0), stop=(kc == 1)), r=["wukv", "ckvT"], w=[("ps", pi)])
                u.rows = 64
            rows = u.rows
            p.op("act", lambda e, pi=pi, sq=sq, rows=rows: e.activation(out=sq[0:rows, :], in_=PS[pi][0:rows, :], func=AF.Square),
                 r=[("ps", pi)], w=[sk])

        def stageB(u):
            pi, i3 = u.pi, u.i3
            sq = sq3[i3]; sk = ("sq3", i3)
            rs = rs3[i3]; rk = ("rs3", i3)
            pj = psr8()
            if u.kind == "q":
                p.op("pe", lambda e, sq=sq, pj=pj: e.matmul(PS[pj][0:96, :], ones_b[0:96, 0:96], sq[0:96, :], start=True, stop=True),
                     r=[sk, "ones_b"], w=[("ps", pj)])
                gv, gkey, dst, dk = qng, "qng", QH, "qkq"
                do_rope = u.ti >= 1
            else:
                p.op("pe", lambda e, sq=sq, pj=pj: e.matmul(PS[pj][0:96, :], ones_b[0:64, 0:96], sq[0:64, :], start=True, stop=False),
                     r=[sk, "ones_b"], w=[("ps", pj)])
                p.op("pe", lambda e, pj=pj, ti=u.ti: e.matmul(PS[pj][0:96, :], ones_b[64:96, 0:96],
                                                              BUFA[64:96, KPS + ti * 512:KPS + (ti + 1) * 512], start=False, stop=True),
                     r=["kps", "ones_b"], w=[("ps", pj)])
                gv, gkey, dst, dk = kng, "kng", KH, "qkk"
                do_rope = 1 <= u.ti <= 4
            rstd_from_ps(pj, 96, 512, 96.0, rs, rk)
            u.do_rope = do_rope
            u.dcol = dst + u.ti * 512
            u.dk = dk
            dcol = u.dcol
            if do_rope:
                tb = tb3[i3]; tk_ = ("tb3", i3)
                outf = lambda r0, r1, tb=tb: tb[r0:r1, :]
                wkeys = [tk_]
            else:
                outf = lambda r0, r1, dcol=dcol: BUFA[r0:r1, dcol:dcol + 512]
                wkeys = [dk]
            if u.kind == "q":
                p.op("dve", lambda e, rs=rs, gv=gv, pi=pi, outf=outf: e.scalar_tensor_tensor(
                    out=outf(0, 96), in0=PS[pi][0:96, :], scalar=gv[0:96, 0:1], in1=rs[0:96, :],
                    op0=ALU.mult, op1=ALU.mult), r=[("ps", pi), rk, gkey], w=wkeys)
            else:
                p.op("dve", lambda e, rs=rs, gv=gv, pi=pi, outf=outf: e.scalar_tensor_tensor(
                    out=outf(0, 64), in0=PS[pi][0:64, :], scalar=gv[0:64, 0:1], in1=rs[0:64, :],
                    op0=ALU.mult, op1=ALU.mult), r=[("ps", pi), rk, gkey], w=wkeys)
                p.op("dve", lambda e, rs=rs, gv=gv, ti=u.ti, outf=outf: e.scalar_tensor_tensor(
                    out=outf(64, 96), in0=BUFA[64:96, KPE + ti * 512:KPE + (ti + 1) * 512], scalar=gv[64:96, 0:1], in1=rs[64:96, :],
                    op0=ALU.mult, op1=ALU.mult), r=["kpe", rk, gkey], w=wkeys)

        def stageC(u):
            if not u.do_rope:
                return
            i3 = u.i3
            tb = tb3[i3]; tk_ = ("tb3", i3)
            ta = ta2[i3 % 2]; tak = ("ta2", i3 % 2)
            ci = u.ti - 1
            dcol = u.dcol
            dk = u.dk
            pr = psr8()
            p.op("pe", lambda e, pr=pr, tb=tb: e.matmul(PS[pr][0:96, :], protf[64:96, 0:96], tb[64:96, :], start=True, stop=True),
                 r=[tk_, "prot"], w=[("ps", pr)])
            p.op("act", lambda e, dcol=dcol, tb=tb: e.copy(out=BUFA[0:64, dcol:dcol + 512], in_=tb[0:64, :]), r=[tk_], w=[dk])
            p.op("dve", lambda e, ci=ci, tb=tb, ta=ta: e.tensor_tensor(out=ta[64:96, :], in0=tb[64:96, :],
                                                                     in1=BUFA[64:96, COSO + ci * 512:COSO + (ci + 1) * 512], op=ALU.mult),
                 r=[tk_, "cos"], w=[tak])
            p.op("dve", lambda e, ci=ci, pr=pr, tb=tb: e.tensor_tensor(out=tb[64:96, :], in0=PS[pr][64:96, :],
                                                                     in1=BUFA[64:96, SINO + ci * 512:SINO + (ci + 1) * 512], op=ALU.mult),
                 r=[("ps", pr), "sin", tk_], w=[tk_])
            p.op("dve", lambda e, dcol=dcol, ta=ta, tb=tb: e.tensor_tensor(out=BUFA[64:96, dcol:dcol + 512], in0=ta[64:96, :], in1=tb[64:96, :], op=ALU.add),
                 r=[tak, tk_], w=[dk])

        for h in range(8):
            units = []
            for (kind, ti) in [("q", tt) for tt in range(NT)] + [("k", kt) for kt in range(6)]:
                u = U_(); u.kind = kind; u.ti = ti; u.h = h
                units.append(u)
            nU = len(units)
            for st_ in range(nU + 2):
                if st_ < nU:
                    stageA(units[st_])
                if 0 <= st_ - 1 < nU:
                    stageB(units[st_ - 1])
                if 0 <= st_ - 2 < nU:
                    stageC(units[st_ - 2])
            voff = (VE if h % 2 == 0 else VO)
            c0 = (h % 2) * 64
            for g3 in range(3):
                pi = psr()
                for j in range(8):
                    kt = g3 * 8 + j
                    for kc in range(2):
                        p.op("pe", lambda e, pi=pi, j=j, kt=kt, kc=kc, h=h: e.matmul(
                            PS[pi][:, j * 64:(j + 1) * 64], BUFA[:, CKV + kc * TK + kt * 128: CKV + kc * TK + (kt + 1) * 128],
                            wukv[:, kc, h * 128 + 64:h * 128 + 128], start=(kc == 0), stop=(kc == 1)),
                            r=["wukv", "ckvT"], w=[("ps", pi)])
                dstv = BUFA[:, voff + g3 * 1024: voff + (g3 + 1) * 1024].rearrange("p (j c) -> p j c", c=128)[:, :, c0:c0 + 64]
                p.op("act", lambda e, pi=pi, dstv=dstv: e.copy(out=dstv, in_=PS[pi][:, :].rearrange("p (j c) -> p j c", c=64)),
                     r=[("ps", pi)], w=["V"])
            r0 = c0
            jobs = [(0, 256, [0, 1]), (256, 256, [2, 3])] + [(512 * tt, 512, list(range(4, 24))) for tt in range(1, 5)]
            for ji, (q0, nq, kts) in enumerate(jobs):
                po, pd = (0, 1)
                pairs = [(kts[i], kts[i + 1]) for i in range(0, len(kts), 2)]
                npair = len(pairs)

                def emitS(pi_, q0=q0, nq=nq, pairs=pairs):
                    dbl = dblr()
                    for hf in range(2):
                        kt = pairs[pi_][hf]
                        p.op("pe", lambda e, dbl=dbl, hf=hf, kt=kt, q0=q0, nq=nq: e.matmul(
                            PS[2 * dbl + hf][:, 0:nq], BUFA[0:96, KH + kt * 128: KH + (kt + 1) * 128], BUFA[0:96, QH + q0: QH + q0 + nq],
                            start=True, stop=True), r=["qkq", "qkk"], w=[("ps", 2 * dbl + hf)])
                    return dbl
                pend = [emitS(0)]
                if npair > 1:
                    pend.append(emitS(1))
                if npair > 2:
                    pend.append(emitS(2))
                first = {"dve": True, "pool": True}
                used = []
                for pi_ in range(npair):
                    dbl = pend[pi_]
                    E = Et[ei[0] % 3]; ek = ("E", ei[0] % 3); ei[0] += 1
                    extra = ["otile"] if (E is Et[2] and not etflag[0]) else []
                    if extra:
                        etflag[0] = True
                    p.op("act", lambda e, dbl=dbl, E=E, nq=nq: e.activation(
                        out=E[:, :].rearrange("p (h c) -> p h c", h=2)[:, :, 0:nq],
                        in_=PSD[dbl][:, :].rearrange("p (h c) -> p h c", h=2)[:, :, 0:nq], func=AF.Exp, scale=SCALE),
                        r=[("ps", 2 * dbl), ("ps", 2 * dbl + 1)], w=[ek] + extra)
                    for hf in range(2):
                        kt = pairs[pi_][hf]
                        first_mm = (pi_ == 0 and hf == 0)
                        last_mm = (pi_ == npair - 1 and hf == 1)
                        p.op("pe", lambda e, E=E, kt=kt, nq=nq, po=po, hf=hf, first_mm=first_mm, last_mm=last_mm, voff=voff: e.matmul(
                            PS[po][:, 0:nq], BUFA[:, voff + kt * 128: voff + (kt + 1) * 128], E[:, hf * 512:hf * 512 + nq],
                            start=first_mm, stop=last_mm), r=[ek, "V"], w=[("ps", po)])
                    if pi_ + 3 < npair:
                        pend.append(emitS(pi_ + 3))
                oh = 64 - r0
                dsb = tb3[ji % 3]; dk_ = ("tb3", ji % 3)
                p.op("act", lambda e, dsb=dsb, po=po, oh=oh, nq=nq: e.copy(out=dsb[oh:oh + 64, 0:nq], in_=PS[po][oh:oh + 64, 0:nq]),
                     r=[("ps", po)], w=[dk_])
                p.op("pe", lambda e, dsb=dsb, pd=pd, oh=oh, r0=r0, nq=nq: e.matmul(
                    PS[pd][r0:r0 + 64, 0:nq], ident_f[oh:oh + 64, oh:oh + 64], dsb[oh:oh + 64, 0:nq], start=True, stop=True),
                    r=[dk_, "ident_f"], w=[("ps", pd)])
                rs = rs3[rsi[0] % 3]; rk = ("rs3", rsi[0] % 3); rsi[0] += 1
                p.op("act", lambda e, rs=rs, pd=pd, nq=nq, r0=r0: e.activation(out=rs[r0:r0 + 64, 0:nq], in_=PS[pd][r0:r0 + 64, 0:nq], func=AF.Ln),
                     r=[("ps", pd)], w=[rk])
                p.op("act", lambda e, rs=rs, nq=nq, r0=r0: e.activation(out=rs[r0:r0 + 64, 0:nq], in_=rs[r0:r0 + 64, 0:nq], func=AF.Exp, scale=-1.0),
                     r=[rk], w=[rk])
                p.op("dve", lambda e, rs=rs, po=po, nq=nq, r0=r0, q0=q0, h=h: e.tensor_tensor(
                    out=BUFB[r0:r0 + 64, (h // 2) * T + q0:(h // 2) * T + q0 + nq], in0=PS[po][r0:r0 + 64, 0:nq],
                    in1=rs[r0:r0 + 64, 0:nq], op=ALU.mult), r=[("ps", po), rk], w=["BI"])
    p.barrier()
    if k.stop_after == "MLA":
        return
    mixers2(k, g, l, scoped)


PADW = T + 64


def pcol(t):
    for si, (s0, n) in enumerate(SEGS):
        if s0 <= t < s0 + n:
            return t + 16 * si + 8
    raise ValueError


def mixers2(k, g, l, scoped):
    nc = k.nc
    p = k.p
    PS = g["PS"]; A = g["A"]
    BUFA = g["BUFA"]; BUFB = g["BUFB"]
    PROJ = g["PROJ"]
    psr = rot(list(range(8)))
    with ExitStack() as ph:
        cw = scoped(ph, "cw", [128, 4, 3], F32)
        cb = scoped(ph, "cb", [128, 4], F32)
        ub = [scoped(ph, "cu0", [128, 3, T], BF16)] * 2
        v = scoped(ph, "cv", [128, T], F32)
        acc = scoped(ph, "cacc", [128, T], F32)
        p.dma("sp", cw[:], g["conv_wT"][l], w=["cw"])
        p.dma("sp", cb[:], g["conv_bT"][l], w=["cb"])
        for j in range(4):
            u = ub[0]; uk = ("cu", 0)
            for i3, c in enumerate((4 + j, 8 + j, 12 + j)):
                p.dma("sp", u[:, i3, :], PROJ[c], r=["PROJ"], w=[uk])
            p.op("dve", lambda e, u=u: e.tensor_tensor(out=v[:], in0=u[:, 2, :], in1=u[:, 0, :], op=ALU.mult), r=[uk], w=["cv"])
            p.op("dve", lambda e, j=j: e.tensor_scalar(out=acc[:], in0=v[:], scalar1=cw[:, j, 1:2], scalar2=cb[:, j:j + 1],
                                                      op0=ALU.mult, op1=ALU.add), r=["cv", "cw", "cb"], w=["cacc"])
            for (s0, n) in SEGS:
                p.op("dve", lambda e, j=j, s0=s0, n=n: e.scalar_tensor_tensor(
                    out=acc[:, s0 + 1:s0 + n], in0=v[:, s0:s0 + n - 1], scalar=cw[:, j, 0:1], in1=acc[:, s0 + 1:s0 + n],
                    op0=ALU.mult, op1=ALU.add), r=["cv", "cacc", "cw"], w=["cacc"])
                p.op("dve", lambda e, j=j, s0=s0, n=n: e.scalar_tensor_tensor(
                    out=acc[:, s0:s0 + n - 1], in0=v[:, s0 + 1:s0 + n], scalar=cw[:, j, 2:3], in1=acc[:, s0:s0 + n - 1],
                    op0=ALU.mult, op1=ALU.add), r=["cv", "cacc", "cw"], w=["cacc"])
            p.op("dve", lambda e, j=j, u=u: e.tensor_tensor(out=A(BUFB, 4 + j, 0, T), in0=acc[:], in1=u[:, 1, :], op=ALU.mult),
                 r=["cacc", uk], w=["BI"])
    p.barrier()
    with ExitStack() as ph:
        pu = scoped(ph, "pu", [128, PADW], BF16)
        w2 = scoped(ph, "pw2", [128, PADW], F32)
        w4 = scoped(ph, "pw4", [128, PADW], F32)
        inv = scoped(ph, "pinv", [128, T], F32)
        pm = scoped(ph, "pm", [128, T], BF16)
        pw = scoped(ph, "pw", [128, 4, 128], BF16)
        psc = scoped(ph, "psc", [128, 4], F32)
        p.dma("pool", pw[:], g["pool_w"][l].rearrange("g c d -> c g d"), w=["pw"])
        p.dma("sp", psc[:], g["pool_sT"][l], w=["psc"])
        p.op("pool", lambda e: e.memset(pu[:], 0.0), w=["pu"])
        p.op("pool", lambda e: e.memset(w2[:], 0.0), w=["pw2"])
        p.op("pool", lambda e: e.memset(w4[:], 0.0), w=["pw4"])
        W_ = PADW
        for gi in range(4):
            for (s0, n) in SEGS:
                p.dma("sp", pu[:, pcol(s0):pcol(s0) + n], PROJ[16 + gi, :, s0:s0 + n], r=["PROJ"], w=["pu"])
            p.dma("sp", inv[:], g["c_pinv"][gi], w=["pinv"])
            p.op("dve", lambda e: e.tensor_tensor(out=w2[:, 1:W_], in0=pu[:, 0:W_ - 1], in1=pu[:, 1:W_], op=ALU.add), r=["pu", "pw4"], w=["pw2"])
            cur, curk, oth, othk = w2, "pw2", w4, "pw4"
            sh = 1
            for lev in range(gi):
                p.op("dve", lambda e, cur=cur, oth=oth, sh=sh: e.tensor_tensor(
                    out=oth[:, sh:W_ - sh], in0=cur[:, 0:W_ - 2 * sh], in1=cur[:, 2 * sh:W_], op=ALU.add), r=[curk], w=[othk])
                cur, curk, oth, othk = oth, othk, cur, curk
                sh *= 2
            for (s0, n) in SEGS:
                c0 = pcol(s0)
                p.op("dve", lambda e, cur=cur, s0=s0, n=n, c0=c0: e.tensor_tensor(
                    out=cur[:, c0:c0 + n], in0=cur[:, c0:c0 + n], in1=inv[:, s0:s0 + n], op=ALU.mult), r=[curk, "pinv"], w=[curk])
                p.op("dve", lambda e, cur=cur, s0=s0, n=n, c0=c0: e.tensor_tensor(
                    out=pm[:, s0:s0 + n], in0=cur[:, c0:c0 + n], in1=pu[:, c0:c0 + n], op=ALU.subtract), r=[curk, "pu"], w=["pm"])
            for tt in range(NT):
                pi = psr()
                p.op("pe", lambda e, pi=pi, gi=gi, tt=tt: e.matmul(PS[pi][:, :], pw[:, gi, :], pm[:, tt * 512:(tt + 1) * 512], start=True, stop=True),
                     r=["pw", "pm"], w=[("ps", pi)])
                p.op("act", lambda e, pi=pi, gi=gi, tt=tt: e.activation(
                    out=A(BUFB, 12 + gi, tt * 512, 512), in_=PS[pi][:, :], func=AF.Copy, scale=psc[:, gi:gi + 1]),
                    r=[("ps", pi), "psc"], w=["BI"])
    p.barrier()
    ssm(k, g, l, scoped)
    if "DBG_BI" in k.debug:
        p.barrier()
        for c in range(16):
            p.dma("sp", k.DBG_BI[c], A(BUFB, c, 0, T), r=["BI"], w=["DBG"])


def ssm(k, g, l, scoped):
    nc = k.nc
    p = k.p
    PS = g["PS"]
    BUFA = g["BUFA"]; BUFB = g["BUFB"]
    ident_b = g["ident_b"]; CAA = g["CAA"]; CAB = g["CAB"]
    UTOK = g["UTOK"]; YTOK = g["YTOK"]
    SSM_BLT = g["SSM_BLT"]; SSM_ML = g["SSM_ML"]; SSM_CS = g["SSM_CS"]
    psr = rot(list(range(8)))
    SEQ = [(0, 16, 0), (17, 16, 16), (34, 128, 32)]
    NCOL = 163
    UL0 = 8 * T
    Sv = BUFA[:, 0:20864].bitcast(F32).rearrange("p (r d g c) -> p r d g c", r=2, d=2, g=16)

    def UL(gg, ch, q0, n):
        o = UL0 + (gg * 2 + ch) * 160 + q0
        return BUFB[:, o:o + n]

    with ExitStack() as ph:
        Sbf = scoped(ph, "Sbf", [128, 2, 2, 16, NCOL], BF16)
        blt = [scoped(ph, "sblt%d" % i, [128, 2, 128], BF16) for i in range(4)]
        mlw = [scoped(ph, "mlw%d" % i, [128, 2, 2, 256], BF16) for i in range(2)]
        csw = [scoped(ph, "csw%d" % i, [128, 2, 2, 256], BF16) for i in range(2)]
        yti = [scoped(ph, "yti%d" % i, [128, 512], BF16) for i in range(3)]
        tf = [scoped(ph, "tf%d" % i, [128, 2, 16, 2], F32) for i in range(2)]
        tb = [scoped(ph, "tb%d" % i, [128, 2, 16, 2], F32) for i in range(2)]
        p.dma("sp", BUFA[0:32, 0:8192], UTOK[0:512, :].rearrange("(q i) f -> q (i f)", i=16), r=["UTOK"], w=["UQ"])
        p.dma("sp", BUFA[:, 8192:16384], UTOK[512:2560, :].rearrange("(q i) f -> q (i f)", i=16), r=["UTOK"], w=["UQ"])
        p.op("dve", lambda e: e.tensor_copy(
            out=BUFA[0:32, 16384:24576].rearrange("p (g i c) -> p g i c", g=32, i=16),
            in_=BUFA[0:32, 0:8192].rearrange("p (i g c) -> p g i c", i=16, g=32)), r=["UQ"], w=["UQ2"])
        p.op("pool", lambda e: e.tensor_copy(
            out=BUFA[:, 24576:32768].rearrange("p (g i c) -> p g i c", g=32, i=16),
            in_=BUFA[:, 8192:16384].rearrange("p (i g c) -> p g i c", i=16, g=32)), r=["UQ"], w=["UQ2"])
        n_ = 0
        for gg in range(32):
            for ch in range(2):
                pi = psr()
                psb = PS[pi][:, 0:80].bitcast(BF16)
                o_ = gg * 256 + ch * 128
                p.op("pe", lambda e, psb=psb, o_=o_: e.transpose(psb[:, 0:32], BUFA[0:32, 16384 + o_:16384 + o_ + 128], ident_b[0:32, 0:32]),
                     r=["UQ2", "ident_b"], w=[("ps", pi)])
                p.op("pe", lambda e, psb=psb, o_=o_: e.transpose(psb[:, 32:160], BUFA[:, 24576 + o_:24576 + o_ + 128], ident_b[:, :]),
                     r=["UQ2", "ident_b"], w=[("ps", pi)])
                if n_ % 2 == 0:
                    p.op("dve", lambda e, psb=psb, gg=gg, ch=ch: e.tensor_copy(out=UL(gg, ch, 0, 160), in_=psb[:, 0:160]), r=[("ps", pi)], w=["UL"])
                else:
                    p.op("act", lambda e, psb=psb, gg=gg, ch=ch: e.copy(out=UL(gg, ch, 0, 160), in_=psb[:, 0:160]), r=[("ps", pi)], w=["UL"])
                n_ += 1
        p.op("pool", lambda e: e.memset(BUFA[:, 0:20864], 0.0), w=["UQ", "UQ2", ("S", 0), ("S", 1)])
        for d in range(2):
            col = 34 if d == 0 else 162
            for reim in range(2):
                for gh in range(2):
                    p.dma("sp", Sv[gh * 64:(gh + 1) * 64, reim, d, :, col],
                          g["state_in"][l, d, reim, gh * 16:(gh + 1) * 16, :].rearrange("g n -> n g"), w=[("S", d)], slow=True)
        bi_ = 0
        for d in range(2):
            for gg in range(32):
                gh, g16 = gg // 16, gg % 16
                r0 = gh * 64
                bt = blt[bi_ % 4]; bk = ("sblt", bi_ % 4); bi_ += 1
                p.dma("sp", bt[:], SSM_BLT[l, d, gg].rearrange("c p f -> p c f"), r=["SSM_BLT"], w=[bk])
                pi = psr()
                for reim in range(2):
                    for ch in range(2):
                        p.op("pe", lambda e, pi=pi, r0=r0, reim=reim, ch=ch, bt=bt, gg=gg: e.matmul(
                            PS[pi][r0:r0 + 64, reim * 160:(reim + 1) * 160], bt[:, ch, reim * 64:(reim + 1) * 64], UL(gg, ch, 0, 160),
                            start=(ch == 0), stop=(ch == 1)), r=[bk, "UL"], w=[("ps", pi)])
                for (b0, Q, q0) in SEQ:
                    c0 = b0 + (1 - d)
                    p.op("dve", lambda e, pi=pi, r0=r0, d=d, g16=g16, c0=c0, Q=Q, q0=q0: e.tensor_copy(
                        out=Sv[r0:r0 + 64, :, d, g16, c0:c0 + Q],
                        in_=PS[pi][r0:r0 + 64, 0:320].rearrange("p (r q) -> p r q", r=2)[:, :, q0:q0 + Q]),
                        r=[("ps", pi)], w=[("S", d)])
        if "SSMD" in k.debug and l == 0:
            d1 = nc.dram_tensor("D_SIN", [128, 10432], F32, kind="ExternalOutput").ap()
            p.dma("sp", d1, BUFA[:, 0:20864].bitcast(F32), r=[("S", 0), ("S", 1)], w=["D_SIN"])
            d2 = nc.dram_tensor("D_UL", [128, 10240], BF16, kind="ExternalOutput").ap()
            p.dma("sp", d2, BUFB[:, UL0:UL0 + 10240], r=["UL"], w=["D_UL"])
        for d, eng, tmp in ((0, "dve", tf), (1, "pool", tb)):
            key = ("S", d)
            fs = slice(d * 16, (d + 1) * 16)
            ca4 = CAA[:, l, :, fs].unsqueeze(3).to_broadcast([128, 2, 16, 2])
            cb4 = CAB[:, l, :, fs].unsqueeze(3).to_broadcast([128, 2, 16, 2])
            ca3 = CAA[:, l, :, fs]
            cb3 = CAB[:, l, :, fs]
            t1, t2 = tmp
            tk = "tmp%d" % d
            steps = range(16) if d == 0 else range(15, -1, -1)
            for q in steps:
                pc = q if d == 0 else q + 1
                ncl = q + 1 if d == 0 else q
                prev = Sv[:, :, d, :, pc:pc + 18:17]
                prsw = Sv[:, ::-1, d, :, pc:pc + 18:17]
                new = Sv[:, :, d, :, ncl:ncl + 18:17]
                p.op(eng, lambda e, prev=prev, t1=t1, ca4=ca4: e.tensor_tensor(out=t1[:], in0=prev, in1=ca4, op=ALU.mult), r=[key], w=[tk + "a"])
                p.op(eng, lambda e, prsw=prsw, t2=t2, cb4=cb4: e.tensor_tensor(out=t2[:], in0=prsw, in1=cb4, op=ALU.mult), r=[key], w=[tk + "b"])
                p.op(eng, lambda e, t1=t1, t2=t2: e.tensor_tensor(out=t1[:], in0=t1[:], in1=t2[:], op=ALU.add), r=[tk + "a", tk + "b"], w=[tk + "a"])
                p.op(eng, lambda e, new=new, t1=t1: e.tensor_tensor(out=new, in0=new, in1=t1[:], op=ALU.add), r=[tk + "a", key], w=[key])
            steps = range(128) if d == 0 else range(127, -1, -1)
            for q in steps:
                pc = 34 + (q if d == 0 else q + 1)
                ncl = 34 + (q + 1 if d == 0 else q)
                prev = Sv[:, :, d, :, pc]
                prsw = Sv[:, ::-1, d, :, pc]
                new = Sv[:, :, d, :, ncl]
                p.op(eng, lambda e, prev=prev, t1=t1, ca3=ca3: e.tensor_tensor(out=t1[:, :, :, 0], in0=prev, in1=ca3, op=ALU.mult), r=[key], w=[tk + "a"])
                p.op(eng, lambda e, prsw=prsw, t2=t2, cb3=cb3: e.tensor_tensor(out=t2[:, :, :, 0], in0=prsw, in1=cb3, op=ALU.mult), r=[key], w=[tk + "b"])
                p.op(eng, lambda e, t1=t1, t2=t2: e.tensor_tensor(out=t1[:, :, :, 0], in0=t1[:, :, :, 0], in1=t2[:, :, :, 0], op=ALU.add), r=[tk + "a", tk + "b"], w=[tk + "a"])
                p.op(eng, lambda e, new=new, t1=t1: e.tensor_tensor(out=new, in0=new, in1=t1[:, :, :, 0], op=ALU.add), r=[tk + "a", key], w=[key])
        if "SSMD" in k.debug and l == 0:
            d4 = nc.dram_tensor("D_CAA", [128, DEPTH, 2, 32], F32, kind="ExternalOutput").ap()
            p.dma("sp", d4, CAA[:], w=["D_CAA"])
            d5 = nc.dram_tensor("D_CAB", [128, DEPTH, 2, 32], F32, kind="ExternalOutput").ap()
            p.dma("sp", d5, CAB[:], w=["D_CAB"])
            d3 = nc.dram_tensor("D_S", [128, 10432], F32, kind="ExternalOutput").ap()
            p.dma("sp", d3, BUFA[:, 0:20864].bitcast(F32), r=[("S", 0), ("S", 1)], w=["D_S"])
        for b_ in range(2):
            b0 = SEQ[b_][0]
            for d in range(2):
                col = b0 + 16 if d == 0 else b0
                for reim in range(2):
                    for gh in range(2):
                        p.dma("sp", g["o_ssm"][b_, l, d, reim, gh * 16:(gh + 1) * 16, :].rearrange("g n -> n g"),
                              Sv[gh * 64:(gh + 1) * 64, reim, d, :, col], r=[("S", d)], w=["o_ssm"], slow=True)
        p.op("act", lambda e: e.copy(out=Sbf[:].rearrange("p r d g c -> p (r d g c)"), in_=BUFA[:, 0:20864].bitcast(F32)),
             r=[("S", 0), ("S", 1)], w=["Sbf"])
        p.barrier()
        n_ = 0
        for gg in range(32):
            gh, g16 = gg // 16, gg % 16
            r0 = gh * 64
            mw = mlw[gg % 2]; mk = ("mlw", gg % 2)
            cw = csw[gg % 2]; ck = ("csw", gg % 2)
            for d in range(2):
                p.dma("sp", mw[:, d], SSM_ML[l, d, gg].rearrange("c p f -> p c f"), r=["SSM_ML"], w=[mk])
                p.dma("sp", cw[r0:r0 + 64, d], SSM_CS[l, d, gg], r=["SSM_CS"], w=[ck])
            for (rows, q0, ybase, sc0) in ((16, 0, 0, 0), (16, 16, 16384, 17), (128, 32, 8192, 34)):
                pi = psr()
                mms = []
                for d in range(2):
                    for ch in range(2):
                        mms.append((UL(gg, ch, q0, rows), mw[:, d, ch, :], [mk, "UL"]))
                    for reim in range(2):
                        cc0 = sc0 + d
                        lh = Sbf[r0:r0 + 64, reim, d, g16, cc0:cc0 + rows]
                        mms.append((lh, cw[r0:r0 + 64, d, reim, :], [ck, "Sbf"]))
                for mi, (lh, rh, rk) in enumerate(mms):
                    p.op("pe", lambda e, pi=pi, lh=lh, rh=rh, mi=mi, rows=rows: e.matmul(
                        PS[pi][0:rows, 0:256], lh, rh, start=(mi == 0), stop=(mi == len(mms) - 1)), r=rk, w=[("ps", pi)])
                dst = BUFA[0:rows, ybase:ybase + 8192].rearrange("p (i g c) -> p i g c", i=16, g=32)[:, :, gg, :]
                src = PS[pi][0:rows, 0:256].rearrange("p (i c) -> p i c", i=16)
                if n_ % 2 == 0:
                    p.op("dve", lambda e, dst=dst, src=src: e.tensor_copy(out=dst, in_=src), r=[("ps", pi)], w=["YQ"])
                else:
                    p.op("act", lambda e, dst=dst, src=src: e.copy(out=dst, in_=src), r=[("ps", pi)], w=["YQ"])
                n_ += 1
        p.dma("sp", YTOK[0:256, :].rearrange("(q i) f -> q (i f)", i=16), BUFA[0:16, 0:8192], r=["YQ"], w=["YTOK"])
        p.dma("sp", YTOK[256:512, :].rearrange("(q i) f -> q (i f)", i=16), BUFA[0:16, 16384:24576], r=["YQ"], w=["YTOK"])
        p.dma("sp", YTOK[512:2560, :].rearrange("(q i) f -> q (i f)", i=16), BUFA[:, 8192:16384], r=["YQ"], w=["YTOK"])
        for ti in range(T // 128):
            yt = yti[ti % 3]; yk = ("yti", ti % 3)
            p.dma("sp", yt[:], YTOK[ti * 128:(ti + 1) * 128, :], r=["YTOK"], w=[yk])
            pi = psr()
            psb = PS[pi][:, 0:256].bitcast(BF16)
            for c in range(4):
                p.op("pe", lambda e, psb=psb, c=c, yt=yt: e.transpose(psb[:, c * 128:(c + 1) * 128], yt[:, c * 128:(c + 1) * 128], ident_b[:, :]),
                     r=[yk, "ident_b"], w=[("ps", pi)])
            dst = BUFB[:, 8 * T:12 * T].rearrange("p (c t) -> p c t", c=4)[:, :, ti * 128:(ti + 1) * 128]
            src = psb.rearrange("p (c t) -> p c t", c=4)
            if ti % 2 == 0:
                p.op("dve", lambda e, dst=dst, src=src: e.tensor_copy(out=dst, in_=src), r=[("ps", pi)], w=["BI", "UL"])
            else:
                p.op("act", lambda e, dst=dst, src=src: e.copy(out=dst, in_=src), r=[("ps", pi)], w=["BI", "UL"])
    p.barrier()


def tail(k, g, l, scoped, norm):
    nc = k.nc
    p = k.p
    PS = g["PS"]; A = g["A"]
    BUFA = g["BUFA"]; BUFB = g["BUFB"]
    modv = g["modv"]; A2 = g["A2"]
    XT = g["XT"]; GS = g["GS"]; MT = g["MT"]; AT = g["AT"]
    psr = rot(list(range(8)))

    def mod_j(tt):
        return 0 if tt == 0 else 1

    def kcp(src):
        return src.rearrange("(kc p) n -> p kc n", p=128)

    with ExitStack() as ph:
        wm = [scoped(ph, "wm%d" % i, [128, 20, 256], BF16) for i in range(2)]
        sgt = scoped(ph, "sgt", [128, 512], F32)
        acc = [scoped(ph, "macc%d" % i, [128, 512], F32) for i in range(2)]
        tmp = [scoped(ph, "mtmp%d" % i, [128, 512], F32) for i in range(3)]
        mstg = [scoped(ph, "mstg%d" % i, [128, T], BF16) for i in range(2)]
        ti_ = [0]

        def T_():
            i = ti_[0] % 3
            ti_[0] += 1
            return tmp[i], ("mtmp", i)
        stages = [BUFA[:, 20480 + i * 10240: 20480 + (i + 1) * 10240].bitcast(F32).rearrange("p (k n) -> p k n", k=20) for i in range(2)]
        pieces = []
        for fg in range(8):
            c0 = fg * 256
            pcs = []
            for bi, src in enumerate((g["w_mla_o"][l][:, c0:c0 + 256], g["w_conv_o"][l][:, c0:c0 + 256],
                                      g["w_pool_o"][l][:, c0:c0 + 256], g["w_glu"][l][:, c0:c0 + 256],
                                      g["w_glu"][l][:, 2048 + c0:2048 + c0 + 256])):
                pcs.append((src, bi * 4, 4, 0, 256))
            pieces.append(pcs)

        def compute(fg, w, wk):
            for f2 in range(2):
                fc = fg * 2 + f2
                gb = fc % 2
                gk = ("gbuf", gb)
                for i in range(4):
                    p.dma("sp", BUFA[:, (gb * 4 + i) * T:(gb * 4 + i + 1) * T], GS[i * 16 + fc], r=["GS"], w=[gk])

                def G(i, tt, gb=gb):
                    return BUFA[:, (gb * 4 + i) * T + tt * 512:(gb * 4 + i) * T + (tt + 1) * 512]
                ms = mstg[fc % 2]; mk = ("mstg", fc % 2)
                cs = f2 * 128
                for tt in range(NT):
                    banks = [psr() for _ in range(5)]
                    specs = [(0, 0), (4, 4), (12, 8), (16, 8), (8, 12)]
                    for bnk, (wi_, bch) in zip(banks, specs):
                        for kc in range(4):
                            p.op("pe", lambda e, bnk=bnk, wi_=wi_, bch=bch, kc=kc, tt=tt, w=w, cs=cs: e.matmul(
                                PS[bnk][:, :], w[:, wi_ + kc, cs:cs + 128], A(BUFB, bch + kc, tt * 512, 512),
                                start=(kc == 0), stop=(kc == 3)), r=[wk, "BI"], w=[("ps", bnk)])
                    pa, pb, pga, pgg, pd = banks
                    p.op("act", lambda e, pgg=pgg: e.activation(out=sgt[:], in_=PS[pgg][:, :], func=AF.Sigmoid),
                         r=[("ps", pgg)], w=["sgt"])
                    ac = acc[tt % 2]; ak = ("macc", tt % 2)
                    p.op("dve", lambda e, ac=ac, pa=pa, tt=tt, G=G: e.tensor_tensor(out=ac[:], in0=PS[pa][:, :], in1=G(0, tt), op=ALU.mult),
                         r=[("ps", pa), gk], w=[ak])
                    t1, t1k = T_()
                    p.op("dve", lambda e, t1=t1, pb=pb, tt=tt, G=G: e.tensor_tensor(out=t1[:], in0=PS[pb][:, :], in1=G(1, tt), op=ALU.mult),
                         r=[("ps", pb), gk], w=[t1k])
                    p.op("pool", lambda e, ac=ac, t1=t1: e.tensor_tensor(out=ac[:], in0=ac[:], in1=t1[:], op=ALU.add), r=[ak, t1k], w=[ak])
                    t2, t2k = T_()
                    p.op("dve", lambda e, t2=t2, pga=pga: e.tensor_tensor(out=t2[:], in0=PS[pga][:, :], in1=sgt[:], op=ALU.mult),
                         r=[("ps", pga), "sgt"], w=[t2k])
                    p.op("pool", lambda e, t2=t2, tt=tt, G=G: e.tensor_tensor(out=t2[:], in0=t2[:], in1=G(2, tt), op=ALU.mult), r=[t2k, gk], w=[t2k])
                    p.op("pool", lambda e, ac=ac, t2=t2: e.tensor_tensor(out=ac[:], in0=ac[:], in1=t2[:], op=ALU.add), r=[ak, t2k], w=[ak])
                    t3, t3k = T_()
                    p.op("dve", lambda e, t3=t3, pd=pd, tt=tt, G=G: e.tensor_tensor(out=t3[:], in0=PS[pd][:, :], in1=G(3, tt), op=ALU.mult),
                         r=[("ps", pd), gk], w=[t3k])
                    p.op("pool", lambda e, ac=ac, t3=t3, ms=ms, tt=tt: e.tensor_tensor(out=ms[:, tt * 512:(tt + 1) * 512], in0=ac[:], in1=t3[:], op=ALU.add),
                         r=[ak, t3k], w=[mk])
                p.dma("sp", MT[fc], ms[:], r=[mk], w=["MT"])
        WStream(p, "wm", stages, wm, cast_engs=("act",)).run(pieces, compute)
    p.barrier()
    if "DBG_MT" in k.debug:
        return
    with ExitStack() as ph:
        wbs = [scoped(ph, "wo%d" % i, [128, 16, 256], BF16) for i in range(2)]
        xs = [scoped(ph, "xs%d" % i, [128, T], F32) for i in range(2)]
        for c in range(16):
            p.dma("sp", A(BUFA, c, 0, T), MT[c], r=["MT"], w=["BUF"])
        stages = [BUFB[:, i * 8192:(i + 1) * 8192].bitcast(F32).rearrange("p (k n) -> p k n", k=16) for i in range(4)]
        pieces = [[(g["w_o"][l][:, fg * 256:(fg + 1) * 256], 0, 16, 0, 256)] for fg in range(8)]

        def compute(fg, wb, wk):
            for f2 in range(2):
                fc = fg * 2 + f2
                x = xs[fc % 2]; xk = ("xs", fc % 2)
                p.dma("sp", x[:], XT[0][fc], r=["XT0"], w=[xk])
                for tt in range(NT):
                    pi = psr()
                    j = mod_j(tt)
                    for kc in range(16):
                        p.op("pe", lambda e, pi=pi, wb=wb, kc=kc, tt=tt, f2=f2: e.matmul(
                            PS[pi][:, :], wb[:, kc, f2 * 128:(f2 + 1) * 128], A(BUFA, kc, tt * 512, 512),
                            start=(kc == 0), stop=(kc == 15)), r=[wk, "BUF"], w=[("ps", pi)])
                    p.op("dve", lambda e, pi=pi, x=x, tt=tt, fc=fc, j=j: e.scalar_tensor_tensor(
                        out=x[:, tt * 512:(tt + 1) * 512], in0=PS[pi][:, :], scalar=modv[:, l, 32 + fc, j:j + 1],
                        in1=x[:, tt * 512:(tt + 1) * 512], op0=ALU.mult, op1=ALU.add), r=[("ps", pi), xk, "modv"], w=[xk])
                p.dma("sp", XT[1][fc], x[:], r=[xk], w=["XT1"])
        WStream(p, "wo", stages, wbs).run(pieces, compute)
    p.barrier()
    if "DBG_X1" in k.debug:
        return
    norm(XT[1], "XT1", A2, l, 48, BUFB)
    with ExitStack() as ph:
        wbs = [scoped(ph, "w1_%d" % i, [128, 16, 256], BF16) for i in range(3)]
        stg = [scoped(ph, "astg%d" % i, [128, T], BF16) for i in range(2)]
        rl = [scoped(ph, "rl%d" % i, [128, 512], F32) for i in range(2)]
        ri = [0]
        stages = [BUFA[:, i * 8192:(i + 1) * 8192].bitcast(F32).rearrange("p (k n) -> p k n", k=16) for i in range(4)]
        pieces = [[(g["w_mlp1"][l][:, hg * 256:(hg + 1) * 256], 0, 16, 0, 256)] for hg in range(32)]

        def compute(hg, wb, wk):
            for f2 in range(2):
                hc = hg * 2 + f2
                sb_ = stg[hc % 2]; sk = ("astg", hc % 2)
                for tt in range(NT):
                    pi = psr()
                    for kc in range(16):
                        p.op("pe", lambda e, pi=pi, wb=wb, kc=kc, tt=tt, f2=f2: e.matmul(
                            PS[pi][:, :], wb[:, kc, f2 * 128:(f2 + 1) * 128], A(BUFB, kc, tt * 512, 512),
                            start=(kc == 0), stop=(kc == 15)), r=[wk, "BUF"], w=[("ps", pi)])
                    r_ = rl[ri[0] % 2]; rk = ("rl", ri[0] % 2); ri[0] += 1
                    p.op("act", lambda e, pi=pi, r_=r_: e.activation(out=r_[:], in_=PS[pi][:, :], func=AF.Relu), r=[("ps", pi)], w=[rk])
                    p.op("dve", lambda e, r_=r_, sb_=sb_, tt=tt: e.tensor_tensor(out=sb_[:, tt * 512:(tt + 1) * 512], in0=r_[:], in1=r_[:], op=ALU.mult),
                         r=[rk], w=[sk])
                p.dma("sp", AT[hc], sb_[:], r=[sk], w=["AT"])
        WStream(p, "w1", stages, wbs, cast_engs=("pool", "act")).run(pieces, compute)
    p.barrier()
    W2B = g["W2B"]
    with ExitStack() as ph:
        x1t = [scoped(ph, "x1t%d" % i, [128, 512], F32) for i in range(3)]
        xi = [0]
        wbf = [BUFB[:, 32768:40960].rearrange("p (k n) -> p k n", k=64), BUFA[:, 32768:40960].rearrange("p (k n) -> p k n", k=64)]
        stg2 = [BUFB[:, i * 16384:(i + 1) * 16384].bitcast(F32).rearrange("p (k n) -> p k n", k=64) for i in range(2)]
        for tt in range(NT):
            j = mod_j(tt)
            p.dma("sp", BUFA[:, 0:64 * 512].rearrange("p (c t) -> p c t", t=512),
                  AT[:, :, tt * 512:(tt + 1) * 512].rearrange("c p t -> p c t"), r=["AT"], w=["atile"])

            def compute(fc, wv, wk, tt=tt, j=j):
                if tt == 0:
                    p.dma("sp", W2B[fc], wv.rearrange("p k n -> p (k n)"), r=[wk], w=[("W2B", fc)])
                xt_ = x1t[xi[0] % 3]; xk = ("x1t", xi[0] % 3); xi[0] += 1
                p.dma("sp", xt_[:], XT[1][fc][:, tt * 512:(tt + 1) * 512], r=["XT1"], w=[xk])
                pi = psr()
                for kc in range(64):
                    p.op("pe", lambda e, pi=pi, wv=wv, kc=kc: e.matmul(
                        PS[pi][:, :], wv[:, kc, :], BUFA[:, kc * 512:(kc + 1) * 512],
                        start=(kc == 0), stop=(kc == 63)), r=[wk, "atile"], w=[("ps", pi)])
                p.op("dve", lambda e, pi=pi, xt_=xt_, fc=fc, j=j: e.scalar_tensor_tensor(
                    out=xt_[:], in0=PS[pi][:, :], scalar=modv[:, l, 80 + fc, j:j + 1], in1=xt_[:],
                    op0=ALU.mult, op1=ALU.add), r=[("ps", pi), xk, "modv"], w=[xk])
                p.dma("sp", XT[0][fc][:, tt * 512:(tt + 1) * 512], xt_[:], r=[xk], w=["XT0"])
            if tt == 0:
                pieces = [[(g["w_mlp2"][l][:, fc * 128:(fc + 1) * 128], 0, 64, 0, 128)] for fc in range(16)]
                WStream(p, "w2", stg2, wbf, cast_engs=("act", "pool")).run(pieces, compute)
            else:
                def ld(fc):
                    p.dma("sp", wbf[fc % 2].rearrange("p k n -> p (k n)"), W2B[fc], r=[("W2B", fc)], w=[("w2wb", fc % 2)])
                ld(0)
                for fc in range(16):
                    if fc + 1 < 16:
                        ld(fc + 1)
                    compute(fc, wbf[fc % 2], ("w2wb", fc % 2))
    p.barrier()


def ssm_gen(k, g, l, gin):
    nc = k.nc
    p = k.p
    PS = g["PS"]
    ident_f = g["ident_f"]
    CAA = g["CAA"]; CAB = g["CAB"]
    TWO_PI = 2.0 * math.pi
    uid = [0]
    with ExitStack() as ph:
        def t_(name, shape, dt=F32):
            uid[0] += 1
            return ph.enter_context(nc.sbuf_tensor("g%d_%d_%s" % (l, uid[0], name), list(shape), dt))
        lr, li, ls, Bre, Bim, Cre, Cim, mF, mB, dI, dvec = gin[l]
        IK = [(nm, l) for nm in ("lr", "li", "ls", "Bre", "Bim", "Cre", "Cim")] + ["mF", "mB", "dI", "dvec"]

        K_ = ["gen"]

        def V(fn):
            p.op("dve", fn, r=K_ + IK, w=K_)

        def ACT(fn):
            p.op("act", fn, r=K_ + IK, w=K_)

        def tt(out, a, b, op):
            V(lambda e: e.tensor_tensor(out=out, in0=a, in1=b, op=op))

        step = t_("step", [128, 32]); a_ = t_("a", [128, 32]); th = t_("th", [128, 32])
        mag = t_("mag", [128, 32]); imag = t_("imag", [128, 32])
        r_ = t_("r", [128, 32]); ri = t_("ri", [128, 32], mybir.dt.int32); rf = t_("rf", [128, 32])
        f_ = t_("f", [128, 32]); fc = t_("fc", [128, 32]); m_ = t_("m", [128, 32])
        sinv = t_("sinv", [128, 32]); cosv = t_("cosv", [128, 32])
        ACT(lambda e: e.activation(out=step[:], in_=ls[:], func=AF.Exp))
        tt(a_[:], lr[:], step[:], ALU.mult)
        tt(th[:], li[:], step[:], ALU.mult)
        ACT(lambda e: e.activation(out=mag[:], in_=a_[:], func=AF.Exp))
        ACT(lambda e: e.activation(out=imag[:], in_=a_[:], func=AF.Exp, scale=-1.0))
        V(lambda e: e.tensor_scalar(out=r_[:], in0=th[:], scalar1=1.0 / TWO_PI, scalar2=None, op0=ALU.mult))
        V(lambda e: e.tensor_copy(out=ri[:], in_=r_[:]))
        V(lambda e: e.tensor_copy(out=rf[:], in_=ri[:]))
        tt(f_[:], r_[:], rf[:], ALU.subtract)
        V(lambda e: e.tensor_scalar(out=fc[:], in0=f_[:], scalar1=0.25, scalar2=None, op0=ALU.add))
        V(lambda e: e.tensor_scalar(out=m_[:], in0=fc[:], scalar1=0.5, scalar2=None, op0=ALU.is_ge))
        tt(fc[:], fc[:], m_[:], ALU.subtract)
        ACT(lambda e: e.activation(out=sinv[:], in_=f_[:], func=AF.Sin, scale=TWO_PI))
        ACT(lambda e: e.activation(out=cosv[:], in_=fc[:], func=AF.Sin, scale=TWO_PI))
        PPr = t_("PPr", [128, 32, 17]); PPi = t_("PPi", [128, 32, 17])
        PNr = t_("PNr", [128, 32, 17]); PNi = t_("PNi", [128, 32, 17])
        tA = t_("tA", [128, 16 * 256]); tB = t_("tB", [128, 16 * 256])

        def cmul(outr, outi, xr, xi, yr, yi, shape):
            n = 1
            for s_ in shape[1:]:
                n *= s_
            pat = {2: None, 3: "p (a b) -> p a b", 4: "p (a b c) -> p a b c"}[len(shape)]

            def view(t):
                v = t[:, 0:n]
                if len(shape) == 3:
                    return v.rearrange(pat, a=shape[1])
                if len(shape) == 4:
                    return v.rearrange(pat, a=shape[1], b=shape[2])
                return v
            ta = view(tA); tb = view(tB)
            tt(ta, xr, yr, ALU.mult)
            tt(tb, xi, yi, ALU.mult)
            tt(outr, ta, tb, ALU.subtract)
            tt(ta, xr, yi, ALU.mult)
            tt(tb, xi, yr, ALU.mult)
            tt(outi, ta, tb, ALU.add)

        for (Pr, Pi, br, bi_, sgn) in ((PPr, PPi, mag, mag, 1.0), (PNr, PNi, imag, imag, -1.0)):
            V(lambda e, Pr=Pr: e.memset(Pr[:, :, 0:1], 1.0))
            V(lambda e, Pi=Pi: e.memset(Pi[:, :, 0:1], 0.0))
            tt(Pr[:, :, 1], br[:], cosv[:], ALU.mult)
            tt(Pi[:, :, 1], bi_[:], sinv[:], ALU.mult)
            if sgn < 0:
                V(lambda e, Pi=Pi: e.tensor_scalar(out=Pi[:, :, 1], in0=Pi[:, :, 1], scalar1=-1.0, scalar2=None, op0=ALU.mult))
            for (o0, n_, s0, k0) in ((2, 1, 1, 1), (3, 2, 1, 2), (5, 4, 1, 4), (9, 8, 1, 8)):
                cmul(Pr[:, :, o0:o0 + n_], Pi[:, :, o0:o0 + n_], Pr[:, :, s0:s0 + n_], Pi[:, :, s0:s0 + n_],
                     Pr[:, :, k0:k0 + 1].to_broadcast([128, 32, n_]), Pi[:, :, k0:k0 + 1].to_broadcast([128, 32, n_]), [128, 32, n_])
        V(lambda e: e.tensor_copy(out=CAA[:, l, 0, :], in_=PPr[:, :, 16]))
        V(lambda e: e.tensor_copy(out=CAA[:, l, 1, :], in_=PPr[:, :, 16]))
        V(lambda e: e.tensor_copy(out=CAB[:, l, 1, :], in_=PPi[:, :, 16]))
        V(lambda e: e.tensor_scalar(out=CAB[:, l, 0, :], in0=PPi[:, :, 16], scalar1=-1.0, scalar2=None, op0=ALU.mult))
        nre = t_("nre", [128, 32]); den = t_("den", [128, 32]); cre = t_("cre", [128, 32]); cim = t_("cim", [128, 32])
        t1 = t_("t1", [128, 32]); t2 = t_("t2", [128, 32])
        V(lambda e: e.tensor_scalar(out=nre[:], in0=PPr[:, :, 1], scalar1=-1.0, scalar2=None, op0=ALU.add))
        tt(t1[:], lr[:], lr[:], ALU.mult)
        tt(t2[:], li[:], li[:], ALU.mult)
        tt(den[:], t1[:], t2[:], ALU.add)
        V(lambda e: e.reciprocal(out=den[:], in_=den[:]))
        tt(t1[:], nre[:], lr[:], ALU.mult)
        tt(t2[:], PPi[:, :, 1], li[:], ALU.mult)
        tt(cre[:], t1[:], t2[:], ALU.add)
        tt(cre[:], cre[:], den[:], ALU.mult)
        tt(t1[:], PPi[:, :, 1], lr[:], ALU.mult)
        tt(t2[:], nre[:], li[:], ALU.mult)
        tt(cim[:], t1[:], t2[:], ALU.subtract)
        tt(cim[:], cim[:], den[:], ALU.mult)
        BBr = t_("BBr", [128, 32, 16]); BBi = t_("BBi", [128, 32, 16])
        cmul(BBr[:], BBi[:], cre[:].unsqueeze(2).to_broadcast([128, 32, 16]), cim[:].unsqueeze(2).to_broadcast([128, 32, 16]),
             Bre[:], Bim[:], [128, 32, 16])
        PCr = t_("PCr", [128, 32, 16]); PCi = t_("PCi", [128, 32, 16])
        PQr = t_("PQr", [128, 32, 16]); PQi = t_("PQi", [128, 32, 16])
        V(lambda e: e.tensor_copy(out=PCr[:, 0:16, :], in_=PPr[:, 0:16, 1:17]))
        V(lambda e: e.tensor_copy(out=PCi[:, 0:16, :], in_=PPi[:, 0:16, 1:17]))
        V(lambda e: e.tensor_copy(out=PQr[:, 0:16, :], in_=PNr[:, 0:16, 1:17]))
        V(lambda e: e.tensor_copy(out=PQi[:, 0:16, :], in_=PNi[:, 0:16, 1:17]))
        cmul(PCr[:, 16:32, :], PCi[:, 16:32, :], PNr[:, 16:32, 0:16], PNi[:, 16:32, 0:16],
             PPr[:, 16:32, 16:17].to_broadcast([128, 16, 16]), PPi[:, 16:32, 16:17].to_broadcast([128, 16, 16]), [128, 16, 16])
        cmul(PQr[:, 16:32, :], PQi[:, 16:32, :], PPr[:, 16:32, 0:16], PPi[:, 16:32, 0:16],
             PNr[:, 16:32, 16:17].to_broadcast([128, 16, 16]), PNi[:, 16:32, 16:17].to_broadcast([128, 16, 16]), [128, 16, 16])
        if "GEN" in k.debug and l == 0:
            for nm, t, shp in (("lr", lr, [128, 32]), ("li", li, [128, 32]), ("ls", ls, [128, 32]), ("step", step, [128, 32]),
                               ("th", th, [128, 32]), ("f", f_, [128, 32]), ("fc", fc, [128, 32]),
                               ("sinv", sinv, [128, 32]), ("cosv", cosv, [128, 32]), ("mag", mag, [128, 32]),
                               ("PPr", PPr, [128, 32, 17]), ("PPi", PPi, [128, 32, 17]), ("PNr", PNr, [128, 32, 17]), ("PNi", PNi, [128, 32, 17]),
                               ("BBr", BBr, [128, 32, 16]), ("BBi", BBi, [128, 32, 16]), ("Cre", Cre, [128, 32, 16]), ("Bre", Bre, [128, 32, 16]),
                               ("PCr", PCr, [128, 32, 16]), ("PCi", PCi, [128, 32, 16]), ("PQr", PQr, [128, 32, 16]), ("PQi", PQi, [128, 32, 16])):
                dt_ = nc.dram_tensor("G_" + nm, shp, F32, kind="ExternalOutput").ap()
                p.dma("sp", dt_, t[:], r=K_ + IK, w=["GDBG"])
        Xr = t_("Xr", [128, 16, 256]); Xi = t_("Xi", [128, 16, 256])
        Qr = t_("Qr", [128, 16, 256]); Qi = t_("Qi", [128, 16, 256])
        BLr = t_("BLr", [128, 16, 256]); BLi = t_("BLi", [128, 16, 256])
        CSb = t_("CSb", [128, 16, 2, 256], BF16)
        mlt = [t_("mlt%d" % i, [128, 256], BF16) for i in range(2)]
        mtmp = t_("mtmp", [128, 256])
        blt = [t_("blt%d" % i, [128, 128], BF16) for i in range(2)]
        cnt = [0]
        sh4 = [128, 16, 16, 16]

        def v4(t):
            return t[:].rearrange("p g (i c) -> p g i c", i=16)
        for d in range(2):
            fs = slice(d * 16, (d + 1) * 16)
            cmul(v4(Xr), v4(Xi), PCr[:, fs, :].unsqueeze(3).to_broadcast(sh4), PCi[:, fs, :].unsqueeze(3).to_broadcast(sh4),
                 Cre[:, fs, :].unsqueeze(2).to_broadcast(sh4), Cim[:, fs, :].unsqueeze(2).to_broadcast(sh4), sh4)
            V(lambda e: e.tensor_scalar(out=Xi[:], in0=Xi[:], scalar1=-1.0, scalar2=None, op0=ALU.mult))
            cmul(v4(Qr), v4(Qi), PQr[:, fs, :].unsqueeze(3).to_broadcast(sh4), PQi[:, fs, :].unsqueeze(3).to_broadcast(sh4),
                 BBr[:, fs, :].unsqueeze(2).to_broadcast(sh4), BBi[:, fs, :].unsqueeze(2).to_broadcast(sh4), sh4)
            cmul(BLr[:], BLi[:], Qr[:], Qi[:], PPr[:, fs, 16:17].to_broadcast([128, 16, 256]), PPi[:, fs, 16:17].to_broadcast([128, 16, 256]),
                 [128, 16, 256])
            ACT(lambda e: e.copy(out=CSb[:, :, 0, :], in_=Xr[:]))
            ACT(lambda e: e.copy(out=CSb[:, :, 1, :], in_=Xi[:]))
            for gh in range(2):
                p.dma("sp", g["SSM_CS"][l, d, gh * 16:(gh + 1) * 16].rearrange("g n r f -> n g (r f)"),
                      CSb[gh * 64:(gh + 1) * 64].rearrange("p g r f -> p g (r f)"), r=K_, w=["SSM_CS"])
            for gg in range(32):
                gh, g16 = gg // 16, gg % 16
                r0 = gh * 64
                for ch in range(2):
                    cs = slice(ch * 128, (ch + 1) * 128)
                    i_ = cnt[0] % 2
                    cnt[0] += 1
                    pi = 4 + (cnt[0] % 2) * 2
                    p.op("pe", lambda e, pi=pi, g16=g16, cs=cs, r0=r0: e.matmul(
                        PS[pi][:, 0:256], Qr[r0:r0 + 64, g16, cs], Xr[r0:r0 + 64, g16, :], start=True, stop=False), r=K_, w=[("ps", pi)])
                    p.op("pe", lambda e, pi=pi, g16=g16, cs=cs, r0=r0: e.matmul(
                        PS[pi][:, 0:256], Qi[r0:r0 + 64, g16, cs], Xi[r0:r0 + 64, g16, :], start=False, stop=True), r=K_, w=[("ps", pi)])
                    msk = (mF if d == 0 else mB)
                    ml = mlt[i_]; mk = ("mlt", i_)
                    if d == 0:
                        p.op("dve", lambda e, pi=pi, ch=ch, msk=msk: e.tensor_tensor(out=mtmp[:], in0=PS[pi][:, 0:256], in1=msk[:, ch, :], op=ALU.mult),
                             r=[("ps", pi), "mF", "mB"], w=["mtmp"])
                        p.op("dve", lambda e, ch=ch, gg=gg, ml=ml: e.scalar_tensor_tensor(
                            out=ml[:], in0=dI[:, ch, :], scalar=dvec[:, l, gg:gg + 1], in1=mtmp[:], op0=ALU.mult, op1=ALU.add),
                            r=["mtmp", "dI", "dvec"], w=[mk])
                    else:
                        p.op("dve", lambda e, pi=pi, ch=ch, msk=msk, ml=ml: e.tensor_tensor(out=ml[:], in0=PS[pi][:, 0:256], in1=msk[:, ch, :], op=ALU.mult),
                             r=[("ps", pi), "mF", "mB"], w=[mk])
                    p.dma("sp", g["SSM_ML"][l, d, gg, ch], ml[:], r=[mk], w=["SSM_ML"])
                    pj = pi + 1
                    p.op("pe", lambda e, pj=pj, g16=g16, cs=cs, r0=r0: e.transpose(
                        PS[pj][:, 0:64], BLr[r0:r0 + 64, g16, cs], ident_f[r0:r0 + 64, r0:r0 + 64]), r=K_ + ["ident_f"], w=[("ps", pj)])
                    p.op("pe", lambda e, pj=pj, g16=g16, cs=cs, r0=r0: e.transpose(
                        PS[pj][:, 64:128], BLi[r0:r0 + 64, g16, cs], ident_f[r0:r0 + 64, r0:r0 + 64]), r=K_ + ["ident_f"], w=[("ps", pj)])
                    bl = blt[i_]; bk = ("blt", i_)
                    p.op("act", lambda e, pj=pj, bl=bl: e.copy(out=bl[:], in_=PS[pj][:, 0:128]), r=[("ps", pj)], w=[bk])
                    p.dma("sp", g["SSM_BLT"][l, d, gg, ch], bl[:], r=[bk], w=["SSM_BLT"])
    p.barrier()


class WStream:
    def __init__(self, p, name, stages, wbs, cast_engs=("act", "pool")):
        self.p = p
        self.name = name
        self.stages = stages
        self.wbs = wbs
        self.cast_engs = cast_engs
        self.ci = 0

    def run(self, groups, compute):
        p = self.p
        ns, nw = len(self.stages), len(self.wbs)
        n = len(groups)

        def dma(i):
            st = self.stages[i % ns]
            sk = (self.name + "st", i % ns)
            for (src, k0, K, c0, nn) in groups[i]:
                p.dma("sp", st[:, k0:k0 + K, c0:c0 + nn], src.rearrange("(kc p) n -> p kc n", p=128), w=[sk])

        def cast(i):
            st = self.stages[i % ns]
            sk = (self.name + "st", i % ns)
            wb = self.wbs[i % nw]
            wk = (self.name + "wb", i % nw)
            for (src, k0, K, c0, nn) in groups[i]:
                eng = self.cast_engs[self.ci % len(self.cast_engs)]
                self.ci += 1
                if eng == "act":
                    p.op("act", lambda e, st=st, wb=wb, k0=k0, K=K, c0=c0, nn=nn: e.copy(
                        out=wb[:, k0:k0 + K, c0:c0 + nn], in_=st[:, k0:k0 + K, c0:c0 + nn]), r=[sk], w=[wk])
                else:
                    p.op(eng, lambda e, st=st, wb=wb, k0=k0, K=K, c0=c0, nn=nn: e.tensor_copy(
                        out=wb[:, k0:k0 + K, c0:c0 + nn], in_=st[:, k0:k0 + K, c0:c0 + nn]), r=[sk], w=[wk])

        for i in range(min(ns - 1, n)):
            dma(i)
        if n:
            cast(0)
        for i in range(n):
            if i + ns - 1 < n:
                dma(i + ns - 1)
            if i + 1 < n:
                cast(i + 1)
            compute(i, self.wbs[i % nw], (self.name + "wb", i % nw))


def ssm_gen_alloc(nc, pro):
    gin = []
    shared = None
    for l in range(DEPTH):
        def t_(name, shape, l=l):
            return pro.enter_context(nc.sbuf_tensor("gi%d_%s" % (l, name), list(shape), F32))
        ts = [t_("lr", [128, 32]), t_("li", [128, 32]), t_("ls", [128, 32]),
              t_("Bre", [128, 32, 16]), t_("Bim", [128, 32, 16]), t_("Cre", [128, 32, 16]), t_("Cim", [128, 32, 16])]
        if shared is None:
            shared = [t_("mF", [128, 2, 256]), t_("mB", [128, 2, 256]), t_("dI", [128, 2, 256]), t_("dvec", [128, DEPTH, 32])]
        gin.append(ts + shared)
    return gin


def ssm_gen_loads(p, g, gin):
    for l in range(DEPTH):
        lr, li, ls, Bre, Bim, Cre, Cim, mF, mB, dI, dvec = gin[l]
        for d in range(2):
            for gh in range(2):
                rows = slice(gh * 64, (gh + 1) * 64)
                gs = slice(gh * 16, (gh + 1) * 16)
                fs = slice(d * 16, (d + 1) * 16)
                p.dma("pool", lr[rows, fs], g["ssm_lam_re"][l, d, gs, :].rearrange("g n -> n g"), w=[("lr", l)], slow=True)
                p.dma("pool", li[rows, fs], g["ssm_lam_im"][l, d, gs, :].rearrange("g n -> n g"), w=[("li", l)], slow=True)
                p.dma("pool", ls[rows, fs], g["ssm_log_step"][l, d, gs].partition_broadcast(64), w=[("ls", l)])
                p.dma("pool", Bre[rows, fs, :], g["ssm_b_re"][l, d, gs, :, :].rearrange("g n c -> n g c"), w=[("Bre", l)])
                p.dma("pool", Bim[rows, fs, :], g["ssm_b_im"][l, d, gs, :, :].rearrange("g n c -> n g c"), w=[("Bim", l)])
                for g16 in range(16):
                    gg = gh * 16 + g16
                    p.dma("pool", Cre[rows, d * 16 + g16, :], g["ssm_c_re"][l, d, gg, :, :].rearrange("c n -> n c"), w=[("Cre", l)], slow=True)
                    p.dma("pool", Cim[rows, d * 16 + g16, :], g["ssm_c_im"][l, d, gg, :, :].rearrange("c n -> n c"), w=[("Cim", l)], slow=True)
        if l == 0:
            p.dma("pool", mF[:], g["c_mF"].rearrange("c p f -> p c f"), w=["mF"])
            p.dma("pool", mB[:], g["c_mB"].rearrange("c p f -> p c f"), w=["mB"])
            p.dma("pool", dI[:], g["c_dI"].rearrange("c p f -> p c f"), w=["dI"])
            p.dma("pool", dvec[:], g["c_dvec"][:, :, :], w=["dvec"])
```
